# Optimizing a Trainium2 kernel written in Bass

```python
import math
import jax, jax.numpy as jnp
from jax import lax
import numpy as np

D_MODEL = 1024
BATCH = 8
SEQ = 4096
DEPTH = 4

DA_HEADS = 4
DA_QK_DIM = 64
DA_V_DIM = 2 * DA_QK_DIM
SB_HEADS = 8
SB_DIM = 64
DIL_PAIRS = ((128, 1), (512, 4), (2048, 16))
DIL_GROUPS = len(DIL_PAIRS)
DIL_HEADS_PER_GROUP = 4
DIL_HEADS = DIL_GROUPS * DIL_HEADS_PER_GROUP
DIL_DIM = 128

N_BRANCHES = 3
Q_BLOCK = 128
D_FF = 4 * D_MODEL
PLE_DIM = 256
REL_BUCKETS = 32
REL_MAX_DIST = 2048
BIAS_HEADS = DA_HEADS + DIL_HEADS
DEEPNORM_ALPHA = (2 * DEPTH) ** 0.25
DEEPNORM_BETA = (8 * DEPTH) ** -0.25
LN_EPS = 1e-5
RMS_EPS = 1e-5

IN_SIZES = (
    DA_HEADS * 2 * DA_QK_DIM, DA_HEADS * 2 * DA_QK_DIM, DA_HEADS * DA_V_DIM,
    SB_HEADS * SB_DIM, SB_HEADS * SB_DIM, SB_HEADS * SB_DIM,
    DIL_HEADS * DIL_DIM, DIL_HEADS * DIL_DIM, DIL_HEADS * DIL_DIM,
    N_BRANCHES * D_MODEL,
)
VALUE_SLOTS = (2, 5, 8)
IN_COLS = sum(IN_SIZES)
DA_OUT = DA_HEADS * DA_V_DIM
SB_OUT = SB_HEADS * SB_DIM
DIL_OUT = DIL_HEADS_PER_GROUP * DIL_DIM

kernel_name = "hybrid_gated_diff_stickbreak_dilated"


def _split_points():
    return [int(c) for c in np.cumsum(IN_SIZES)[:-1]]


def layer_norm(x, g, b):
    xf = x.astype(jnp.float32)
    mu = xf.mean(-1, keepdims=True)
    var = jnp.square(xf - mu).mean(-1, keepdims=True)
    return ((xf - mu) * lax.rsqrt(var + LN_EPS) * g + b).astype(x.dtype)


def rel_bucket(dist):
    max_exact = REL_BUCKETS // 2
    d = jnp.maximum(dist, 0)
    log_ratio = jnp.log(jnp.maximum(d, 1).astype(jnp.float32) / max_exact) / math.log(REL_MAX_DIST / max_exact)
    large = max_exact + (log_ratio * (REL_BUCKETS - max_exact)).astype(jnp.int32)
    return jnp.where(d < max_exact, d, jnp.minimum(large, REL_BUCKETS - 1))


def diff_attention(q, k, v, lam, norm_g, bias_table, lam_init):
    B, S, H, _, d = q.shape
    nblk = S // Q_BLOCK
    scale = d ** -0.5
    q = q.transpose(0, 2, 3, 1, 4)
    k = k.transpose(0, 2, 3, 1, 4)
    vt = v.transpose(0, 2, 1, 3)
    qb = q.reshape(B, H, 2, nblk, Q_BLOCK, d).transpose(3, 0, 1, 2, 4, 5)
    k_pos = jnp.arange(S)

    def block(args):
        qi, i = args
        q_pos = i * Q_BLOCK + jnp.arange(Q_BLOCK)
        rel = q_pos[:, None] - k_pos[None, :]
        bias = bias_table[rel_bucket(rel)].transpose(2, 0, 1).astype(jnp.float32)
        logits = jnp.einsum('bhmqd,bhmkd->bhmqk', qi, k).astype(jnp.float32) * scale
        logits = jnp.where(rel >= 0, logits + bias[None, :, None], -jnp.inf)
        prob = jax.nn.softmax(logits, axis=-1)
        w = prob[:, :, 0] - lam * prob[:, :, 1]
        return jnp.einsum('bhqk,bhkd->bhqd', w.astype(vt.dtype), vt)

    o = lax.map(block, (qb, jnp.arange(nblk)))
    o = o.transpose(1, 0, 3, 2, 4).reshape(B, S, H, -1).astype(jnp.float32)
    o = o * lax.rsqrt(jnp.mean(jnp.square(o), axis=-1, keepdims=True) + RMS_EPS) * norm_g
    o = o * (1.0 - lam_init)
    return o.reshape(B, S, -1).astype(v.dtype)


def stick_breaking_attention(q, k, v):
    B, S, H, d = q.shape
    nblk = S // Q_BLOCK
    scale = d ** -0.5
    qb = q.reshape(B, nblk, Q_BLOCK, H, d).transpose(1, 0, 3, 2, 4)
    kt = k.transpose(0, 2, 1, 3)
    vt = v.transpose(0, 2, 1, 3)
    k_pos = jnp.arange(S)

    def block(args):
        qi, i = args
        q_pos = i * Q_BLOCK + jnp.arange(Q_BLOCK)
        before = k_pos[None, :] < q_pos[:, None]
        z = jnp.einsum('bhqd,bhkd->bhqk', qi, kt).astype(jnp.float32) * scale
        log_keep = jnp.where(before, jax.nn.log_sigmoid(-z), 0.0)
        later = lax.cumsum(log_keep, axis=3, reverse=True) - log_keep
        a = jnp.where(before, jnp.exp(jax.nn.log_sigmoid(z) + later), 0.0)
        return jnp.einsum('bhqk,bhkd->bhqd', a.astype(vt.dtype), vt)

    o = lax.map(block, (qb, jnp.arange(nblk)))
    return o.transpose(1, 0, 3, 2, 4).reshape(B, S, H * d)


def dilated_group(q, k, v, bias_table, window, dilation):
    B, S, H, dh = q.shape
    n = window // dilation
    L = -(-S // dilation)
    Lp = -(-L // n) * n
    nb = Lp // n

    def to_strided(a):
        a = jnp.pad(a, ((0, 0), (0, Lp * dilation - S), (0, 0), (0, 0)))
        a = a.reshape(B, Lp, dilation, H, dh).transpose(0, 2, 3, 1, 4)
        return a.reshape(B, dilation, H, nb, n, dh)

    def with_prev(a):
        prev = jnp.pad(a, ((0, 0), (0, 0), (0, 0), (1, 0), (0, 0), (0, 0)))[:, :, :, :-1]
        return jnp.concatenate([prev, a], axis=4)

    qs = to_strided(q)
    kb = with_prev(to_strided(k))
    vb = with_prev(to_strided(v))
    steps = jnp.arange(n)[:, None] + n - jnp.arange(2 * n)[None, :]
    valid_local = (steps >= 0) & (steps <= n)
    no_prev = (jnp.arange(nb)[:, None, None] == 0) & (jnp.arange(2 * n)[None, None, :] < n)
    valid = valid_local[None] & ~no_prev
    bias = bias_table[rel_bucket(steps * dilation)].transpose(2, 0, 1).astype(jnp.float32)
    logits = jnp.einsum('brhcqd,brhckd->brhcqk', qs, kb).astype(jnp.float32) * dh ** -0.5
    logits = jnp.where(valid, logits + bias[None, None, :, None], -jnp.inf)
    lse = jax.nn.logsumexp(logits, axis=-1)
    prob = jnp.exp(logits - lse[..., None])
    o = jnp.einsum('brhcqk,brhckd->brhcqd', prob.astype(vb.dtype), vb)
    o = o.reshape(B, dilation, H, Lp, dh).transpose(0, 3, 1, 2, 4).reshape(B, Lp * dilation, H, dh)[:, :S]
    lse = lse.reshape(B, dilation, H, Lp).transpose(0, 3, 1, 2).reshape(B, Lp * dilation, H)[:, :S]
    return o, lse


def dilated_attention(q, k, v, bias_table):
    B, S = q.shape[:2]
    outs, lses = [], []
    for g, (window, dilation) in enumerate(DIL_PAIRS):
        cols = bias_table[:, g * DIL_HEADS_PER_GROUP:(g + 1) * DIL_HEADS_PER_GROUP]
        o, lse = dilated_group(q[:, :, g], k[:, :, g], v[:, :, g], cols, window, dilation)
        outs.append(o)
        lses.append(lse)
    wts = jax.nn.softmax(jnp.stack(lses, axis=0), axis=0)
    o = wts[0][..., None] * outs[0] + wts[1][..., None] * outs[1] + wts[2][..., None] * outs[2]
    return o.reshape(B, S, -1).astype(q.dtype)


def mixer(x, w_in, da_lambda, da_norm, w_branch_da, w_branch_sb, w_branch_dil, w_out, rel_bias, lam_init):
    B, S, D = x.shape
    proj = x @ w_in
    (da_q, da_k, da_v, sb_q, sb_k, sb_v, dl_q, dl_k, dl_v, gates) = jnp.split(proj, _split_points(), axis=-1)
    lam_f = da_lambda.astype(jnp.float32)
    lam = jnp.exp(jnp.sum(lam_f[0] * lam_f[1])) - jnp.exp(jnp.sum(lam_f[2] * lam_f[3])) + lam_init
    o_da = diff_attention(da_q.reshape(B, S, DA_HEADS, 2, DA_QK_DIM),
                          da_k.reshape(B, S, DA_HEADS, 2, DA_QK_DIM),
                          da_v.reshape(B, S, DA_HEADS, DA_V_DIM),
                          lam, da_norm, rel_bias[:, :DA_HEADS], lam_init)
    o_sb = stick_breaking_attention(sb_q.reshape(B, S, SB_HEADS, SB_DIM),
                                    sb_k.reshape(B, S, SB_HEADS, SB_DIM),
                                    sb_v.reshape(B, S, SB_HEADS, SB_DIM))
    dil_shape = (B, S, DIL_GROUPS, DIL_HEADS_PER_GROUP, DIL_DIM)
    o_dl = dilated_attention(dl_q.reshape(dil_shape), dl_k.reshape(dil_shape), dl_v.reshape(dil_shape),
                             rel_bias[:, DA_HEADS:])
    g = jax.nn.sigmoid(gates).reshape(B, S, N_BRANCHES, D)
    merged = (g[:, :, 0] * (o_da @ w_branch_da) + g[:, :, 1] * (o_sb @ w_branch_sb)
              + g[:, :, 2] * (o_dl @ w_branch_dil))
    return merged @ w_out


def setup_inputs(seed: int = 0) -> dict:
    key = jax.random.key(seed)
    ks = jax.random.split(key, 20)
    f32 = jnp.float32
    nrm = lambda k, shape, s: jax.random.normal(k, shape, f32) * s
    col_scale = jnp.concatenate([jnp.full((s,), DEEPNORM_BETA if idx in VALUE_SLOTS else 1.0, f32)
                                 for idx, s in enumerate(IN_SIZES)])
    return {
        "x": nrm(ks[0], (BATCH, SEQ, D_MODEL), 1.0),
        "p": nrm(ks[1], (DEPTH, BATCH, SEQ, PLE_DIM), 1.0),
        "w_in": nrm(ks[2], (DEPTH, D_MODEL, IN_COLS), D_MODEL ** -0.5) * col_scale,
        "da_lambda": nrm(ks[3], (DEPTH, 4, DA_QK_DIM), 0.1),
        "da_norm": 1.0 + nrm(ks[4], (DEPTH, DA_V_DIM), 0.02),
        "w_branch_da": nrm(ks[5], (DEPTH, DA_OUT, D_MODEL), DA_OUT ** -0.5),
        "w_branch_sb": nrm(ks[6], (DEPTH, SB_OUT, D_MODEL), SB_OUT ** -0.5),
        "w_branch_dil": nrm(ks[7], (DEPTH, DIL_OUT, D_MODEL), DIL_OUT ** -0.5),
        "w_out": nrm(ks[8], (DEPTH, D_MODEL, D_MODEL), D_MODEL ** -0.5 * DEEPNORM_BETA),
        "ln1_g": 1.0 + nrm(ks[9], (DEPTH, D_MODEL), 0.02),
        "ln1_b": nrm(ks[10], (DEPTH, D_MODEL), 0.02),
        "w_up": nrm(ks[11], (DEPTH, D_MODEL, D_FF), D_MODEL ** -0.5),
        "w_down": nrm(ks[12], (DEPTH, D_FF, D_MODEL), D_FF ** -0.5 * DEEPNORM_BETA),
        "w_ple_gate": nrm(ks[13], (DEPTH, D_MODEL, D_MODEL), D_MODEL ** -0.5),
        "w_ple": nrm(ks[14], (DEPTH, PLE_DIM, D_MODEL), PLE_DIM ** -0.5 * DEEPNORM_BETA),
        "ln2_g": 1.0 + nrm(ks[15], (DEPTH, D_MODEL), 0.02),
        "ln2_b": nrm(ks[16], (DEPTH, D_MODEL), 0.02),
        "rel_bias": nrm(ks[17], (REL_BUCKETS, BIAS_HEADS), 0.3),
    }


def reference(x, p, w_in, da_lambda, da_norm, w_branch_da, w_branch_sb, w_branch_dil, w_out,
              ln1_g, ln1_b, w_up, w_down, w_ple_gate, w_ple, ln2_g, ln2_b, rel_bias):
    for i in range(DEPTH):
        lam_init = 0.8 - 0.6 * math.exp(-0.3 * i)
        m = mixer(x, w_in[i], da_lambda[i], da_norm[i], w_branch_da[i], w_branch_sb[i],
                  w_branch_dil[i], w_out[i], rel_bias, lam_init)
        x = layer_norm(DEEPNORM_ALPHA * x + m, ln1_g[i], ln1_b[i])
        hid = jax.nn.relu(x @ w_up[i])
        c = jnp.square(hid) @ w_down[i]
        ple = jax.nn.sigmoid(x @ w_ple_gate[i]) * (p[i] @ w_ple[i])
        x = layer_norm(DEEPNORM_ALPHA * x + c + ple, ln2_g[i], ln2_b[i])
    return x
```

```python
import math
import numpy as np
from contextlib import ExitStack
import concourse.bass as bass
import concourse.mybir as mybir
from concourse.bass_utils import run_bass_kernel_spmd

F32 = mybir.dt.float32
BF16 = mybir.dt.bfloat16
AF = mybir.ActivationFunctionType
ALU = mybir.AluOpType
AX = mybir.AxisListType

D = 1024
DFF = 4096
PLE = 256
INC = 10752
A_Q, A_K, A_V = 0, 512, 1024
B_Q, B_K, B_V = 1536, 2048, 2560
C_Q, C_K, C_V = 3072, 4608, 6144
GATE = 7680
DIL = ((128, 1), (512, 4), (2048, 16))
DEPTH = 4
ALPHA = (2 * DEPTH) ** 0.25
LN_EPS = 1e-5
RMS_EPS = 1e-5
PADA = 384


class Buf:
    __slots__ = ("name", "w", "r")

    def __init__(self, name=""):
        self.name = name
        self.w = None
        self.r = {}


class Src:
    def __init__(self, name, sem, step):
        self.name = name
        self.sem = sem
        self.val = 0
        self.step = step


class Sched:
    ENGS = ("tensor", "vector", "scalar", "gpsimd", "sync")

    def __init__(self, nc, stack):
        self.nc = nc
        self.stack = stack
        self.ops = {e: [] for e in self.ENGS}
        self.src = {}
        self.seen = {e: {} for e in self.ENGS}
        self.chans = []
        for e in self.ENGS:
            self.src[e] = Src(e, stack.enter_context(nc.semaphore("s_" + e)), 1)
        self.nops = 0
        self.nroll = 0

    def chan(self, name):
        c = Src(name, self.stack.enter_context(self.nc.semaphore("c_" + name)), 16)
        self.chans.append(c)
        return c

    def _need(self, eng, deps):
        best = {}
        for s, v in deps:
            if best.get(s, 0) < v:
                best[s] = v
        for s, v in best.items():
            if self.seen[eng].get(s, 0) >= v:
                continue
            self.seen[eng][s] = v
            self.ops[eng].append(("wait", s.sem, v))

    @staticmethod
    def _deps(reads, writes):
        deps = []
        for b in reads:
            if b.w is not None:
                deps.append(b.w)
        for b in writes:
            if b.w is not None:
                deps.append(b.w)
            for rs, rv in b.r.items():
                deps.append((rs, rv))
        return deps

    LIMIT = 30000

    def _roll(self, s):
        if s.val < self.LIMIT:
            return s
        self.nroll += 1
        n = Src(s.name, self.stack.enter_context(self.nc.semaphore("r%d_%s" % (self.nroll, s.name))), s.step)
        return n

    def op(self, eng, fn, reads=(), writes=(), signal=True):
        s = self._roll(self.src[eng])
        self.src[eng] = s
        deps = self._deps(reads, writes)
        if eng == "tensor":
            deps = [d for d in deps if d[0] is not s]
        else:
            deps = [d for d in deps if not (d[0] is s and d[1] > s.val)]
        self._need(eng, deps)
        if signal:
            s.val += 1
            tag = (s, s.val)
            self.ops[eng].append(("op", fn, s.sem, 1))
        else:
            tag = (s, s.val + 1)
            self.ops[eng].append(("op", fn, None, 0))
        self.nops += 1
        for b in reads:
            if b.r.get(tag[0], 0) < tag[1]:
                b.r[tag[0]] = tag[1]
        for b in writes:
            b.w = tag
            b.r = {}
        return tag

    def dma(self, eng, ch, fn, reads=(), writes=()):
        deps = self._deps(reads, writes)
        self._need(eng, deps)
        ch.val += 16
        tag = (ch, ch.val)
        self.ops[eng].append(("op", fn, ch.sem, 16))
        self.nops += 1
        for b in reads:
            if b.r.get(ch, 0) < ch.val:
                b.r[ch] = ch.val
        for b in writes:
            b.w = tag
            b.r = {}
        return tag

    def barrier(self):
        allsrc = [(self.src[e], self.src[e].val) for e in self.ENGS if self.src[e].val > 0]
        allsrc += [(c, c.val) for c in self.chans if c.val > 0]
        for e in self.ENGS:
            self._need(e, [d for d in allsrc if d[0] is not self.src[e]])

    def wait_all(self, eng, tags):
        self._need(eng, tags)

    def prewait(self, eng, reads=(), writes=()):
        s = self.src[eng]
        deps = self._deps(reads, writes)
        if eng == "tensor":
            deps = [d for d in deps if d[0] is not s]
        else:
            deps = [d for d in deps if not (d[0] is s and d[1] > s.val)]
        self._need(eng, deps)

    def emit(self, block):
        def mk(eng):
            lst = self.ops[eng]

            def body(e):
                for it in lst:
                    if it[0] == "wait":
                        e.wait_ge(it[1], it[2])
                    else:
                        ins = it[1](e)
                        if it[2] is not None:
                            ins.then_inc(it[2], it[3])
            return body
        block.tensor(mk("tensor"))
        block.vector(mk("vector"))
        block.scalar(mk("scalar"))
        block.gpsimd(mk("gpsimd"))
        block.sync(mk("sync"))
        self.ops = {e: [] for e in self.ENGS}


def MM(out, lhsT, rhs, start, stop):
    return lambda e: e.matmul(out, lhsT=lhsT, rhs=rhs, start=start, stop=stop)


def TR(out, in_, ident):
    return lambda e: e.transpose(out, in_, ident)


def ACT(out, in_, func, scale=1.0, bias=0.0):
    return lambda e: e.activation(out=out, in_=in_, func=func, bias=bias, scale=scale)


def CP(out, in_):
    return lambda e: e.tensor_copy(out=out, in_=in_)


def TT(out, in0, in1, op):
    return lambda e: e.tensor_tensor(out=out, in0=in0, in1=in1, op=op)


def TS(out, in0, s1, s2, op0, op1=None):
    if op1 is None:
        return lambda e: e.tensor_scalar(out=out, in0=in0, scalar1=s1, scalar2=None, op0=op0)
    return lambda e: e.tensor_scalar(out=out, in0=in0, scalar1=s1, scalar2=s2, op0=op0, op1=op1)


def STT(out, in0, scalar, in1, op0, op1):
    return lambda e: e.scalar_tensor_tensor(out=out, in0=in0, scalar=scalar, in1=in1, op0=op0, op1=op1)


def DMA(out, in_):
    return lambda e: e.dma_start(out=out, in_=in_)


def ASEL(out, in_, pattern, cmp, fill, base, cm):
    return lambda e: e.affine_select(out=out, in_=in_, pattern=pattern, compare_op=cmp, fill=fill,
                                     base=base, channel_multiplier=cm)


class Rot:
    def __init__(self, items):
        self.items = items
        self.i = 0

    def next(self):
        it = self.items[self.i % len(self.items)]
        self.i += 1
        return it


def build_program(S, NL, dbg=False):
    T = S // 128
    G = S // 512
    WA = S + PADA
    nc = bass.Bass("TRN2", target_bir_lowering=False)

    def din(name, shape):
        return nc.dram_tensor(name, list(shape), F32, kind="ExternalInput").ap()

    x_d = din("x", [S, D])
    pT_d = din("pT", [NL, PLE, S])
    w_in_d = din("w_in", [NL, D, INC])
    lam_d = din("da_lambda", [NL, 256])
    dan_d = din("da_norm", [NL, 128])
    wb_d = [din("w_branch_da", [NL, 512, D]), din("w_branch_sb", [NL, 512, D]), din("w_branch_dil", [NL, 512, D])]
    wout_d = din("w_out", [NL, D, D])
    ln1g_d = din("ln1_g", [NL, D])
    ln1b_d = din("ln1_b", [NL, D])
    wup_d = din("w_up", [NL, D, DFF])
    wdn_d = din("w_down", [NL, DFF, D])
    wpg_d = din("w_ple_gate", [NL, D, D])
    wple_d = din("w_ple", [NL, PLE, D])
    ln2g_d = din("ln2_g", [NL, D])
    ln2b_d = din("ln2_b", [NL, D])
    biasA_d = din("biasA", [4, 128, WA])
    biasD_d = din("biasD", [12, 128, 256])
    lamc_d = din("lamc", [NL, 128, 2])
    out_d = nc.dram_tensor("out", [S, D], F32, kind="ExternalOutput").ap()
    okind = "ExternalOutput" if dbg else "Internal"
    res1_d = nc.dram_tensor("res1", [S, D], F32, kind=okind).ap()
    ot_d = nc.dram_tensor("ot", [12, 128, S], BF16, kind=okind).ap()
    ea_d = nc.dram_tensor("ea", [4, 128, WA], BF16, kind="Internal").ap()
    c1_d = nc.dram_tensor("c1", [S, D], F32, kind="Internal").ap()

    with ExitStack() as st:
        Sc = Sched(nc, st)

        uid = [0]

        def sbuf(stack, name, shape, dt):
            uid[0] += 1
            return stack.enter_context(nc.sbuf_tensor("%s_u%d" % (name, uid[0]), list(shape), dt))

        banks = [st.enter_context(nc.psum_tensor("bank%d" % i, [128, 512], F32)) for i in range(8)]
        xT = sbuf(st, "xT", [128, 8, S], BF16)
        ident = sbuf(st, "ident", [128, 128], BF16)
        ones_bf = sbuf(st, "ones_bf", [128, 128], BF16)
        nones_bf = sbuf(st, "nones_bf", [128, 128], BF16)
        uneg = sbuf(st, "uneg", [128, 128], BF16)
        onesf = sbuf(st, "onesf", [128, 512], F32)

        class P:
            bank = [Buf("bank%d" % i) for i in range(8)]
            xT = [Buf("xT%d" % g) for g in range(G)]
            const = Buf("const")
            ED = Buf("ED")
            ea = [Buf("ea%d" % h) for h in range(4)]
            ot = [[Buf("ot%d_%d" % (i, g)) for g in range(G)] for i in range(12)]
            res1 = [Buf("res1_%d" % t) for t in range(T)]
            outb = [Buf("out_%d" % t) for t in range(T)]
            c1 = [Buf("c1_%d" % t) for t in range(T)]

        ch_out = Sc.chan("out")
        ch_res1 = Sc.chan("res1")
        ch_ot = Sc.chan("ot")
        ch_ea = Sc.chan("ea")
        ch_misc = [Sc.chan("misc%d" % i) for i in range(12)]
        ch_w = [Sc.chan("w%d" % i) for i in range(12)]

        def end_phase(stack_unused=None):
            Sc.barrier()
            with nc.Block() as block:
                Sc.emit(block)

        def evac(i, out, in_, scale, reads, writes):
            if i % 2 == 0:
                Sc.op("scalar", ACT(out, in_, AF.Copy, scale=scale), reads=reads, writes=writes)
            else:
                Sc.op("vector", TS(out, in_, scale, None, ALU.mult), reads=reads, writes=writes)

        pj = Rot([(banks[6], P.bank[6]), (banks[7], P.bank[7])])
        cnt = {"ev": 0}

        def proj_feat(wt, wbuf, c0, dst_fn, dst_bufs, scale):
            for tg in range(G):
                bk, bb = pj.next()
                for kc in range(8):
                    Sc.op("tensor", MM(bk[:], wt[:, kc, c0:c0 + 128], xT[:, kc, tg * 512:(tg + 1) * 512], kc == 0, kc == 7),
                          reads=[wbuf, P.xT[tg]], writes=[bb], signal=(kc == 7))
                o, i_ = dst_fn(tg, bk)
                cnt["ev"] += 1
                evac(cnt["ev"], o, i_, scale, [bb], dst_bufs)

        def proj_tok(wt, wbuf, c0, ncols, vdst, vbuf, tok_ap_fn):
            per = 512 // ncols
            for t0 in range(0, T, per):
                bk, bb = pj.next()
                n = min(per, T - t0)
                for j in range(n):
                    for kc in range(8):
                        Sc.op("tensor", MM(bk[:, j * ncols:(j + 1) * ncols], tok_ap_fn(kc, t0 + j), wt[:, kc, c0:c0 + ncols], kc == 0, kc == 7),
                              reads=[wbuf] + P.xT, writes=[bb], signal=(kc == 7 and j == n - 1))
                cnt["ev"] += 1
                evac(cnt["ev"], vdst[:, t0:t0 + n, :], bk[:, 0:n * ncols].rearrange("p (t c) -> p t c", c=ncols), 1.0, [bb], [vbuf])

        def load_w_cols(stack_, wt, wbuf, ch, l, cols):
            o = 0
            for (c0, n) in cols:
                src = w_in_d[l, :, c0:c0 + n].rearrange("(kc p) c -> p kc c", p=128)
                Sc.dma("gpsimd", ch, DMA(wt[:, :, o:o + n], src), writes=[wbuf])
                o += n

        def transposes_to_xT(xb_ap, xb_buf, t):
            tp = banks[7][:].bitcast(BF16)
            for c in range(8):
                Sc.op("tensor", TR(tp[:, c * 128:(c + 1) * 128], xb_ap[:, c * 128:(c + 1) * 128], ident[:]),
                      reads=[xb_buf, P.const], writes=[P.bank[7]], signal=(c == 7))
            tg = t // 4
            Sc.op("scalar", ACT(xT[:, :, t * 128:(t + 1) * 128], tp.rearrange("p (c n) -> p c n", c=8), AF.Copy),
                  reads=[P.bank[7]], writes=[P.xT[tg]])

        with ExitStack() as ph:
            Sc.op("vector", lambda e: e.memset(onesf[:], 1.0), writes=[P.const])
            Sc.op("vector", CP(ones_bf[:], onesf[:, 0:128]), reads=[P.const], writes=[P.const])
            Sc.op("vector", TS(nones_bf[:], onesf[:, 0:128], -1.0, None, ALU.mult), reads=[P.const], writes=[P.const])
            Sc.op("gpsimd", ASEL(ident[:], ones_bf[:], [[1, 128]], ALU.is_equal, 0.0, 0, -1), reads=[P.const], writes=[P.const])
            Sc.op("gpsimd", ASEL(uneg[:], nones_bf[:], [[-1, 128]], ALU.is_ge, 0.0, 0, 1), reads=[P.const], writes=[P.const])
            CH = 1024
            rawA = [sbuf(ph, "rawA%d" % i, [128, CH], F32) for i in range(2)]
            rawAb = [Buf("rawA%d" % i) for i in range(2)]
            eab = [sbuf(ph, "eab%d" % i, [128, CH], BF16) for i in range(2)]
            eabb = [Buf("eab%d" % i) for i in range(2)]
            k = 0
            for h in range(4):
                for c0 in range(0, WA, CH):
                    n = min(CH, WA - c0)
                    i = k % 2
                    k += 1
                    Sc.dma("sync", ch_misc[1 + i], DMA(rawA[i][:, 0:n], biasA_d[h, :, c0:c0 + n]), writes=[rawAb[i]])
                    Sc.op("scalar", ACT(rawA[i][:, 0:n], rawA[i][:, 0:n], AF.Exp), reads=[rawAb[i]], writes=[rawAb[i]])
                    Sc.op("gpsimd", ASEL(eab[i][:, 0:n], rawA[i][:, 0:n], [[1, n]], ALU.is_ge, 0.0, c0 - PADA, -1),
                          reads=[rawAb[i]], writes=[eabb[i]])
                    Sc.dma("sync", ch_ea, DMA(ea_d[h, :, c0:c0 + n], eab[i][:, 0:n]), reads=[eabb[i]], writes=[P.ea[h]])
            xin = [sbuf(ph, "xin%d" % i, [128, D], F32) for i in range(2)]
            xinb = [Buf("xin%d" % i) for i in range(2)]
            xbf = [sbuf(ph, "xbf%d" % i, [128, D], BF16) for i in range(2)]
            xbfb = [Buf("xbf%d" % i) for i in range(2)]
            for t in range(T):
                i = t % 2
                Sc.dma("sync", ch_misc[3 + i], DMA(xin[i][:], x_d[t * 128:(t + 1) * 128, :]), writes=[xinb[i]])
                Sc.op("vector", CP(xbf[i][:], xin[i][:]), reads=[xinb[i]], writes=[xbfb[i]])
                transposes_to_xT(xbf[i], xbfb[i], t)
            end_phase()

        for l in range(NL):
            res_in = x_d if l == 0 else out_d
            res_in_bufs = None if l == 0 else P.outb

            with ExitStack() as ph:
                qk = [(sbuf(ph, "qA%d" % i, [128, S], BF16), sbuf(ph, "kA%d" % i, [128, S], BF16),
                       sbuf(ph, "vA%d" % i, [128, T, 128], BF16), Buf("qkvA%d" % i)) for i in range(2)]
                wA = [(sbuf(ph, "wA%d" % i, [128, 8, 384], BF16), Buf("wA%d" % i)) for i in range(2)]
                EAt = [(sbuf(ph, "EA%d" % i, [128, WA], BF16), Buf("EA%d" % i)) for i in range(2)]
                praw = Rot([(sbuf(ph, "praw%d" % i, [128, 512], BF16), Buf()) for i in range(6)])
                pTt = Rot([(sbuf(ph, "pT%d" % i, [128, 512], BF16), Buf()) for i in range(8)])
                lamt = sbuf(ph, "lamt", [128, 256], F32)
                lprod = sbuf(ph, "lprod", [128, 128], F32)
                lsum = sbuf(ph, "lsum", [128, 2], F32)
                lamc = sbuf(ph, "lamc", [128, 2], F32)
                nlam = sbuf(ph, "nlam", [128, 1], F32)
                gA = sbuf(ph, "gA", [128, 1], F32)
                lb = Buf("lam")
                r0 = sbuf(ph, "r0", [128, 512], F32)
                r1 = sbuf(ph, "r1", [128, 512], F32)
                t0 = sbuf(ph, "t0", [128, 512], F32)
                t1 = sbuf(ph, "t1", [128, 512], F32)
                oo = sbuf(ph, "oo", [128, 512], F32)
                sq = sbuf(ph, "sq", [128, 512], BF16)
                lnv = sbuf(ph, "lnv", [128, 512], F32)
                postb = Buf("post")
                ostage = Rot([(sbuf(ph, "ostA%d" % i, [128, 512], BF16), Buf()) for i in range(2)])

                Sc.dma("sync", ch_misc[0], DMA(lamt[:], lam_d[l, :].partition_broadcast(128)), writes=[lb])
                Sc.dma("sync", ch_misc[1], DMA(lamc[:], lamc_d[l, :, :]), writes=[lb])
                Sc.dma("sync", ch_misc[2], DMA(gA[:], dan_d[l, :].rearrange("(p o) -> p o", o=1)), writes=[lb])
                l4 = lamt[:].rearrange("p (a b d) -> p a b d", a=2, b=2)
                Sc.op("vector", TT(lprod[:].rearrange("p (a d) -> p a d", a=2), l4[:, :, 0, :], l4[:, :, 1, :], ALU.mult), reads=[lb], writes=[lb])
                Sc.op("vector", lambda e: e.tensor_reduce(out=lsum[:], in_=lprod[:].rearrange("p (a d) -> p a d", a=2), axis=AX.X, op=ALU.add),
                      reads=[lb], writes=[lb])
                Sc.op("scalar", ACT(lsum[:], lsum[:], AF.Exp), reads=[lb], writes=[lb])
                Sc.op("vector", TT(nlam[:], lsum[:, 1:2], lsum[:, 0:1], ALU.subtract), reads=[lb], writes=[lb])
                Sc.op("vector", TT(nlam[:], nlam[:], lamc[:, 0:1], ALU.subtract), reads=[lb], writes=[lb])
                Sc.op("vector", TT(gA[:], gA[:], lamc[:, 1:2], ALU.mult), reads=[lb], writes=[lb])

                def loadA(h):
                    wt, wbuf = wA[h % 2]
                    load_w_cols(ph, wt, wbuf, ch_w[h % 2], l, [(A_Q + h * 128, 128), (A_K + h * 128, 128), (A_V + h * 128, 128)])
                    et, eb = EAt[h % 2]
                    Sc.dma("sync", ch_w[2 + h % 2], DMA(et[:], ea_d[h, :, :]), reads=[P.ea[h]], writes=[eb])

                loadA(0)
                for h in range(4):
                    if h + 1 < 4:
                        loadA(h + 1)
                    wt, wbuf = wA[h % 2]
                    et, eb = EAt[h % 2]
                    qT, kT, vv, qb = qk[h % 2]
                    proj_feat(wt, wbuf, 0, lambda tg, bk: (qT[:, tg * 512:(tg + 1) * 512], bk[:]), [qb], 0.125)
                    proj_feat(wt, wbuf, 128, lambda tg, bk: (kT[:, tg * 512:(tg + 1) * 512], bk[:]), [qb], 1.0)
                    proj_tok(wt, wbuf, 256, 128, vv, qb, lambda kc, t: xT[:, kc, t * 128:(t + 1) * 128])
                    for g in range(G):
                        nk = 4 * g + 4
                        n = 2 * nk
                        Ub = [(banks[2], P.bank[2]), (banks[3], P.bank[3])]
                        Sb_ = [(banks[4], P.bank[4]), (banks[5], P.bank[5])]
                        sbk = [(banks[0], P.bank[0]), (banks[1], P.bank[1]), (banks[6], P.bank[6]), (banks[7], P.bank[7])]
                        pts = {}
                        prs = {}

                        def stage_S(kt):
                            for m in range(2):
                                bk, bb = sbk[(2 * kt + m) % 4]
                                Sc.op("tensor", MM(bk[:], kT[64 * m:64 * m + 64, kt * 128:(kt + 1) * 128],
                                                   qT[64 * m:64 * m + 64, g * 512:(g + 1) * 512], True, True),
                                      reads=[qb], writes=[bb], signal=(m == 1))

                        def stage_E(kt):
                            for m in range(2):
                                bk, bb = sbk[(2 * kt + m) % 4]
                                pr, prb = praw.next()
                                Sc.op("scalar", ACT(pr[:], bk[:], AF.Exp), reads=[bb], writes=[prb])
                                prs[(kt, m)] = (pr, prb)

                        def stage_M(kt):
                            for m in range(2):
                                pr, prb = prs.pop((kt, m))
                                pt, ptb = pTt.next()
                                off = PADA + g * 512 - kt * 128
                                Sc.op("vector", TT(pt[:], pr[:], et[:, off:off + 512], ALU.mult), reads=[prb, eb], writes=[ptb])
                                pts[(kt, m)] = (pt, ptb)

                        def stage_V(kt):
                            for m in range(2):
                                pt, ptb = pts.pop((kt, m))
                                ub, ubb = Ub[m]
                                sb_, sbb = Sb_[m]
                                Sc.op("tensor", MM(ub[:], vv[:, kt, :], pt[:], kt == 0, kt == nk - 1),
                                      reads=[qb, ptb], writes=[ubb], signal=False)
                                Sc.op("tensor", MM(sb_[:], ones_bf[:], pt[:], kt == 0, kt == nk - 1),
                                      reads=[P.const, ptb], writes=[sbb], signal=(m == 1))

                        for it in range(nk + 2):
                            rd = [qb, P.const]
                            wr = []
                            if it < nk:
                                wr += [sbk[(2 * it + m) % 4][1] for m in range(2)]
                            if it >= 2:
                                rd += [pts[(it - 2, m)][1] for m in range(2)]
                                wr += [Ub[0][1], Ub[1][1], Sb_[0][1], Sb_[1][1]]
                            Sc.prewait("tensor", rd, wr)
                            if it < nk:
                                stage_S(it)
                            if it >= 2:
                                stage_V(it - 2)
                            if it < nk:
                                stage_E(it)
                            if 1 <= it <= nk:
                                stage_M(it - 1)
                        Sc.op("scalar", ACT(r0[:], banks[4][:], AF.Ln), reads=[P.bank[4]], writes=[postb])
                        Sc.op("scalar", ACT(r1[:], banks[5][:], AF.Ln), reads=[P.bank[5]], writes=[postb])
                        Sc.op("scalar", ACT(r0[:], r0[:], AF.Exp, scale=-1.0), reads=[postb], writes=[postb])
                        Sc.op("scalar", ACT(r1[:], r1[:], AF.Exp, scale=-1.0), reads=[postb], writes=[postb])
                        Sc.op("vector", TT(t0[:], banks[2][:], r0[:], ALU.mult), reads=[P.bank[2], postb], writes=[postb])
                        Sc.op("vector", TT(t1[:], banks[3][:], r1[:], ALU.mult), reads=[P.bank[3], postb], writes=[postb])
                        Sc.op("vector", STT(oo[:], t1[:], nlam[:, 0:1], t0[:], ALU.mult, ALU.add), reads=[postb, lb], writes=[postb])
                        Sc.op("scalar", ACT(sq[:], oo[:], AF.Square), reads=[postb], writes=[postb])
                        Sc.op("tensor", MM(banks[0][:], ones_bf[:], sq[:], True, True), reads=[postb, P.const], writes=[P.bank[0]])
                        Sc.op("scalar", ACT(lnv[:], banks[0][:], AF.Ln, scale=1.0 / 128.0, bias=RMS_EPS), reads=[P.bank[0]], writes=[postb])
                        Sc.op("scalar", ACT(lnv[:], lnv[:], AF.Exp, scale=-0.5), reads=[postb], writes=[postb])
                        os_, osb = ostage.next()
                        Sc.op("vector", STT(os_[:], oo[:], gA[:, 0:1], lnv[:], ALU.mult, ALU.mult), reads=[postb, lb], writes=[osb])
                        Sc.dma("sync", ch_ot, DMA(ot_d[h, :, g * 512:(g + 1) * 512], os_[:]), reads=[osb], writes=[P.ot[h][g]])
                end_phase()

            with ExitStack() as ph:
                qk = [(sbuf(ph, "qB%d" % i, [128, S], BF16), sbuf(ph, "kB%d" % i, [128, S], BF16),
                       sbuf(ph, "vB%d" % i, [128, T, 128], BF16), Buf("qkvB%d" % i)) for i in range(2)]
                wB = [(sbuf(ph, "wB%d" % i, [128, 8, 384], BF16), Buf("wB%d" % i)) for i in range(2)]
                e32 = Rot([(sbuf(ph, "e32_%d" % i, [128, 512], F32), Buf()) for i in range(3)])
                spt = Rot([(sbuf(ph, "sp%d" % i, [128, 512], BF16), Buf()) for i in range(5)])
                tmpt = Rot([(sbuf(ph, "tmpB%d" % i, [128, 512], F32), Buf()) for i in range(3)])
                at = Rot([(sbuf(ph, "aB%d" % i, [128, 512], BF16), Buf()) for i in range(6)])
                csb = [(sbuf(ph, "csb%d" % i, [128, 512], F32), Buf()) for i in range(2)]
                ostage = Rot([(sbuf(ph, "ostB%d" % i, [128, 512], BF16), Buf()) for i in range(2)])
                maskB = sbuf(ph, "maskB", [128, 4, 512], BF16)
                mkb = Buf("maskB")
                for dd in range(4):
                    Sc.op("gpsimd", ASEL(maskB[:, dd, :], onesf[:], [[1, 512]], ALU.is_gt, 0.0, -128 * dd, -1),
                          reads=[P.const], writes=[mkb])

                def loadB(hp):
                    wt, wbuf = wB[hp % 2]
                    load_w_cols(ph, wt, wbuf, ch_w[hp % 2], l, [(B_Q + hp * 128, 128), (B_K + hp * 128, 128), (B_V + hp * 128, 128)])

                loadB(0)
                for hp in range(4):
                    if hp + 1 < 4:
                        loadB(hp + 1)
                    wt, wbuf = wB[hp % 2]
                    qT, kT, vv, qb = qk[hp % 2]
                    proj_feat(wt, wbuf, 0, lambda tg, bk: (qT[:, tg * 512:(tg + 1) * 512], bk[:]), [qb], 0.125)
                    proj_feat(wt, wbuf, 128, lambda tg, bk: (kT[:, tg * 512:(tg + 1) * 512], bk[:]), [qb], 1.0)
                    proj_tok(wt, wbuf, 256, 128, vv, qb, lambda kc, t: xT[:, kc, t * 128:(t + 1) * 128])
                    for g in range(G):
                        n = 4 * g + 4
                        for hh in range(2):
                            rs = slice(64 * hh, 64 * hh + 64)
                            zb = [(banks[0], P.bank[0]), (banks[1], P.bank[1])]
                            za = [(banks[2], P.bank[2]), (banks[3], P.bank[3])]
                            cbk = [(banks[4], P.bank[4]), (banks[7], P.bank[7])]
                            NFILL = 2
                            st_e = {}
                            st_sp = {}
                            st_a = {}

                            def kt_of(i):
                                return 4 * g + 3 - i

                            def stZ1_pe(i):
                                kt = kt_of(i)
                                bk, bb = zb[i % 2]
                                Sc.op("tensor", MM(bk[:], kT[rs, kt * 128:(kt + 1) * 128], qT[rs, g * 512:(g + 1) * 512], True, True),
                                      reads=[qb], writes=[bb])

                            def stZ1_act(i):
                                bk, bb = zb[i % 2]
                                e_, eb_ = e32.next()
                                Sc.op("scalar", ACT(e_[:], bk[:], AF.Exp), reads=[bb], writes=[eb_])
                                st_e[i] = (e_, eb_)

                            def stZ2(i):
                                e_, eb_ = st_e.pop(i)
                                sp_, spb = spt.next()
                                Sc.op("scalar", ACT(sp_[:], e_[:], AF.Ln, bias=1.0), reads=[eb_], writes=[spb])
                                if i <= 3:
                                    Sc.op("gpsimd", TT(sp_[:], sp_[:], maskB[:, 3 - i, :], ALU.mult), reads=[spb, mkb], writes=[spb])
                                st_sp[i] = (sp_, spb)

                            def stA_pe(i):
                                sp_, spb = st_sp[i]
                                kt = kt_of(i)
                                bk, bb = za[i % 2]
                                Sc.op("tensor", MM(bk[:], uneg[:], sp_[:], True, False), reads=[spb, P.const], writes=[bb], signal=False)
                                Sc.op("tensor", MM(bk[:], kT[rs, kt * 128:(kt + 1) * 128], qT[rs, g * 512:(g + 1) * 512], False, True),
                                      reads=[qb], writes=[bb])
                                if i < n - 1:
                                    cb_, cbb_ = cbk[i % 2]
                                    Sc.op("tensor", MM(cb_[:], nones_bf[:], sp_[:], True, True), reads=[spb, P.const], writes=[cbb_])

                            def stA_dve(i):
                                st_sp.pop(i)
                                if i < n - 1:
                                    cb_, cbb_ = cbk[i % 2]
                                    cs, csbuf = csb[i % 2]
                                    if i == 0:
                                        Sc.op("vector", CP(cs[:], cb_[:]), reads=[cbb_], writes=[csbuf])
                                    else:
                                        pc, pcb = csb[(i - 1) % 2]
                                        Sc.op("vector", TT(cs[:], cb_[:], pc[:], ALU.add), reads=[cbb_, pcb], writes=[csbuf])

                            def stD(i):
                                bk, bb = za[i % 2]
                                a_, ab_ = at.next()
                                if i == 0:
                                    Sc.op("scalar", ACT(a_[:], bk[:], AF.Exp), reads=[bb], writes=[ab_])
                                else:
                                    pc, pcb = csb[(i - 1) % 2]
                                    tm, tmb = tmpt.next()
                                    Sc.op("vector", TT(tm[:], bk[:], pc[:], ALU.add), reads=[bb, pcb], writes=[tmb])
                                    Sc.op("scalar", ACT(a_[:], tm[:], AF.Exp), reads=[tmb], writes=[ab_])
                                if i <= 3:
                                    Sc.op("gpsimd", TT(a_[:], a_[:], maskB[:, 3 - i, :], ALU.mult), reads=[ab_, mkb], writes=[ab_])
                                st_a[i] = (a_, ab_)

                            def stV(i):
                                kt = kt_of(i)
                                a_, ab_ = st_a.pop(i)
                                Sc.op("tensor", MM(banks[5][rs, :], vv[:, kt, rs], a_[:], i == 0, i == n - 1),
                                      reads=[qb, ab_], writes=[P.bank[5]], signal=(i == n - 1))

                            for it in range(n + 4):
                                rd = [qb, P.const]
                                wr = []
                                if it < n:
                                    wr.append(zb[it % 2][1])
                                if 0 <= it - 2 < n:
                                    rd.append(st_sp[it - 2][1])
                                    wr.append(za[(it - 2) % 2][1])
                                    if it - 2 < n - 1:
                                        wr.append(cbk[(it - 2) % 2][1])
                                if 0 <= it - 4 < n:
                                    rd.append(st_a[it - 4][1])
                                    wr.append(P.bank[5])
                                Sc.prewait("tensor", rd, wr)
                                if it < n:
                                    stZ1_pe(it)
                                if 0 <= it - 2 < n:
                                    stA_pe(it - 2)
                                if 0 <= it - 4 < n:
                                    stV(it - 4)
                                if it < n:
                                    for _f in range(NFILL):
                                        Sc.op("tensor", MM(banks[6][:], ones_bf[:], qT[:, g * 512:(g + 1) * 512], True, True),
                                              reads=[qb, P.const], writes=[P.bank[6]], signal=False)
                                if it < n:
                                    stZ1_act(it)
                                if 0 <= it - 2 < n:
                                    stD(it - 2)
                                if it < n:
                                    stZ2(it)
                                if 0 <= it - 2 < n:
                                    stA_dve(it - 2)
                        os_, osb = ostage.next()
                        Sc.op("vector", CP(os_[:], banks[5][:]), reads=[P.bank[5]], writes=[osb])
                        Sc.dma("sync", ch_ot, DMA(ot_d[4 + hp, :, g * 512:(g + 1) * 512], os_[:]), reads=[osb], writes=[P.ot[4 + hp][g]])
                end_phase()

            with ExitStack() as ph:
                qk = [(sbuf(ph, "qC%d" % i, [128, S], BF16), sbuf(ph, "kC%d" % i, [128, S], BF16),
                       sbuf(ph, "vC%d" % i, [128, T, 128], BF16), Buf("qkvC%d" % i)) for i in range(2)]
                wC = [(sbuf(ph, "wC%d" % i, [128, 8, 384], BF16), Buf("wC%d" % i)) for i in range(2)]
                praw = Rot([(sbuf(ph, "prawC%d" % i, [128, 256], F32), Buf()) for i in range(3)])
                pTt = Rot([(sbuf(ph, "pTC%d" % i, [128, 256], BF16), Buf()) for i in range(3)])
                Uacc = sbuf(ph, "Uacc", [128, S], F32)
                Sacc = sbuf(ph, "Sacc", [128, S], F32)
                accb = Buf("acc")
                rc = sbuf(ph, "rcC", [128, 512], F32)
                rcb = Buf("rc")
                ostage = Rot([(sbuf(ph, "ostC%d" % i, [128, 512], BF16), Buf()) for i in range(2)])
                scale_c = 128.0 ** -0.5
                ED = sbuf(ph, "ED", [128, 12, 256], BF16)
                rawD = sbuf(ph, "rawD", [128, 12, 256], F32)
                rawDb = Buf("rawD")
                Sc.dma("sync", ch_misc[0], DMA(rawD[:], biasD_d.rearrange("h p f -> p h f")), writes=[rawDb])
                Sc.op("scalar", ACT(rawD[:], rawD[:], AF.Exp), reads=[rawDb], writes=[rawDb])
                for hh in range(12):
                    Sc.op("gpsimd", ASEL(rawD[:, hh, :], rawD[:, hh, :], [[1, 256]], ALU.is_ge, 0.0, 0, -1), reads=[rawDb], writes=[rawDb])
                    Sc.op("gpsimd", ASEL(ED[:, hh, :], rawD[:, hh, :], [[-1, 256]], ALU.is_ge, 0.0, 128, 1), reads=[rawDb], writes=[P.ED])

                def loadC(idx):
                    gi, hs = idx % 3, idx // 3
                    hd = gi * 4 + hs
                    wt, wbuf = wC[idx % 2]
                    load_w_cols(ph, wt, wbuf, ch_w[idx % 2], l, [(C_Q + hd * 128, 128), (C_K + hd * 128, 128), (C_V + hd * 128, 128)])

                loadC(0)
                for idx in range(12):
                    if idx + 1 < 12:
                        loadC(idx + 1)
                    gi, hs = idx % 3, idx // 3
                    hd = gi * 4 + hs
                    dl = DIL[gi][1]
                    L = S // dl
                    nblk = L // 128
                    wt, wbuf = wC[idx % 2]
                    qT, kT, vv, qb = qk[idx % 2]

                    def dstq(tg, bk, dst=None):
                        if dl == 1:
                            return dst[:, tg * 512:(tg + 1) * 512], bk[:]
                        m0 = tg * (512 // dl)
                        return (dst[:].rearrange("p (b m) -> p b m", b=dl)[:, :, m0:m0 + 512 // dl],
                                bk[:].rearrange("p (a b) -> p b a", b=dl))

                    proj_feat(wt, wbuf, 0, lambda tg, bk: dstq(tg, bk, qT), [qb], scale_c)
                    proj_feat(wt, wbuf, 128, lambda tg, bk: dstq(tg, bk, kT), [qb], 1.0)

                    def tokap(kc, pi):
                        r, j = pi // nblk, pi % nblk
                        s0 = r + dl * 128 * j
                        return xT[:, kc, s0:s0 + dl * 127 + 1:dl]

                    proj_tok(wt, wbuf, 256, 128, vv, qb, tokap)
                    ub = [(banks[2], P.bank[2]), (banks[3], P.bank[3])]
                    sb_ = [(banks[4], P.bank[4]), (banks[5], P.bank[5])]
                    sbk = [(banks[0], P.bank[0]), (banks[1], P.bank[1])]
                    pts = {}

                    def cS(pi):
                        r, j = pi // nblk, pi % nblk
                        ncol = 256 if j < nblk - 1 else 128
                        bk, bb = sbk[pi % 2]
                        Sc.op("tensor", MM(bk[:, 0:ncol], kT[:, pi * 128:(pi + 1) * 128], qT[:, pi * 128:pi * 128 + ncol], True, True),
                              reads=[qb], writes=[bb])
                        pr, prb = praw.next()
                        Sc.op("scalar", ACT(pr[:, 0:ncol], bk[:, 0:ncol], AF.Exp), reads=[bb], writes=[prb])
                        pt, ptb = pTt.next()
                        eng = "vector" if pi % 2 == 0 else "gpsimd"
                        Sc.op(eng, TT(pt[:, 0:ncol], pr[:, 0:ncol], ED[:, hd, 0:ncol], ALU.mult), reads=[prb, P.ED], writes=[ptb])
                        pts[pi] = (pt, ptb)

                    def cV(pi):
                        r, j = pi // nblk, pi % nblk
                        pt, ptb = pts.pop(pi)
                        u, ubb = ub[(pi // 4) % 2]
                        s_, sbb = sb_[(pi // 4) % 2]
                        c = pi % 4
                        Sc.op("tensor", MM(u[:, c * 128:(c + 1) * 128], vv[:, pi, :], pt[:, 0:128], j == 0, True),
                              reads=[qb, ptb], writes=[ubb], signal=False)
                        Sc.op("tensor", MM(s_[:, c * 128:(c + 1) * 128], ones_bf[:], pt[:, 0:128], j == 0, True),
                              reads=[P.const, ptb], writes=[sbb], signal=True)
                        if c == 3 or pi == T - 1:
                            flush(pi // 4)
                        if j < nblk - 1:
                            u2, ubb2 = ub[((pi + 1) // 4) % 2]
                            s2, sbb2 = sb_[((pi + 1) // 4) % 2]
                            c2 = (pi + 1) % 4
                            Sc.op("tensor", MM(u2[:, c2 * 128:(c2 + 1) * 128], vv[:, pi, :], pt[:, 128:256], True, False),
                                  reads=[qb, ptb], writes=[ubb2], signal=False)
                            Sc.op("tensor", MM(s2[:, c2 * 128:(c2 + 1) * 128], ones_bf[:], pt[:, 128:256], True, False),
                                  reads=[P.const, ptb], writes=[sbb2], signal=False)

                    def flush(bi):
                        u, ubb = ub[bi % 2]
                        s_, sbb = sb_[bi % 2]
                        for c in range(4):
                            pi = bi * 4 + c
                            if pi >= T:
                                break
                            r, j = pi // nblk, pi % nblk
                            s0 = r + dl * 128 * j
                            dU = Uacc[:, s0:s0 + dl * 127 + 1:dl]
                            dS = Sacc[:, s0:s0 + dl * 127 + 1:dl]
                            if gi == 0:
                                Sc.op("vector", CP(dU, u[:, c * 128:(c + 1) * 128]), reads=[ubb], writes=[accb])
                                Sc.op("vector", CP(dS, s_[:, c * 128:(c + 1) * 128]), reads=[sbb], writes=[accb])
                            else:
                                Sc.op("vector", TT(dU, u[:, c * 128:(c + 1) * 128], dU, ALU.add), reads=[ubb, accb], writes=[accb])
                                Sc.op("vector", TT(dS, s_[:, c * 128:(c + 1) * 128], dS, ALU.add), reads=[sbb, accb], writes=[accb])

                    for step in range(T + 1):
                        if step < T:
                            cS(step)
                        if step >= 1:
                            cV(step - 1)
                    if gi == 2:
                        for g in range(G):
                            Sc.op("vector", lambda e, g=g: e.reciprocal(out=rc[:], in_=Sacc[:, g * 512:(g + 1) * 512]), reads=[accb], writes=[rcb])
                            os_, osb = ostage.next()
                            Sc.op("vector", TT(os_[:], Uacc[:, g * 512:(g + 1) * 512], rc[:], ALU.mult), reads=[accb, rcb], writes=[osb])
                            Sc.dma("sync", ch_ot, DMA(ot_d[8 + hs, :, g * 512:(g + 1) * 512], os_[:]), reads=[osb], writes=[P.ot[8 + hs][g]])
                end_phase()

            def ln_tail(ph_t, hh, hb, gbc, bbc, lnb, t, dst_d, dst_buf, ch_dst, stats, mv, xn, xnb_, xnf_b, xnb_b):
                Sc.op("vector", lambda e: e.bn_stats(out=stats[:, 0:6], in_=hh[:, 0:512]), reads=[hb], writes=[lnb])
                Sc.op("vector", lambda e: e.bn_stats(out=stats[:, 6:12], in_=hh[:, 512:1024]), reads=[hb], writes=[lnb])
                Sc.op("vector", lambda e: e.bn_aggr(out=mv[:], in_=stats[:]), reads=[lnb], writes=[lnb])
                Sc.op("scalar", ACT(mv[:, 1:2], mv[:, 1:2], AF.Sqrt, bias=LN_EPS), reads=[lnb], writes=[lnb])
                Sc.op("vector", lambda e: e.reciprocal(out=mv[:, 1:2], in_=mv[:, 1:2]), reads=[lnb], writes=[lnb])
                Sc.op("vector", TS(xn[:], hh[:], mv[:, 0:1], mv[:, 1:2], ALU.subtract, ALU.mult), reads=[hb, lnb], writes=[xnf_b])
                Sc.op("gpsimd", TT(xn[:], xn[:], gbc[:], ALU.mult), reads=[xnf_b, P.const], writes=[xnf_b])
                Sc.op("gpsimd", TT(xn[:], xn[:], bbc[:], ALU.add), reads=[xnf_b, P.const], writes=[xnf_b])
                Sc.dma("sync", ch_dst, DMA(dst_d[t * 128:(t + 1) * 128, :], xn[:]), reads=[xnf_b], writes=[dst_buf])
                Sc.op("gpsimd", CP(xnb_[:], xn[:]), reads=[xnf_b], writes=[xnb_b])
                return lambda: transposes_to_xT(xnb_, xnb_b, t)

            with ExitStack() as ph:
                wg = [(sbuf(ph, "wg%d" % i, [128, 8, 512], BF16), Buf("wg%d" % i)) for i in range(2)]
                wbr = [(sbuf(ph, "wbr%d" % i, [128, 4, 1024], BF16), Buf("wbr%d" % i)) for i in range(3)]
                wo = sbuf(ph, "wo", [128, 8, 1024], BF16)
                wob = Buf("wo")
                gbc = sbuf(ph, "g1bc", [128, D], F32)
                bbc = sbuf(ph, "b1bc", [128, D], F32)
                otile = [(sbuf(ph, "otile%d" % i, [128, 12, 512], BF16), Buf("otile%d" % i)) for i in range(1)]
                merged = sbuf(ph, "merged", [128, 8, 512], BF16)
                mergb = Buf("merged")
                sg = Rot([(sbuf(ph, "sgM%d" % i, [128, 512], F32), Buf()) for i in range(2)])
                macc = sbuf(ph, "macc", [128, 4, 512], F32)
                maccb = Buf("macc")
                xres = [(sbuf(ph, "xres%d" % i, [128, D], F32), Buf()) for i in range(2)]
                hh = [(sbuf(ph, "hM%d" % i, [128, D], F32), Buf()) for i in range(2)]
                xn = [(sbuf(ph, "xnM%d" % i, [128, D], F32), Buf()) for i in range(2)]
                xnb = [(sbuf(ph, "xnbM%d" % i, [128, D], BF16), Buf()) for i in range(2)]
                stats = sbuf(ph, "statsM", [128, 12], F32)
                mv = sbuf(ph, "mvM", [128, 2], F32)
                lnb = Buf("lnM")

                for i in range(3):
                    Sc.dma("gpsimd", ch_w[2 + i], DMA(wbr[i][0][:], wb_d[i][l, :, :].rearrange("(c p) n -> p c n", p=128)), writes=[wbr[i][1]])
                Sc.dma("gpsimd", ch_w[5], DMA(wo[:], wout_d[l, :, :].rearrange("(c p) n -> p c n", p=128)), writes=[wob])
                Sc.dma("sync", ch_misc[0], DMA(gbc[:], ln1g_d[l, :].partition_broadcast(128)), writes=[P.const])
                Sc.dma("sync", ch_misc[1], DMA(bbc[:], ln1b_d[l, :].partition_broadcast(128)), writes=[P.const])
                gq = Rot([(banks[0], P.bank[0]), (banks[1], P.bank[1])])
                bq = Rot([(banks[2], P.bank[2]), (banks[3], P.bank[3])])
                kk = 0
                pend = [None]
                for tg in range(G):
                    ot_, otb = otile[0]
                    Sc.dma("sync", ch_misc[2 + tg % 2], DMA(ot_[:], ot_d[:, :, tg * 512:(tg + 1) * 512].rearrange("c p n -> p c n")),
                           reads=[P.ot[i][tg] for i in range(12)], writes=[otb])
                    for half2 in range(2):
                        for i in range(3):
                            wgt, wgb = wg[kk % 2]
                            c0 = GATE + i * 1024 + half2 * 512
                            Sc.dma("gpsimd", ch_w[kk % 2], DMA(wgt[:], w_in_d[l, :, c0:c0 + 512].rearrange("(c p) n -> p c n", p=128)),
                                   writes=[wgb])
                            kk += 1
                            for oc4 in range(4):
                                oc = half2 * 4 + oc4
                                gb_, gbb = gq.next()
                                for kc in range(8):
                                    Sc.op("tensor", MM(gb_[:], wgt[:, kc, oc4 * 128:(oc4 + 1) * 128], xT[:, kc, tg * 512:(tg + 1) * 512], kc == 0, kc == 7),
                                          reads=[wgb, P.xT[tg]], writes=[gbb], signal=(kc == 7))
                                bb_, bbb = bq.next()
                                for c4 in range(4):
                                    Sc.op("tensor", MM(bb_[:], wbr[i][0][:, c4, oc * 128:(oc + 1) * 128], ot_[:, 4 * i + c4, :], c4 == 0, c4 == 3),
                                          reads=[wbr[i][1], otb], writes=[bbb], signal=(c4 == 3))
                                s_, sb2 = sg.next()
                                Sc.op("scalar", ACT(s_[:], gb_[:], AF.Sigmoid), reads=[gbb], writes=[sb2])
                                if i == 0:
                                    Sc.op("vector", TT(macc[:, oc4, :], s_[:], bb_[:], ALU.mult), reads=[sb2, bbb], writes=[maccb])
                                elif i == 1:
                                    Sc.op("vector", TT(s_[:], s_[:], bb_[:], ALU.mult), reads=[sb2, bbb], writes=[sb2])
                                    Sc.op("gpsimd", TT(macc[:, oc4, :], macc[:, oc4, :], s_[:], ALU.add), reads=[sb2, maccb], writes=[maccb])
                                else:
                                    Sc.op("vector", TT(s_[:], s_[:], bb_[:], ALU.mult), reads=[sb2, bbb], writes=[sb2])
                                    Sc.op("gpsimd", TT(merged[:, oc, :], macc[:, oc4, :], s_[:], ALU.add), reads=[sb2, maccb], writes=[mergb])
                    for tt_ in range(4):
                        t = tg * 4 + tt_
                        xr, xrb = xres[t % 2]
                        rd = [] if res_in_bufs is None else [res_in_bufs[t]]
                        Sc.dma("sync", ch_misc[4 + t % 2], DMA(xr[:], res_in[t * 128:(t + 1) * 128, :]), reads=rd, writes=[xrb])
                        h_, hb = hh[t % 2]
                        for half in range(2):
                            yb, ybb = (banks[4], P.bank[4]) if half == 0 else (banks[5], P.bank[5])
                            for oc in range(8):
                                Sc.op("tensor", MM(yb[:], merged[:, oc, tt_ * 128:(tt_ + 1) * 128], wo[:, oc, half * 512:(half + 1) * 512], oc == 0, oc == 7),
                                      reads=[mergb, wob], writes=[ybb], signal=(oc == 7))
                            Sc.op("vector", STT(h_[:, half * 512:(half + 1) * 512], xr[:, half * 512:(half + 1) * 512], ALPHA, yb[:], ALU.mult, ALU.add),
                                  reads=[xrb, ybb], writes=[hb])
                        if pend[0] is not None:
                            pend[0]()
                        pend[0] = ln_tail(ph, h_, hb, gbc, bbc, lnb, t, res1_d, P.res1[t], ch_res1, stats, mv, xn[t % 2][0], xnb[t % 2][0], xn[t % 2][1], xnb[t % 2][1])
                if pend[0] is not None:
                    pend[0]()
                end_phase()

            for hp_ in range(2):
                with ExitStack() as ph:
                    wupr = sbuf(ph, "wupr", [128, 8, 2048], BF16)
                    wupb = [Buf() for _ in range(4)]
                    wdnr = sbuf(ph, "wdnr", [128, 16, 1024], BF16)
                    wdnb = [Buf() for _ in range(4)]
                    hidT = sbuf(ph, "hidT", [128, 16, 512], BF16)
                    hidb = Buf("hidT")
                    r32 = Rot([(sbuf(ph, "r32_%d" % i, [128, 512], F32), Buf()) for i in range(2)])
                    for q4 in range(4):
                        c0 = hp_ * 2048 + q4 * 512
                        Sc.dma("gpsimd", ch_w[q4], DMA(wupr[:, :, q4 * 512:(q4 + 1) * 512], wup_d[l, :, c0:c0 + 512].rearrange("(c p) n -> p c n", p=128)),
                               writes=[wupb[q4]])
                    for q4 in range(4):
                        r0_ = hp_ * 2048 + q4 * 512
                        Sc.dma("gpsimd", ch_w[4 + q4], DMA(wdnr[:, q4 * 4:(q4 + 1) * 4, :], wdn_d[l, r0_:r0_ + 512, :].rearrange("(c p) n -> p c n", p=128)),
                               writes=[wdnb[q4]])
                    hq = Rot([(banks[0], P.bank[0]), (banks[1], P.bank[1])])
                    pendF = [None]
                    if hp_ == 0:
                        cpart = [(sbuf(ph, "cpart%d" % i, [128, D], F32), Buf()) for i in range(2)]
                    else:
                        wpg = sbuf(ph, "wpg", [128, 8, 1024], BF16)
                        wpgb = Buf("wpg")
                        wpl = sbuf(ph, "wpl", [128, 2, 1024], BF16)
                        wplb = Buf("wpl")
                        ptile = [(sbuf(ph, "ptile%d" % i, [128, 2, 512], BF16), Buf()) for i in range(1)]
                        gbc = sbuf(ph, "g2bc", [128, D], F32)
                        bbc = sbuf(ph, "b2bc", [128, D], F32)
                        x1 = [(sbuf(ph, "x1_%d" % i, [128, D], F32), Buf()) for i in range(2)]
                        c1 = [(sbuf(ph, "c1_%d" % i, [128, D], F32), Buf()) for i in range(1)]
                        sgt = Rot([(sbuf(ph, "sgF%d" % i, [128, 512], F32), Buf()) for i in range(2)])
                        xn = [(sbuf(ph, "xnF%d" % i, [128, D], F32), Buf()) for i in range(1)]
                        xnb = [(sbuf(ph, "xnbF%d" % i, [128, D], BF16), Buf()) for i in range(2)]
                        stats = sbuf(ph, "statsF", [128, 12], F32)
                        mv = sbuf(ph, "mvF", [128, 2], F32)
                        lnb = Buf("lnF")
                        Sc.dma("gpsimd", ch_w[8], DMA(wpg[:], wpg_d[l, :, :].rearrange("(c p) n -> p c n", p=128)), writes=[wpgb])
                        Sc.dma("gpsimd", ch_w[9], DMA(wpl[:], wple_d[l, :, :].rearrange("(c p) n -> p c n", p=128)), writes=[wplb])
                        Sc.dma("sync", ch_misc[0], DMA(gbc[:], ln2g_d[l, :].partition_broadcast(128)), writes=[P.const])
                        Sc.dma("sync", ch_misc[1], DMA(bbc[:], ln2b_d[l, :].partition_broadcast(128)), writes=[P.const])
                    for tg in range(G):
                        if hp_ == 1:
                            pt_, ptb = ptile[0]
                            Sc.dma("gpsimd", ch_w[10], DMA(pt_[:], pT_d[l, :, tg * 512:(tg + 1) * 512].rearrange("(c p) n -> p c n", p=128)), writes=[ptb])
                        for hc in range(16):
                            hb_, hbb = hq.next()
                            for kc in range(8):
                                Sc.op("tensor", MM(hb_[:], wupr[:, kc, hc * 128:(hc + 1) * 128], xT[:, kc, tg * 512:(tg + 1) * 512], kc == 0, kc == 7),
                                      reads=[wupb[hc // 4], P.xT[tg]], writes=[hbb], signal=(kc == 7))
                            r_, rb = r32.next()
                            Sc.op("scalar", ACT(r_[:], hb_[:], AF.Relu), reads=[hbb], writes=[rb])
                            Sc.op("vector", STT(hidT[:, hc, :], hb_[:], 0.0, r_[:], ALU.max, ALU.mult), reads=[hbb, rb], writes=[hidb])
                        for tt_ in range(4):
                            t = tg * 4 + tt_
                            cb2 = [(banks[2], P.bank[2]), (banks[3], P.bank[3])] if tt_ % 2 == 0 else [(banks[4], P.bank[4]), (banks[5], P.bank[5])]
                            for half in range(2):
                                cs = slice(half * 512, (half + 1) * 512)
                                cb, cbb = cb2[half]
                                for hc in range(16):
                                    Sc.op("tensor", MM(cb[:], hidT[:, hc, tt_ * 128:(tt_ + 1) * 128], wdnr[:, hc, cs], hc == 0, hc == 15),
                                          reads=[hidb, wdnb[hc // 4]], writes=[cbb], signal=(hc == 15))
                            if hp_ == 0:
                                cp_, cpb = cpart[t % 2]
                                Sc.op("scalar", ACT(cp_[:, 0:512], cb2[0][0][:], AF.Copy), reads=[cb2[0][1]], writes=[cpb])
                                Sc.op("vector", CP(cp_[:, 512:1024], cb2[1][0][:]), reads=[cb2[1][1]], writes=[cpb])
                                Sc.dma("sync", ch_res1, DMA(c1_d[t * 128:(t + 1) * 128, :], cp_[:]), reads=[cpb], writes=[P.c1[t]])
                            else:
                                h_, hb = x1[t % 2]
                                c1t, c1b = c1[0]
                                Sc.dma("sync", ch_misc[2 + t % 2], DMA(h_[:], res1_d[t * 128:(t + 1) * 128, :]), reads=[P.res1[t]], writes=[hb])
                                Sc.dma("sync", ch_misc[4], DMA(c1t[:], c1_d[t * 128:(t + 1) * 128, :]), reads=[P.c1[t]], writes=[c1b])
                                for half in range(2):
                                    cs = slice(half * 512, (half + 1) * 512)
                                    cb, cbb = cb2[half]
                                    gbk, gbkb = (banks[6], P.bank[6])
                                    for kc in range(8):
                                        Sc.op("tensor", MM(gbk[:], xT[:, kc, t * 128:(t + 1) * 128], wpg[:, kc, cs], kc == 0, kc == 7),
                                              reads=[wpgb, P.xT[tg]], writes=[gbkb], signal=(kc == 7))
                                    pbk, pbkb = (banks[0], P.bank[0]) if half == 0 else (banks[1], P.bank[1])
                                    for c2 in range(2):
                                        Sc.op("tensor", MM(pbk[:], pt_[:, c2, tt_ * 128:(tt_ + 1) * 128], wpl[:, c2, cs], c2 == 0, c2 == 1),
                                              reads=[wplb, ptb], writes=[pbkb], signal=(c2 == 1))
                                    s_, sb2 = sgt.next()
                                    Sc.op("scalar", ACT(s_[:], gbk[:], AF.Sigmoid), reads=[gbkb], writes=[sb2])
                                    Sc.op("vector", TT(s_[:], s_[:], pbk[:], ALU.mult), reads=[sb2, pbkb], writes=[sb2])
                                    Sc.op("vector", STT(h_[:, cs], h_[:, cs], ALPHA, s_[:], ALU.mult, ALU.add), reads=[hb, sb2], writes=[hb])
                                    Sc.op("vector", TT(h_[:, cs], h_[:, cs], cb[:], ALU.add), reads=[hb, cbb], writes=[hb])
                                Sc.op("gpsimd", TT(h_[:], h_[:], c1t[:], ALU.add), reads=[hb, c1b], writes=[hb])
                                if pendF[0] is not None:
                                    pendF[0]()
                                pendF[0] = ln_tail(ph, h_, hb, gbc, bbc, lnb, t, out_d, P.outb[t], ch_out, stats, mv, xn[0][0], xnb[t % 2][0], xn[0][1], xnb[t % 2][1])
                    if pendF[0] is not None:
                        pendF[0]()
                    end_phase()

        Sc.wait_all("sync", [(ch_out, ch_out.val)])
        if dbg:
            Sc.wait_all("sync", [(ch_res1, ch_res1.val), (ch_ot, ch_ot.val)])
        with nc.Block() as block:
            Sc.emit(block)
    return nc


def _bucket(d):
    d = np.maximum(d, 0).astype(np.int64)
    dm = np.maximum(d, 1).astype(np.float32)
    lr = np.log(dm / np.float32(16.0)) / np.float32(math.log(2048 / 16))
    large = 16 + (lr * np.float32(16.0)).astype(np.int32)
    return np.where(d < 16, d, np.minimum(large, 31)).astype(np.int64)


def bias_tables(rel_bias, S):
    WA = S + PADA
    p = np.arange(128)[:, None]
    j = np.arange(WA)[None, :]
    bA = _bucket(j - PADA - p)
    biasA = np.ascontiguousarray(rel_bias[bA][:, :, 0:4].transpose(2, 0, 1)).astype(np.float32)
    f = np.arange(256)[None, :]
    step = np.clip(f - p, 0, 128)
    biasD = np.zeros((12, 128, 256), np.float32)
    for gi, (_, dl) in enumerate(DIL):
        bD = _bucket(step * dl)
        for hs in range(4):
            biasD[gi * 4 + hs] = rel_bias[bD, 4 + gi * 4 + hs]
    return biasA, biasD


def lam_consts(layers):
    out = np.zeros((len(layers), 128, 2), np.float32)
    for i, l in enumerate(layers):
        li = 0.8 - 0.6 * math.exp(-0.3 * l)
        out[i, :, 0] = li
        out[i, :, 1] = 1.0 - li
    return out


_WNAMES = ["w_in", "da_lambda", "da_norm", "w_branch_da", "w_branch_sb", "w_branch_dil", "w_out", "ln1_g", "ln1_b",
           "w_up", "w_down", "w_ple_gate", "w_ple", "ln2_g", "ln2_b"]
_PROG = {}


def get_program(S, NL):
    key = (S, NL)
    if key not in _PROG:
        _PROG[key] = build_program(S, NL)
    return _PROG[key]


def make_in_maps(inputs, xs, layers, S):
    f32 = lambda a: np.ascontiguousarray(np.asarray(a, dtype=np.float32))
    biasA, biasD = bias_tables(f32(inputs["rel_bias"]), S)
    lamc = lam_consts(layers)
    shared = {}
    for n in _WNAMES:
        a = f32(inputs[n])[layers]
        if n == "da_lambda":
            a = a.reshape(len(layers), 256)
        shared[n] = np.ascontiguousarray(a)
    shared["biasA"] = biasA
    shared["biasD"] = biasD
    shared["lamc"] = lamc
    p = inputs["p"]
    maps = []
    for b in range(len(xs)):
        m = dict(shared)
        m["x"] = f32(xs[b])
        m["pT"] = np.ascontiguousarray(np.asarray(p[layers, b], dtype=np.float32).transpose(0, 2, 1))
        maps.append(m)
    return maps


FUSED = True


def kernel(**inputs):
    x = np.asarray(inputs["x"], dtype=np.float32)
    B, S, _ = x.shape
    NLT = inputs["w_in"].shape[0]
    xs = [x[b] for b in range(B)]
    if FUSED:
        nc = get_program(S, NLT)
        maps = make_in_maps(inputs, xs, list(range(NLT)), S)
        res = run_bass_kernel_spmd(nc, maps, core_ids=list(range(B)))
        xs = [np.asarray(r["out"]) for r in res.results]
    else:
        nc = get_program(S, 1)
        for l in range(NLT):
            maps = make_in_maps(inputs, xs, [l], S)
            res = run_bass_kernel_spmd(nc, maps, core_ids=list(range(B)))
            xs = [np.asarray(r["out"]) for r in res.results]
    return np.stack(xs, axis=0).astype(np.float32)
```

```python
import math
import numpy as np
from contextlib import ExitStack
import concourse.bass as bass
import concourse.mybir as mybir
from concourse.bass_utils import run_bass_kernel_spmd

F32 = mybir.dt.float32
BF16 = mybir.dt.bfloat16
AF = mybir.ActivationFunctionType
ALU = mybir.AluOpType
AX = mybir.AxisListType

D = 1024
DFF = 4096
PLE = 256
INC = 10752
A_Q, A_K, A_V = 0, 512, 1024
B_Q, B_K, B_V = 1536, 2048, 2560
C_Q, C_K, C_V = 3072, 4608, 6144
GATE = 7680
DIL = ((128, 1), (512, 4), (2048, 16))
DEPTH = 4
ALPHA = (2 * DEPTH) ** 0.25
LN_EPS = 1e-5
RMS_EPS = 1e-5
PADA = 384


class Buf:
    __slots__ = ("name", "w", "r")

    def __init__(self, name=""):
        self.name = name
        self.w = None
        self.r = {}


class Src:
    def __init__(self, name, sem, step):
        self.name = name
        self.sem = sem
        self.val = 0
        self.step = step


class Sched:
    ENGS = ("tensor", "vector", "scalar", "gpsimd", "sync")

    def __init__(self, nc, stack):
        self.nc = nc
        self.stack = stack
        self.ops = {e: [] for e in self.ENGS}
        self.src = {}
        self.seen = {e: {} for e in self.ENGS}
        self.chans = []
        for e in self.ENGS:
            self.src[e] = Src(e, stack.enter_context(nc.semaphore("s_" + e)), 1)
        self.nops = 0
        self.nroll = 0

    def chan(self, name):
        c = Src(name, self.stack.enter_context(self.nc.semaphore("c_" + name)), 16)
        self.chans.append(c)
        return c

    def _need(self, eng, deps):
        best = {}
        for s, v in deps:
            if best.get(s, 0) < v:
                best[s] = v
        for s, v in best.items():
            if self.seen[eng].get(s, 0) >= v:
                continue
            self.seen[eng][s] = v
            self.ops[eng].append(("wait", s.sem, v))

    @staticmethod
    def _deps(reads, writes):
        deps = []
        for b in reads:
            if b.w is not None:
                deps.append(b.w)
        for b in writes:
            if b.w is not None:
                deps.append(b.w)
            for rs, rv in b.r.items():
                deps.append((rs, rv))
        return deps

    LIMIT = 30000

    def _roll(self, s):
        if s.val < self.LIMIT:
            return s
        self.nroll += 1
        n = Src(s.name, self.stack.enter_context(self.nc.semaphore("r%d_%s" % (self.nroll, s.name))), s.step)
        return n

    def op(self, eng, fn, reads=(), writes=(), signal=True):
        s = self._roll(self.src[eng])
        self.src[eng] = s
        deps = self._deps(reads, writes)
        if eng == "tensor":
            deps = [d for d in deps if d[0] is not s]
        else:
            deps = [d for d in deps if not (d[0] is s and d[1] > s.val)]
        self._need(eng, deps)
        if signal:
            s.val += 1
            tag = (s, s.val)
            self.ops[eng].append(("op", fn, s.sem, 1))
        else:
            tag = (s, s.val + 1)
            self.ops[eng].append(("op", fn, None, 0))
        self.nops += 1
        for b in reads:
            if b.r.get(tag[0], 0) < tag[1]:
                b.r[tag[0]] = tag[1]
        for b in writes:
            b.w = tag
            b.r = {}
        return tag

    def dma(self, eng, ch, fn, reads=(), writes=()):
        deps = self._deps(reads, writes)
        self._need(eng, deps)
        ch.val += 16
        tag = (ch, ch.val)
        self.ops[eng].append(("op", fn, ch.sem, 16))
        self.nops += 1
        for b in reads:
            if b.r.get(ch, 0) < ch.val:
                b.r[ch] = ch.val
        for b in writes:
            b.w = tag
            b.r = {}
        return tag

    def barrier(self):
        allsrc = [(self.src[e], self.src[e].val) for e in self.ENGS if self.src[e].val > 0]
        allsrc += [(c, c.val) for c in self.chans if c.val > 0]
        for e in self.ENGS:
            self._need(e, [d for d in allsrc if d[0] is not self.src[e]])

    def wait_all(self, eng, tags):
        self._need(eng, tags)

    def prewait(self, eng, reads=(), writes=()):
        s = self.src[eng]
        deps = self._deps(reads, writes)
        if eng == "tensor":
            deps = [d for d in deps if d[0] is not s]
        else:
            deps = [d for d in deps if not (d[0] is s and d[1] > s.val)]
        self._need(eng, deps)

    def emit(self, block):
        def mk(eng):
            lst = self.ops[eng]

            def body(e):
                for it in lst:
                    if it[0] == "wait":
                        e.wait_ge(it[1], it[2])
                    else:
                        ins = it[1](e)
                        if it[2] is not None:
                            ins.then_inc(it[2], it[3])
            return body
        block.tensor(mk("tensor"))
        block.vector(mk("vector"))
        block.scalar(mk("scalar"))
        block.gpsimd(mk("gpsimd"))
        block.sync(mk("sync"))
        self.ops = {e: [] for e in self.ENGS}


def MM(out, lhsT, rhs, start, stop):
    return lambda e: e.matmul(out, lhsT=lhsT, rhs=rhs, start=start, stop=stop)


def TR(out, in_, ident):
    return lambda e: e.transpose(out, in_, ident)


def ACT(out, in_, func, scale=1.0, bias=0.0):
    return lambda e: e.activation(out=out, in_=in_, func=func, bias=bias, scale=scale)


def CP(out, in_):
    return lambda e: e.tensor_copy(out=out, in_=in_)


def TT(out, in0, in1, op):
    return lambda e: e.tensor_tensor(out=out, in0=in0, in1=in1, op=op)


def TS(out, in0, s1, s2, op0, op1=None):
    if op1 is None:
        return lambda e: e.tensor_scalar(out=out, in0=in0, scalar1=s1, scalar2=None, op0=op0)
    return lambda e: e.tensor_scalar(out=out, in0=in0, scalar1=s1, scalar2=s2, op0=op0, op1=op1)


def STT(out, in0, scalar, in1, op0, op1):
    return lambda e: e.scalar_tensor_tensor(out=out, in0=in0, scalar=scalar, in1=in1, op0=op0, op1=op1)


def DMA(out, in_):
    return lambda e: e.dma_start(out=out, in_=in_)


def ASEL(out, in_, pattern, cmp, fill, base, cm):
    return lambda e: e.affine_select(out=out, in_=in_, pattern=pattern, compare_op=cmp, fill=fill,
                                     base=base, channel_multiplier=cm)


class Rot:
    def __init__(self, items):
        self.items = items
        self.i = 0

    def next(self):
        it = self.items[self.i % len(self.items)]
        self.i += 1
        return it


def build_program(S, NL, dbg=False):
    T = S // 128
    G = S // 512
    WA = S + PADA
    nc = bass.Bass("TRN2", target_bir_lowering=False)

    def din(name, shape):
        return nc.dram_tensor(name, list(shape), F32, kind="ExternalInput").ap()

    x_d = din("x", [S, D])
    pT_d = din("pT", [NL, PLE, S])
    w_in_d = din("w_in", [NL, D, INC])
    lam_d = din("da_lambda", [NL, 256])
    dan_d = din("da_norm", [NL, 128])
    wb_d = [din("w_branch_da", [NL, 512, D]), din("w_branch_sb", [NL, 512, D]), din("w_branch_dil", [NL, 512, D])]
    wout_d = din("w_out", [NL, D, D])
    ln1g_d = din("ln1_g", [NL, D])
    ln1b_d = din("ln1_b", [NL, D])
    wup_d = din("w_up", [NL, D, DFF])
    wdn_d = din("w_down", [NL, DFF, D])
    wpg_d = din("w_ple_gate", [NL, D, D])
    wple_d = din("w_ple", [NL, PLE, D])
    ln2g_d = din("ln2_g", [NL, D])
    ln2b_d = din("ln2_b", [NL, D])
    biasA_d = din("biasA", [4, 128, WA])
    biasD_d = din("biasD", [12, 128, 256])
    lamc_d = din("lamc", [NL, 128, 2])
    out_d = nc.dram_tensor("out", [S, D], F32, kind="ExternalOutput").ap()
    okind = "ExternalOutput" if dbg else "Internal"
    res1_d = nc.dram_tensor("res1", [S, D], F32, kind=okind).ap()
    ot_d = nc.dram_tensor("ot", [12, 128, S], BF16, kind=okind).ap()
    ea_d = nc.dram_tensor("ea", [4, 128, WA], BF16, kind="Internal").ap()
    c1_d = nc.dram_tensor("c1", [S, D], F32, kind="Internal").ap()

    with ExitStack() as st:
        Sc = Sched(nc, st)

        uid = [0]

        def sbuf(stack, name, shape, dt):
            uid[0] += 1
            return stack.enter_context(nc.sbuf_tensor("%s_u%d" % (name, uid[0]), list(shape), dt))

        banks = [st.enter_context(nc.psum_tensor("bank%d" % i, [128, 512], F32)) for i in range(8)]
        xT = sbuf(st, "xT", [128, 8, S], BF16)
        ident = sbuf(st, "ident", [128, 128], BF16)
        ones_bf = sbuf(st, "ones_bf", [128, 128], BF16)
        nones_bf = sbuf(st, "nones_bf", [128, 128], BF16)
        uneg = sbuf(st, "uneg", [128, 128], BF16)
        onesf = sbuf(st, "onesf", [128, 512], F32)

        class P:
            bank = [Buf("bank%d" % i) for i in range(8)]
            xT = [Buf("xT%d" % g) for g in range(G)]
            const = Buf("const")
            ED = Buf("ED")
            ea = [Buf("ea%d" % h) for h in range(4)]
            ot = [[Buf("ot%d_%d" % (i, g)) for g in range(G)] for i in range(12)]
            res1 = [Buf("res1_%d" % t) for t in range(T)]
            outb = [Buf("out_%d" % t) for t in range(T)]
            c1 = [Buf("c1_%d" % t) for t in range(T)]

        ch_out = Sc.chan("out")
        ch_res1 = Sc.chan("res1")
        ch_ot = Sc.chan("ot")
        ch_ea = Sc.chan("ea")
        ch_misc = [Sc.chan("misc%d" % i) for i in range(12)]
        ch_w = [Sc.chan("w%d" % i) for i in range(12)]

        def end_phase(stack_unused=None):
            Sc.barrier()
            with nc.Block() as block:
                Sc.emit(block)

        def evac(i, out, in_, scale, reads, writes):
            if i % 2 == 0:
                Sc.op("scalar", ACT(out, in_, AF.Copy, scale=scale), reads=reads, writes=writes)
            else:
                Sc.op("vector", TS(out, in_, scale, None, ALU.mult), reads=reads, writes=writes)

        pj = Rot([(banks[6], P.bank[6]), (banks[7], P.bank[7])])
        cnt = {"ev": 0}

        def proj_feat(wt, wbuf, c0, dst_fn, dst_bufs, scale):
            for tg in range(G):
                bk, bb = pj.next()
                for kc in range(8):
                    Sc.op("tensor", MM(bk[:], wt[:, kc, c0:c0 + 128], xT[:, kc, tg * 512:(tg + 1) * 512], kc == 0, kc == 7),
                          reads=[wbuf, P.xT[tg]], writes=[bb], signal=(kc == 7))
                o, i_ = dst_fn(tg, bk)
                cnt["ev"] += 1
                evac(cnt["ev"], o, i_, scale, [bb], dst_bufs)

        def proj_tok(wt, wbuf, c0, ncols, vdst, vbuf, tok_ap_fn):
            per = 512 // ncols
            for t0 in range(0, T, per):
                bk, bb = pj.next()
                n = min(per, T - t0)
                for j in range(n):
                    for kc in range(8):
                        Sc.op("tensor", MM(bk[:, j * ncols:(j + 1) * ncols], tok_ap_fn(kc, t0 + j), wt[:, kc, c0:c0 + ncols], kc == 0, kc == 7),
                              reads=[wbuf] + P.xT, writes=[bb], signal=(kc == 7 and j == n - 1))
                cnt["ev"] += 1
                evac(cnt["ev"], vdst[:, t0:t0 + n, :], bk[:, 0:n * ncols].rearrange("p (t c) -> p t c", c=ncols), 1.0, [bb], [vbuf])

        def load_w_cols(stack_, wt, wbuf, ch, l, cols):
            o = 0
            for (c0, n) in cols:
                src = w_in_d[l, :, c0:c0 + n].rearrange("(kc p) c -> p kc c", p=128)
                Sc.dma("gpsimd", ch, DMA(wt[:, :, o:o + n], src), writes=[wbuf])
                o += n

        def transposes_to_xT(xb_ap, xb_buf, t):
            tp = banks[7][:].bitcast(BF16)
            for c in range(8):
                Sc.op("tensor", TR(tp[:, c * 128:(c + 1) * 128], xb_ap[:, c * 128:(c + 1) * 128], ident[:]),
                      reads=[xb_buf, P.const], writes=[P.bank[7]], signal=(c == 7))
            tg = t // 4
            Sc.op("scalar", ACT(xT[:, :, t * 128:(t + 1) * 128], tp.rearrange("p (c n) -> p c n", c=8), AF.Copy),
                  reads=[P.bank[7]], writes=[P.xT[tg]])

        with ExitStack() as ph:
            Sc.op("vector", lambda e: e.memset(onesf[:], 1.0), writes=[P.const])
            Sc.op("vector", CP(ones_bf[:], onesf[:, 0:128]), reads=[P.const], writes=[P.const])
            Sc.op("vector", TS(nones_bf[:], onesf[:, 0:128], -1.0, None, ALU.mult), reads=[P.const], writes=[P.const])
            Sc.op("gpsimd", ASEL(ident[:], ones_bf[:], [[1, 128]], ALU.is_equal, 0.0, 0, -1), reads=[P.const], writes=[P.const])
            Sc.op("gpsimd", ASEL(uneg[:], nones_bf[:], [[-1, 128]], ALU.is_ge, 0.0, 0, 1), reads=[P.const], writes=[P.const])
            CH = 1024
            rawA = [sbuf(ph, "rawA%d" % i, [128, CH], F32) for i in range(2)]
            rawAb = [Buf("rawA%d" % i) for i in range(2)]
            eab = [sbuf(ph, "eab%d" % i, [128, CH], BF16) for i in range(2)]
            eabb = [Buf("eab%d" % i) for i in range(2)]
            k = 0
            for h in range(4):
                for c0 in range(0, WA, CH):
                    n = min(CH, WA - c0)
                    i = k % 2
                    k += 1
                    Sc.dma("sync", ch_misc[1 + i], DMA(rawA[i][:, 0:n], biasA_d[h, :, c0:c0 + n]), writes=[rawAb[i]])
                    Sc.op("scalar", ACT(rawA[i][:, 0:n], rawA[i][:, 0:n], AF.Exp), reads=[rawAb[i]], writes=[rawAb[i]])
                    Sc.op("gpsimd", ASEL(eab[i][:, 0:n], rawA[i][:, 0:n], [[1, n]], ALU.is_ge, 0.0, c0 - PADA, -1),
                          reads=[rawAb[i]], writes=[eabb[i]])
                    Sc.dma("sync", ch_ea, DMA(ea_d[h, :, c0:c0 + n], eab[i][:, 0:n]), reads=[eabb[i]], writes=[P.ea[h]])
            xin = [sbuf(ph, "xin%d" % i, [128, D], F32) for i in range(2)]
            xinb = [Buf("xin%d" % i) for i in range(2)]
            xbf = [sbuf(ph, "xbf%d" % i, [128, D], BF16) for i in range(2)]
            xbfb = [Buf("xbf%d" % i) for i in range(2)]
            for t in range(T):
                i = t % 2
                Sc.dma("sync", ch_misc[3 + i], DMA(xin[i][:], x_d[t * 128:(t + 1) * 128, :]), writes=[xinb[i]])
                Sc.op("vector", CP(xbf[i][:], xin[i][:]), reads=[xinb[i]], writes=[xbfb[i]])
                transposes_to_xT(xbf[i], xbfb[i], t)
            end_phase()

        for l in range(NL):
            res_in = x_d if l == 0 else out_d
            res_in_bufs = None if l == 0 else P.outb

            with ExitStack() as ph:
                qk = [(sbuf(ph, "qA%d" % i, [128, S], BF16), sbuf(ph, "kA%d" % i, [128, S], BF16),
                       sbuf(ph, "vA%d" % i, [128, T, 128], BF16), Buf("qkvA%d" % i)) for i in range(2)]
                wA = [(sbuf(ph, "wA%d" % i, [128, 8, 384], BF16), Buf("wA%d" % i)) for i in range(2)]
                EAt = [(sbuf(ph, "EA%d" % i, [128, WA], BF16), Buf("EA%d" % i)) for i in range(2)]
                praw = Rot([(sbuf(ph, "praw%d" % i, [128, 512], BF16), Buf()) for i in range(6)])
                pTt = Rot([(sbuf(ph, "pT%d" % i, [128, 512], BF16), Buf()) for i in range(8)])
                lamt = sbuf(ph, "lamt", [128, 256], F32)
                lprod = sbuf(ph, "lprod", [128, 128], F32)
                lsum = sbuf(ph, "lsum", [128, 2], F32)
                lamc = sbuf(ph, "lamc", [128, 2], F32)
                nlam = sbuf(ph, "nlam", [128, 1], F32)
                gA = sbuf(ph, "gA", [128, 1], F32)
                lb = Buf("lam")
                r0 = sbuf(ph, "r0", [128, 512], F32)
                r1 = sbuf(ph, "r1", [128, 512], F32)
                t0 = sbuf(ph, "t0", [128, 512], F32)
                t1 = sbuf(ph, "t1", [128, 512], F32)
                oo = sbuf(ph, "oo", [128, 512], F32)
                sq = sbuf(ph, "sq", [128, 512], BF16)
                lnv = sbuf(ph, "lnv", [128, 512], F32)
                postb = Buf("post")
                ostage = Rot([(sbuf(ph, "ostA%d" % i, [128, 512], BF16), Buf()) for i in range(2)])

                Sc.dma("sync", ch_misc[0], DMA(lamt[:], lam_d[l, :].partition_broadcast(128)), writes=[lb])
                Sc.dma("sync", ch_misc[1], DMA(lamc[:], lamc_d[l, :, :]), writes=[lb])
                Sc.dma("sync", ch_misc[2], DMA(gA[:], dan_d[l, :].rearrange("(p o) -> p o", o=1)), writes=[lb])
                l4 = lamt[:].rearrange("p (a b d) -> p a b d", a=2, b=2)
                Sc.op("vector", TT(lprod[:].rearrange("p (a d) -> p a d", a=2), l4[:, :, 0, :], l4[:, :, 1, :], ALU.mult), reads=[lb], writes=[lb])
                Sc.op("vector", lambda e: e.tensor_reduce(out=lsum[:], in_=lprod[:].rearrange("p (a d) -> p a d", a=2), axis=AX.X, op=ALU.add),
                      reads=[lb], writes=[lb])
                Sc.op("scalar", ACT(lsum[:], lsum[:], AF.Exp), reads=[lb], writes=[lb])
                Sc.op("vector", TT(nlam[:], lsum[:, 1:2], lsum[:, 0:1], ALU.subtract), reads=[lb], writes=[lb])
                Sc.op("vector", TT(nlam[:], nlam[:], lamc[:, 0:1], ALU.subtract), reads=[lb], writes=[lb])
                Sc.op("vector", TT(gA[:], gA[:], lamc[:, 1:2], ALU.mult), reads=[lb], writes=[lb])

                def loadA(h):
                    wt, wbuf = wA[h % 2]
                    load_w_cols(ph, wt, wbuf, ch_w[h % 2], l, [(A_Q + h * 128, 128), (A_K + h * 128, 128), (A_V + h * 128, 128)])
                    et, eb = EAt[h % 2]
                    Sc.dma("sync", ch_w[2 + h % 2], DMA(et[:], ea_d[h, :, :]), reads=[P.ea[h]], writes=[eb])

                loadA(0)
                for h in range(4):
                    if h + 1 < 4:
                        loadA(h + 1)
                    wt, wbuf = wA[h % 2]
                    et, eb = EAt[h % 2]
                    qT, kT, vv, qb = qk[h % 2]
                    proj_feat(wt, wbuf, 0, lambda tg, bk: (qT[:, tg * 512:(tg + 1) * 512], bk[:]), [qb], 0.125)
                    proj_feat(wt, wbuf, 128, lambda tg, bk: (kT[:, tg * 512:(tg + 1) * 512], bk[:]), [qb], 1.0)
                    proj_tok(wt, wbuf, 256, 128, vv, qb, lambda kc, t: xT[:, kc, t * 128:(t + 1) * 128])
                    for g in range(G):
                        nk = 4 * g + 4
                        n = 2 * nk
                        Ub = [(banks[2], P.bank[2]), (banks[3], P.bank[3])]
                        Sb_ = [(banks[4], P.bank[4]), (banks[5], P.bank[5])]
                        sbk = [(banks[0], P.bank[0]), (banks[1], P.bank[1]), (banks[6], P.bank[6]), (banks[7], P.bank[7])]
                        pts = {}
                        prs = {}

                        def stage_S(kt):
                            for m in range(2):
                                bk, bb = sbk[(2 * kt + m) % 4]
                                Sc.op("tensor", MM(bk[:], kT[64 * m:64 * m + 64, kt * 128:(kt + 1) * 128],
                                                   qT[64 * m:64 * m + 64, g * 512:(g + 1) * 512], True, True),
                                      reads=[qb], writes=[bb], signal=(m == 1))

                        def stage_E(kt):
                            for m in range(2):
                                bk, bb = sbk[(2 * kt + m) % 4]
                                pr, prb = praw.next()
                                Sc.op("scalar", ACT(pr[:], bk[:], AF.Exp), reads=[bb], writes=[prb])
                                prs[(kt, m)] = (pr, prb)

                        def stage_M(kt):
                            for m in range(2):
                                pr, prb = prs.pop((kt, m))
                                pt, ptb = pTt.next()
                                off = PADA + g * 512 - kt * 128
                                Sc.op("vector", TT(pt[:], pr[:], et[:, off:off + 512], ALU.mult), reads=[prb, eb], writes=[ptb])
                                pts[(kt, m)] = (pt, ptb)

                        def stage_V(kt):
                            for m in range(2):
                                pt, ptb = pts.pop((kt, m))
                                ub, ubb = Ub[m]
                                sb_, sbb = Sb_[m]
                                Sc.op("tensor", MM(ub[:], vv[:, kt, :], pt[:], kt == 0, kt == nk - 1),
                                      reads=[qb, ptb], writes=[ubb], signal=False)
                                Sc.op("tensor", MM(sb_[:], ones_bf[:], pt[:], kt == 0, kt == nk - 1),
                                      reads=[P.const, ptb], writes=[sbb], signal=(m == 1))

                        for it in range(nk + 2):
                            rd = [qb, P.const]
                            wr = []
                            if it < nk:
                                wr += [sbk[(2 * it + m) % 4][1] for m in range(2)]
                            if it >= 2:
                                rd += [pts[(it - 2, m)][1] for m in range(2)]
                                wr += [Ub[0][1], Ub[1][1], Sb_[0][1], Sb_[1][1]]
                            Sc.prewait("tensor", rd, wr)
                            if it < nk:
                                stage_S(it)
                            if it >= 2:
                                stage_V(it - 2)
                            if it < nk:
                                stage_E(it)
                            if 1 <= it <= nk:
                                stage_M(it - 1)
                        Sc.op("scalar", ACT(r0[:], banks[4][:], AF.Ln), reads=[P.bank[4]], writes=[postb])
                        Sc.op("scalar", ACT(r1[:], banks[5][:], AF.Ln), reads=[P.bank[5]], writes=[postb])
                        Sc.op("scalar", ACT(r0[:], r0[:], AF.Exp, scale=-1.0), reads=[postb], writes=[postb])
                        Sc.op("scalar", ACT(r1[:], r1[:], AF.Exp, scale=-1.0), reads=[postb], writes=[postb])
                        Sc.op("vector", TT(t0[:], banks[2][:], r0[:], ALU.mult), reads=[P.bank[2], postb], writes=[postb])
                        Sc.op("vector", TT(t1[:], banks[3][:], r1[:], ALU.mult), reads=[P.bank[3], postb], writes=[postb])
                        Sc.op("vector", STT(oo[:], t1[:], nlam[:, 0:1], t0[:], ALU.mult, ALU.add), reads=[postb, lb], writes=[postb])
                        Sc.op("scalar", ACT(sq[:], oo[:], AF.Square), reads=[postb], writes=[postb])
                        Sc.op("tensor", MM(banks[0][:], ones_bf[:], sq[:], True, True), reads=[postb, P.const], writes=[P.bank[0]])
                        Sc.op("scalar", ACT(lnv[:], banks[0][:], AF.Ln, scale=1.0 / 128.0, bias=RMS_EPS), reads=[P.bank[0]], writes=[postb])
                        Sc.op("scalar", ACT(lnv[:], lnv[:], AF.Exp, scale=-0.5), reads=[postb], writes=[postb])
                        os_, osb = ostage.next()
                        Sc.op("vector", STT(os_[:], oo[:], gA[:, 0:1], lnv[:], ALU.mult, ALU.mult), reads=[postb, lb], writes=[osb])
                        Sc.dma("sync", ch_ot, DMA(ot_d[h, :, g * 512:(g + 1) * 512], os_[:]), reads=[osb], writes=[P.ot[h][g]])
                end_phase()

            with ExitStack() as ph:
                qk = [(sbuf(ph, "qB%d" % i, [128, S], BF16), sbuf(ph, "kB%d" % i, [128, S], BF16),
                       sbuf(ph, "vB%d" % i, [128, T, 128], BF16), Buf("qkvB%d" % i)) for i in range(2)]
                wB = [(sbuf(ph, "wB%d" % i, [128, 8, 384], BF16), Buf("wB%d" % i)) for i in range(2)]
                e32 = Rot([(sbuf(ph, "e32_%d" % i, [128, 512], F32), Buf()) for i in range(3)])
                spt = Rot([(sbuf(ph, "sp%d" % i, [128, 512], BF16), Buf()) for i in range(5)])
                tmpt = Rot([(sbuf(ph, "tmpB%d" % i, [128, 512], F32), Buf()) for i in range(3)])
                at = Rot([(sbuf(ph, "aB%d" % i, [128, 512], BF16), Buf()) for i in range(6)])
                csb = [(sbuf(ph, "csb%d" % i, [128, 512], F32), Buf()) for i in range(2)]
                ostage = Rot([(sbuf(ph, "ostB%d" % i, [128, 512], BF16), Buf()) for i in range(2)])
                maskB = sbuf(ph, "maskB", [128, 4, 512], BF16)
                mkb = Buf("maskB")
                for dd in range(4):
                    Sc.op("gpsimd", ASEL(maskB[:, dd, :], onesf[:], [[1, 512]], ALU.is_gt, 0.0, -128 * dd, -1),
                          reads=[P.const], writes=[mkb])

                def loadB(hp):
                    wt, wbuf = wB[hp % 2]
                    load_w_cols(ph, wt, wbuf, ch_w[hp % 2], l, [(B_Q + hp * 128, 128), (B_K + hp * 128, 128), (B_V + hp * 128, 128)])

                loadB(0)
                for hp in range(4):
                    if hp + 1 < 4:
                        loadB(hp + 1)
                    wt, wbuf = wB[hp % 2]
                    qT, kT, vv, qb = qk[hp % 2]
                    proj_feat(wt, wbuf, 0, lambda tg, bk: (qT[:, tg * 512:(tg + 1) * 512], bk[:]), [qb], 0.125)
                    proj_feat(wt, wbuf, 128, lambda tg, bk: (kT[:, tg * 512:(tg + 1) * 512], bk[:]), [qb], 1.0)
                    proj_tok(wt, wbuf, 256, 128, vv, qb, lambda kc, t: xT[:, kc, t * 128:(t + 1) * 128])
                    for g in range(G):
                        n = 4 * g + 4
                        for hh in range(2):
                            rs = slice(64 * hh, 64 * hh + 64)
                            zb = [(banks[0], P.bank[0]), (banks[1], P.bank[1])]
                            za = [(banks[2], P.bank[2]), (banks[3], P.bank[3])]
                            cbk = [(banks[4], P.bank[4]), (banks[7], P.bank[7])]
                            otb = [(banks[5], P.bank[5]), (banks[6], P.bank[6])]
                            NFILL = 0
                            st_e = {}
                            st_sp = {}
                            st_a = {}

                            def kt_of(i):
                                return 4 * g + 3 - i

                            def stZ1_pe(i):
                                kt = kt_of(i)
                                bk, bb = zb[i % 2]
                                Sc.op("tensor", MM(bk[:], kT[rs, kt * 128:(kt + 1) * 128], qT[rs, g * 512:(g + 1) * 512], True, True),
                                      reads=[qb], writes=[bb])

                            def stZ1_act(i):
                                bk, bb = zb[i % 2]
                                e_, eb_ = e32.next()
                                Sc.op("scalar", ACT(e_[:], bk[:], AF.Exp), reads=[bb], writes=[eb_])
                                st_e[i] = (e_, eb_)

                            def stZ2(i):
                                e_, eb_ = st_e.pop(i)
                                sp_, spb = spt.next()
                                Sc.op("scalar", ACT(sp_[:], e_[:], AF.Ln, bias=1.0), reads=[eb_], writes=[spb])
                                if i <= 3:
                                    Sc.op("gpsimd", TT(sp_[:], sp_[:], maskB[:, 3 - i, :], ALU.mult), reads=[spb, mkb], writes=[spb])
                                st_sp[i] = (sp_, spb)

                            def stA_pe(i):
                                sp_, spb = st_sp[i]
                                kt = kt_of(i)
                                bk, bb = za[i % 2]
                                Sc.op("tensor", MM(bk[:], uneg[:], sp_[:], True, False), reads=[spb, P.const], writes=[bb], signal=False)
                                Sc.op("tensor", MM(bk[:], kT[rs, kt * 128:(kt + 1) * 128], qT[rs, g * 512:(g + 1) * 512], False, True),
                                      reads=[qb], writes=[bb])
                                if i < n - 1:
                                    cb_, cbb_ = cbk[i % 2]
                                    Sc.op("tensor", MM(cb_[:], nones_bf[:], sp_[:], True, True), reads=[spb, P.const], writes=[cbb_])

                            def stA_dve(i):
                                st_sp.pop(i)
                                if i < n - 1:
                                    cb_, cbb_ = cbk[i % 2]
                                    cs, csbuf = csb[i % 2]
                                    if i == 0:
                                        Sc.op("vector", CP(cs[:], cb_[:]), reads=[cbb_], writes=[csbuf])
                                    else:
                                        pc, pcb = csb[(i - 1) % 2]
                                        Sc.op("vector", TT(cs[:], cb_[:], pc[:], ALU.add), reads=[cbb_, pcb], writes=[csbuf])

                            def stD(i):
                                bk, bb = za[i % 2]
                                a_, ab_ = at.next()
                                if i == 0:
                                    Sc.op("scalar", ACT(a_[:], bk[:], AF.Exp), reads=[bb], writes=[ab_])
                                else:
                                    pc, pcb = csb[(i - 1) % 2]
                                    tm, tmb = tmpt.next()
                                    Sc.op("vector", TT(tm[:], bk[:], pc[:], ALU.add), reads=[bb, pcb], writes=[tmb])
                                    Sc.op("scalar", ACT(a_[:], tm[:], AF.Exp), reads=[tmb], writes=[ab_])
                                if i <= 3:
                                    Sc.op("gpsimd", TT(a_[:], a_[:], maskB[:, 3 - i, :], ALU.mult), reads=[ab_, mkb], writes=[ab_])
                                st_a[i] = (a_, ab_)

                            def stV(i):
                                kt = kt_of(i)
                                a_, ab_ = st_a.pop(i)
                                Sc.op("tensor", MM(otb[hh][0][:], vv[:, kt, :], a_[:], i == 0, i == n - 1),
                                      reads=[qb, ab_], writes=[otb[hh][1]], signal=(i == n - 1))

                            for it in range(n + 4):
                                rd = [qb, P.const]
                                wr = []
                                if it < n:
                                    wr.append(zb[it % 2][1])
                                if 0 <= it - 2 < n:
                                    rd.append(st_sp[it - 2][1])
                                    wr.append(za[(it - 2) % 2][1])
                                    if it - 2 < n - 1:
                                        wr.append(cbk[(it - 2) % 2][1])
                                if 0 <= it - 4 < n:
                                    rd.append(st_a[it - 4][1])
                                    wr.append(otb[hh][1])
                                Sc.prewait("tensor", rd, wr)
                                if it < n:
                                    stZ1_pe(it)
                                if 0 <= it - 2 < n:
                                    stA_pe(it - 2)
                                if 0 <= it - 4 < n:
                                    stV(it - 4)
                                if it < n:
                                    for _f in range(NFILL):
                                        Sc.op("tensor", MM(banks[6][:], ones_bf[:], qT[:, g * 512:(g + 1) * 512], True, True),
                                              reads=[qb, P.const], writes=[P.bank[6]], signal=False)
                                if it < n:
                                    stZ1_act(it)
                                if 0 <= it - 2 < n:
                                    stD(it - 2)
                                if it < n:
                                    stZ2(it)
                                if 0 <= it - 2 < n:
                                    stA_dve(it - 2)
                        os_, osb = ostage.next()
                        Sc.op("vector", CP(os_[0:64, :], banks[5][0:64, :]), reads=[P.bank[5]], writes=[osb])
                        Sc.op("scalar", ACT(os_[64:128, :], banks[6][64:128, :], AF.Copy), reads=[P.bank[6]], writes=[osb])
                        Sc.dma("sync", ch_ot, DMA(ot_d[4 + hp, :, g * 512:(g + 1) * 512], os_[:]), reads=[osb], writes=[P.ot[4 + hp][g]])
                end_phase()

            with ExitStack() as ph:
                qk = [(sbuf(ph, "qC%d" % i, [128, S], BF16), sbuf(ph, "kC%d" % i, [128, S], BF16),
                       sbuf(ph, "vC%d" % i, [128, T, 128], BF16), Buf("qkvC%d" % i)) for i in range(2)]
                wC = [(sbuf(ph, "wC%d" % i, [128, 8, 384], BF16), Buf("wC%d" % i)) for i in range(2)]
                praw = Rot([(sbuf(ph, "prawC%d" % i, [128, 256], F32), Buf()) for i in range(3)])
                pTt = Rot([(sbuf(ph, "pTC%d" % i, [128, 256], BF16), Buf()) for i in range(3)])
                Uacc = sbuf(ph, "Uacc", [128, S], F32)
                Sacc = sbuf(ph, "Sacc", [128, S], F32)
                accb = Buf("acc")
                rc = sbuf(ph, "rcC", [128, 512], F32)
                rcb = Buf("rc")
                ostage = Rot([(sbuf(ph, "ostC%d" % i, [128, 512], BF16), Buf()) for i in range(2)])
                scale_c = 128.0 ** -0.5
                ED = sbuf(ph, "ED", [128, 12, 256], BF16)
                rawD = sbuf(ph, "rawD", [128, 12, 256], F32)
                rawDb = Buf("rawD")
                Sc.dma("sync", ch_misc[0], DMA(rawD[:], biasD_d.rearrange("h p f -> p h f")), writes=[rawDb])
                Sc.op("scalar", ACT(rawD[:], rawD[:], AF.Exp), reads=[rawDb], writes=[rawDb])
                for hh in range(12):
                    Sc.op("gpsimd", ASEL(rawD[:, hh, :], rawD[:, hh, :], [[1, 256]], ALU.is_ge, 0.0, 0, -1), reads=[rawDb], writes=[rawDb])
                    Sc.op("gpsimd", ASEL(ED[:, hh, :], rawD[:, hh, :], [[-1, 256]], ALU.is_ge, 0.0, 128, 1), reads=[rawDb], writes=[P.ED])

                def loadC(idx):
                    gi, hs = idx % 3, idx // 3
                    hd = gi * 4 + hs
                    wt, wbuf = wC[idx % 2]
                    load_w_cols(ph, wt, wbuf, ch_w[idx % 2], l, [(C_Q + hd * 128, 128), (C_K + hd * 128, 128), (C_V + hd * 128, 128)])

                loadC(0)
                for idx in range(12):
                    if idx + 1 < 12:
                        loadC(idx + 1)
                    gi, hs = idx % 3, idx // 3
                    hd = gi * 4 + hs
                    dl = DIL[gi][1]
                    L = S // dl
                    nblk = L // 128
                    wt, wbuf = wC[idx % 2]
                    qT, kT, vv, qb = qk[idx % 2]

                    def dstq(tg, bk, dst=None):
                        if dl == 1:
                            return dst[:, tg * 512:(tg + 1) * 512], bk[:]
                        m0 = tg * (512 // dl)
                        return (dst[:].rearrange("p (b m) -> p b m", b=dl)[:, :, m0:m0 + 512 // dl],
                                bk[:].rearrange("p (a b) -> p b a", b=dl))

                    proj_feat(wt, wbuf, 0, lambda tg, bk: dstq(tg, bk, qT), [qb], scale_c)
                    proj_feat(wt, wbuf, 128, lambda tg, bk: dstq(tg, bk, kT), [qb], 1.0)

                    def tokap(kc, pi):
                        r, j = pi // nblk, pi % nblk
                        s0 = r + dl * 128 * j
                        return xT[:, kc, s0:s0 + dl * 127 + 1:dl]

                    proj_tok(wt, wbuf, 256, 128, vv, qb, tokap)
                    ub = [(banks[2], P.bank[2]), (banks[3], P.bank[3])]
                    sb_ = [(banks[4], P.bank[4]), (banks[5], P.bank[5])]
                    sbk = [(banks[0], P.bank[0]), (banks[1], P.bank[1])]
                    pts = {}

                    def cS(pi):
                        r, j = pi // nblk, pi % nblk
                        ncol = 256 if j < nblk - 1 else 128
                        bk, bb = sbk[pi % 2]
                        Sc.op("tensor", MM(bk[:, 0:ncol], kT[:, pi * 128:(pi + 1) * 128], qT[:, pi * 128:pi * 128 + ncol], True, True),
                              reads=[qb], writes=[bb])
                        pr, prb = praw.next()
                        Sc.op("scalar", ACT(pr[:, 0:ncol], bk[:, 0:ncol], AF.Exp), reads=[bb], writes=[prb])
                        pt, ptb = pTt.next()
                        eng = "vector" if pi % 2 == 0 else "gpsimd"
                        Sc.op(eng, TT(pt[:, 0:ncol], pr[:, 0:ncol], ED[:, hd, 0:ncol], ALU.mult), reads=[prb, P.ED], writes=[ptb])
                        pts[pi] = (pt, ptb)

                    def cV(pi):
                        r, j = pi // nblk, pi % nblk
                        pt, ptb = pts.pop(pi)
                        u, ubb = ub[(pi // 4) % 2]
                        s_, sbb = sb_[(pi // 4) % 2]
                        c = pi % 4
                        Sc.op("tensor", MM(u[:, c * 128:(c + 1) * 128], vv[:, pi, :], pt[:, 0:128], j == 0, True),
                              reads=[qb, ptb], writes=[ubb], signal=False)
                        Sc.op("tensor", MM(s_[:, c * 128:(c + 1) * 128], ones_bf[:], pt[:, 0:128], j == 0, True),
                              reads=[P.const, ptb], writes=[sbb], signal=True)
                        if c == 3 or pi == T - 1:
                            flush(pi // 4)
                        if j < nblk - 1:
                            u2, ubb2 = ub[((pi + 1) // 4) % 2]
                            s2, sbb2 = sb_[((pi + 1) // 4) % 2]
                            c2 = (pi + 1) % 4
                            Sc.op("tensor", MM(u2[:, c2 * 128:(c2 + 1) * 128], vv[:, pi, :], pt[:, 128:256], True, False),
                                  reads=[qb, ptb], writes=[ubb2], signal=False)
                            Sc.op("tensor", MM(s2[:, c2 * 128:(c2 + 1) * 128], ones_bf[:], pt[:, 128:256], True, False),
                                  reads=[P.const, ptb], writes=[sbb2], signal=False)

                    def flush(bi):
                        u, ubb = ub[bi % 2]
                        s_, sbb = sb_[bi % 2]
                        for c in range(4):
                            pi = bi * 4 + c
                            if pi >= T:
                                break
                            r, j = pi // nblk, pi % nblk
                            s0 = r + dl * 128 * j
                            dU = Uacc[:, s0:s0 + dl * 127 + 1:dl]
                            dS = Sacc[:, s0:s0 + dl * 127 + 1:dl]
                            if gi == 0:
                                Sc.op("vector", CP(dU, u[:, c * 128:(c + 1) * 128]), reads=[ubb], writes=[accb])
                                Sc.op("vector", CP(dS, s_[:, c * 128:(c + 1) * 128]), reads=[sbb], writes=[accb])
                            else:
                                Sc.op("vector", TT(dU, u[:, c * 128:(c + 1) * 128], dU, ALU.add), reads=[ubb, accb], writes=[accb])
                                Sc.op("vector", TT(dS, s_[:, c * 128:(c + 1) * 128], dS, ALU.add), reads=[sbb, accb], writes=[accb])

                    for step in range(T + 1):
                        if step < T:
                            cS(step)
                        if step >= 1:
                            cV(step - 1)
                    if gi == 2:
                        for g in range(G):
                            Sc.op("vector", lambda e, g=g: e.reciprocal(out=rc[:], in_=Sacc[:, g * 512:(g + 1) * 512]), reads=[accb], writes=[rcb])
                            os_, osb = ostage.next()
                            Sc.op("vector", TT(os_[:], Uacc[:, g * 512:(g + 1) * 512], rc[:], ALU.mult), reads=[accb, rcb], writes=[osb])
                            Sc.dma("sync", ch_ot, DMA(ot_d[8 + hs, :, g * 512:(g + 1) * 512], os_[:]), reads=[osb], writes=[P.ot[8 + hs][g]])
                end_phase()

            def ln_tail(ph_t, hh, hb, gbc, bbc, lnb, t, dst_d, dst_buf, ch_dst, stats, mv, xn, xnb_, xnf_b, xnb_b):
                Sc.op("vector", lambda e: e.bn_stats(out=stats[:, 0:6], in_=hh[:, 0:512]), reads=[hb], writes=[lnb])
                Sc.op("vector", lambda e: e.bn_stats(out=stats[:, 6:12], in_=hh[:, 512:1024]), reads=[hb], writes=[lnb])
                Sc.op("vector", lambda e: e.bn_aggr(out=mv[:], in_=stats[:]), reads=[lnb], writes=[lnb])
                Sc.op("scalar", ACT(mv[:, 1:2], mv[:, 1:2], AF.Sqrt, bias=LN_EPS), reads=[lnb], writes=[lnb])
                Sc.op("vector", lambda e: e.reciprocal(out=mv[:, 1:2], in_=mv[:, 1:2]), reads=[lnb], writes=[lnb])
                Sc.op("vector", TS(xn[:], hh[:], mv[:, 0:1], mv[:, 1:2], ALU.subtract, ALU.mult), reads=[hb, lnb], writes=[xnf_b])
                Sc.op("gpsimd", TT(xn[:], xn[:], gbc[:], ALU.mult), reads=[xnf_b, P.const], writes=[xnf_b])
                Sc.op("gpsimd", TT(xn[:], xn[:], bbc[:], ALU.add), reads=[xnf_b, P.const], writes=[xnf_b])
                Sc.dma("sync", ch_dst, DMA(dst_d[t * 128:(t + 1) * 128, :], xn[:]), reads=[xnf_b], writes=[dst_buf])
                Sc.op("gpsimd", CP(xnb_[:], xn[:]), reads=[xnf_b], writes=[xnb_b])
                return lambda: transposes_to_xT(xnb_, xnb_b, t)

            with ExitStack() as ph:
                wg = [(sbuf(ph, "wg%d" % i, [128, 8, 512], BF16), Buf("wg%d" % i)) for i in range(2)]
                wbr = [(sbuf(ph, "wbr%d" % i, [128, 4, 1024], BF16), Buf("wbr%d" % i)) for i in range(3)]
                wo = sbuf(ph, "wo", [128, 8, 1024], BF16)
                wob = Buf("wo")
                gbc = sbuf(ph, "g1bc", [128, D], F32)
                bbc = sbuf(ph, "b1bc", [128, D], F32)
                otile = [(sbuf(ph, "otile%d" % i, [128, 12, 512], BF16), Buf("otile%d" % i)) for i in range(1)]
                merged = sbuf(ph, "merged", [128, 8, 512], BF16)
                mergb = Buf("merged")
                sg = Rot([(sbuf(ph, "sgM%d" % i, [128, 512], F32), Buf()) for i in range(2)])
                macc = sbuf(ph, "macc", [128, 4, 512], F32)
                maccb = Buf("macc")
                xres = [(sbuf(ph, "xres%d" % i, [128, D], F32), Buf()) for i in range(2)]
                hh = [(sbuf(ph, "hM%d" % i, [128, D], F32), Buf()) for i in range(2)]
                xn = [(sbuf(ph, "xnM%d" % i, [128, D], F32), Buf()) for i in range(2)]
                xnb = [(sbuf(ph, "xnbM%d" % i, [128, D], BF16), Buf()) for i in range(2)]
                stats = sbuf(ph, "statsM", [128, 12], F32)
                mv = sbuf(ph, "mvM", [128, 2], F32)
                lnb = Buf("lnM")

                for i in range(3):
                    Sc.dma("gpsimd", ch_w[2 + i], DMA(wbr[i][0][:], wb_d[i][l, :, :].rearrange("(c p) n -> p c n", p=128)), writes=[wbr[i][1]])
                Sc.dma("gpsimd", ch_w[5], DMA(wo[:], wout_d[l, :, :].rearrange("(c p) n -> p c n", p=128)), writes=[wob])
                Sc.dma("sync", ch_misc[0], DMA(gbc[:], ln1g_d[l, :].partition_broadcast(128)), writes=[P.const])
                Sc.dma("sync", ch_misc[1], DMA(bbc[:], ln1b_d[l, :].partition_broadcast(128)), writes=[P.const])
                gq = Rot([(banks[0], P.bank[0]), (banks[1], P.bank[1])])
                bq = Rot([(banks[2], P.bank[2]), (banks[3], P.bank[3])])
                kk = 0
                pend = [None]
                for tg in range(G):
                    ot_, otb = otile[0]
                    Sc.dma("sync", ch_misc[2 + tg % 2], DMA(ot_[:], ot_d[:, :, tg * 512:(tg + 1) * 512].rearrange("c p n -> p c n")),
                           reads=[P.ot[i][tg] for i in range(12)], writes=[otb])
                    for half2 in range(2):
                        for i in range(3):
                            wgt, wgb = wg[kk % 2]
                            c0 = GATE + i * 1024 + half2 * 512
                            Sc.dma("gpsimd", ch_w[kk % 2], DMA(wgt[:], w_in_d[l, :, c0:c0 + 512].rearrange("(c p) n -> p c n", p=128)),
                                   writes=[wgb])
                            kk += 1
                            for oc4 in range(4):
                                oc = half2 * 4 + oc4
                                gb_, gbb = gq.next()
                                for kc in range(8):
                                    Sc.op("tensor", MM(gb_[:], wgt[:, kc, oc4 * 128:(oc4 + 1) * 128], xT[:, kc, tg * 512:(tg + 1) * 512], kc == 0, kc == 7),
                                          reads=[wgb, P.xT[tg]], writes=[gbb], signal=(kc == 7))
                                bb_, bbb = bq.next()
                                for c4 in range(4):
                                    Sc.op("tensor", MM(bb_[:], wbr[i][0][:, c4, oc * 128:(oc + 1) * 128], ot_[:, 4 * i + c4, :], c4 == 0, c4 == 3),
                                          reads=[wbr[i][1], otb], writes=[bbb], signal=(c4 == 3))
                                s_, sb2 = sg.next()
                                Sc.op("scalar", ACT(s_[:], gb_[:], AF.Sigmoid), reads=[gbb], writes=[sb2])
                                if i == 0:
                                    Sc.op("vector", TT(macc[:, oc4, :], s_[:], bb_[:], ALU.mult), reads=[sb2, bbb], writes=[maccb])
                                elif i == 1:
                                    Sc.op("vector", TT(s_[:], s_[:], bb_[:], ALU.mult), reads=[sb2, bbb], writes=[sb2])
                                    Sc.op("gpsimd", TT(macc[:, oc4, :], macc[:, oc4, :], s_[:], ALU.add), reads=[sb2, maccb], writes=[maccb])
                                else:
                                    Sc.op("vector", TT(s_[:], s_[:], bb_[:], ALU.mult), reads=[sb2, bbb], writes=[sb2])
                                    Sc.op("gpsimd", TT(merged[:, oc, :], macc[:, oc4, :], s_[:], ALU.add), reads=[sb2, maccb], writes=[mergb])
                    for tt_ in range(4):
                        t = tg * 4 + tt_
                        xr, xrb = xres[t % 2]
                        rd = [] if res_in_bufs is None else [res_in_bufs[t]]
                        Sc.dma("sync", ch_misc[4 + t % 2], DMA(xr[:], res_in[t * 128:(t + 1) * 128, :]), reads=rd, writes=[xrb])
                        h_, hb = hh[t % 2]
                        for half in range(2):
                            yb, ybb = (banks[4], P.bank[4]) if half == 0 else (banks[5], P.bank[5])
                            for oc in range(8):
                                Sc.op("tensor", MM(yb[:], merged[:, oc, tt_ * 128:(tt_ + 1) * 128], wo[:, oc, half * 512:(half + 1) * 512], oc == 0, oc == 7),
                                      reads=[mergb, wob], writes=[ybb], signal=(oc == 7))
                            Sc.op("vector", STT(h_[:, half * 512:(half + 1) * 512], xr[:, half * 512:(half + 1) * 512], ALPHA, yb[:], ALU.mult, ALU.add),
                                  reads=[xrb, ybb], writes=[hb])
                        if pend[0] is not None:
                            pend[0]()
                        pend[0] = ln_tail(ph, h_, hb, gbc, bbc, lnb, t, res1_d, P.res1[t], ch_res1, stats, mv, xn[t % 2][0], xnb[t % 2][0], xn[t % 2][1], xnb[t % 2][1])
                if pend[0] is not None:
                    pend[0]()
                end_phase()

            for hp_ in range(2):
                with ExitStack() as ph:
                    wupr = sbuf(ph, "wupr", [128, 8, 2048], BF16)
                    wupb = [Buf() for _ in range(4)]
                    wdnr = sbuf(ph, "wdnr", [128, 16, 1024], BF16)
                    wdnb = [Buf() for _ in range(4)]
                    hidT = sbuf(ph, "hidT", [128, 16, 512], BF16)
                    hidb = Buf("hidT")
                    r32 = Rot([(sbuf(ph, "r32_%d" % i, [128, 512], F32), Buf()) for i in range(2)])
                    for q4 in range(4):
                        c0 = hp_ * 2048 + q4 * 512
                        Sc.dma("gpsimd", ch_w[q4], DMA(wupr[:, :, q4 * 512:(q4 + 1) * 512], wup_d[l, :, c0:c0 + 512].rearrange("(c p) n -> p c n", p=128)),
                               writes=[wupb[q4]])
                    for q4 in range(4):
                        r0_ = hp_ * 2048 + q4 * 512
                        Sc.dma("gpsimd", ch_w[4 + q4], DMA(wdnr[:, q4 * 4:(q4 + 1) * 4, :], wdn_d[l, r0_:r0_ + 512, :].rearrange("(c p) n -> p c n", p=128)),
                               writes=[wdnb[q4]])
                    hq = Rot([(banks[0], P.bank[0]), (banks[1], P.bank[1])])
                    pendF = [None]
                    if hp_ == 0:
                        cpart = [(sbuf(ph, "cpart%d" % i, [128, D], F32), Buf()) for i in range(2)]
                    else:
                        wpg = sbuf(ph, "wpg", [128, 8, 1024], BF16)
                        wpgb = Buf("wpg")
                        wpl = sbuf(ph, "wpl", [128, 2, 1024], BF16)
                        wplb = Buf("wpl")
                        ptile = [(sbuf(ph, "ptile%d" % i, [128, 2, 512], BF16), Buf()) for i in range(1)]
                        gbc = sbuf(ph, "g2bc", [128, D], F32)
                        bbc = sbuf(ph, "b2bc", [128, D], F32)
                        x1 = [(sbuf(ph, "x1_%d" % i, [128, D], F32), Buf()) for i in range(2)]
                        c1 = [(sbuf(ph, "c1_%d" % i, [128, D], F32), Buf()) for i in range(1)]
                        sgt = Rot([(sbuf(ph, "sgF%d" % i, [128, 512], F32), Buf()) for i in range(2)])
                        xn = [(sbuf(ph, "xnF%d" % i, [128, D], F32), Buf()) for i in range(1)]
                        xnb = [(sbuf(ph, "xnbF%d" % i, [128, D], BF16), Buf()) for i in range(2)]
                        stats = sbuf(ph, "statsF", [128, 12], F32)
                        mv = sbuf(ph, "mvF", [128, 2], F32)
                        lnb = Buf("lnF")
                        Sc.dma("gpsimd", ch_w[8], DMA(wpg[:], wpg_d[l, :, :].rearrange("(c p) n -> p c n", p=128)), writes=[wpgb])
                        Sc.dma("gpsimd", ch_w[9], DMA(wpl[:], wple_d[l, :, :].rearrange("(c p) n -> p c n", p=128)), writes=[wplb])
                        Sc.dma("sync", ch_misc[0], DMA(gbc[:], ln2g_d[l, :].partition_broadcast(128)), writes=[P.const])
                        Sc.dma("sync", ch_misc[1], DMA(bbc[:], ln2b_d[l, :].partition_broadcast(128)), writes=[P.const])
                    for tg in range(G):
                        if hp_ == 1:
                            pt_, ptb = ptile[0]
                            Sc.dma("gpsimd", ch_w[10], DMA(pt_[:], pT_d[l, :, tg * 512:(tg + 1) * 512].rearrange("(c p) n -> p c n", p=128)), writes=[ptb])
                        for hc in range(16):
                            hb_, hbb = hq.next()
                            for kc in range(8):
                                Sc.op("tensor", MM(hb_[:], wupr[:, kc, hc * 128:(hc + 1) * 128], xT[:, kc, tg * 512:(tg + 1) * 512], kc == 0, kc == 7),
                                      reads=[wupb[hc // 4], P.xT[tg]], writes=[hbb], signal=(kc == 7))
                            r_, rb = r32.next()
                            Sc.op("scalar", ACT(r_[:], hb_[:], AF.Relu), reads=[hbb], writes=[rb])
                            Sc.op("vector", STT(hidT[:, hc, :], hb_[:], 0.0, r_[:], ALU.max, ALU.mult), reads=[hbb, rb], writes=[hidb])
                        for tt_ in range(4):
                            t = tg * 4 + tt_
                            cb2 = [(banks[2], P.bank[2]), (banks[3], P.bank[3])] if tt_ % 2 == 0 else [(banks[4], P.bank[4]), (banks[5], P.bank[5])]
                            for half in range(2):
                                cs = slice(half * 512, (half + 1) * 512)
                                cb, cbb = cb2[half]
                                for hc in range(16):
                                    Sc.op("tensor", MM(cb[:], hidT[:, hc, tt_ * 128:(tt_ + 1) * 128], wdnr[:, hc, cs], hc == 0, hc == 15),
                                          reads=[hidb, wdnb[hc // 4]], writes=[cbb], signal=(hc == 15))
                            if hp_ == 0:
                                cp_, cpb = cpart[t % 2]
                                Sc.op("scalar", ACT(cp_[:, 0:512], cb2[0][0][:], AF.Copy), reads=[cb2[0][1]], writes=[cpb])
                                Sc.op("vector", CP(cp_[:, 512:1024], cb2[1][0][:]), reads=[cb2[1][1]], writes=[cpb])
                                Sc.dma("sync", ch_res1, DMA(c1_d[t * 128:(t + 1) * 128, :], cp_[:]), reads=[cpb], writes=[P.c1[t]])
                            else:
                                h_, hb = x1[t % 2]
                                c1t, c1b = c1[0]
                                Sc.dma("sync", ch_misc[2 + t % 2], DMA(h_[:], res1_d[t * 128:(t + 1) * 128, :]), reads=[P.res1[t]], writes=[hb])
                                Sc.dma("sync", ch_misc[4], DMA(c1t[:], c1_d[t * 128:(t + 1) * 128, :]), reads=[P.c1[t]], writes=[c1b])
                                for half in range(2):
                                    cs = slice(half * 512, (half + 1) * 512)
                                    cb, cbb = cb2[half]
                                    gbk, gbkb = (banks[6], P.bank[6])
                                    for kc in range(8):
                                        Sc.op("tensor", MM(gbk[:], xT[:, kc, t * 128:(t + 1) * 128], wpg[:, kc, cs], kc == 0, kc == 7),
                                              reads=[wpgb, P.xT[tg]], writes=[gbkb], signal=(kc == 7))
                                    pbk, pbkb = (banks[0], P.bank[0]) if half == 0 else (banks[1], P.bank[1])
                                    for c2 in range(2):
                                        Sc.op("tensor", MM(pbk[:], pt_[:, c2, tt_ * 128:(tt_ + 1) * 128], wpl[:, c2, cs], c2 == 0, c2 == 1),
                                              reads=[wplb, ptb], writes=[pbkb], signal=(c2 == 1))
                                    s_, sb2 = sgt.next()
                                    Sc.op("scalar", ACT(s_[:], gbk[:], AF.Sigmoid), reads=[gbkb], writes=[sb2])
                                    Sc.op("vector", TT(s_[:], s_[:], pbk[:], ALU.mult), reads=[sb2, pbkb], writes=[sb2])
                                    Sc.op("vector", STT(h_[:, cs], h_[:, cs], ALPHA, s_[:], ALU.mult, ALU.add), reads=[hb, sb2], writes=[hb])
                                    Sc.op("vector", TT(h_[:, cs], h_[:, cs], cb[:], ALU.add), reads=[hb, cbb], writes=[hb])
                                Sc.op("gpsimd", TT(h_[:], h_[:], c1t[:], ALU.add), reads=[hb, c1b], writes=[hb])
                                if pendF[0] is not None:
                                    pendF[0]()
                                pendF[0] = ln_tail(ph, h_, hb, gbc, bbc, lnb, t, out_d, P.outb[t], ch_out, stats, mv, xn[0][0], xnb[t % 2][0], xn[0][1], xnb[t % 2][1])
                    if pendF[0] is not None:
                        pendF[0]()
                    end_phase()

        Sc.wait_all("sync", [(ch_out, ch_out.val)])
        if dbg:
            Sc.wait_all("sync", [(ch_res1, ch_res1.val), (ch_ot, ch_ot.val)])
        with nc.Block() as block:
            Sc.emit(block)
    return nc


def _bucket(d):
    d = np.maximum(d, 0).astype(np.int64)
    dm = np.maximum(d, 1).astype(np.float32)
    lr = np.log(dm / np.float32(16.0)) / np.float32(math.log(2048 / 16))
    large = 16 + (lr * np.float32(16.0)).astype(np.int32)
    return np.where(d < 16, d, np.minimum(large, 31)).astype(np.int64)


def bias_tables(rel_bias, S):
    WA = S + PADA
    p = np.arange(128)[:, None]
    j = np.arange(WA)[None, :]
    bA = _bucket(j - PADA - p)
    biasA = np.ascontiguousarray(rel_bias[bA][:, :, 0:4].transpose(2, 0, 1)).astype(np.float32)
    f = np.arange(256)[None, :]
    step = np.clip(f - p, 0, 128)
    biasD = np.zeros((12, 128, 256), np.float32)
    for gi, (_, dl) in enumerate(DIL):
        bD = _bucket(step * dl)
        for hs in range(4):
            biasD[gi * 4 + hs] = rel_bias[bD, 4 + gi * 4 + hs]
    return biasA, biasD


def lam_consts(layers):
    out = np.zeros((len(layers), 128, 2), np.float32)
    for i, l in enumerate(layers):
        li = 0.8 - 0.6 * math.exp(-0.3 * l)
        out[i, :, 0] = li
        out[i, :, 1] = 1.0 - li
    return out


_WNAMES = ["w_in", "da_lambda", "da_norm", "w_branch_da", "w_branch_sb", "w_branch_dil", "w_out", "ln1_g", "ln1_b",
           "w_up", "w_down", "w_ple_gate", "w_ple", "ln2_g", "ln2_b"]
_PROG = {}


def get_program(S, NL):
    key = (S, NL)
    if key not in _PROG:
        _PROG[key] = build_program(S, NL)
    return _PROG[key]


def make_in_maps(inputs, xs, layers, S):
    f32 = lambda a: np.ascontiguousarray(np.asarray(a, dtype=np.float32))
    biasA, biasD = bias_tables(f32(inputs["rel_bias"]), S)
    lamc = lam_consts(layers)
    shared = {}
    for n in _WNAMES:
        a = f32(inputs[n])[layers]
        if n == "da_lambda":
            a = a.reshape(len(layers), 256)
        shared[n] = np.ascontiguousarray(a)
    shared["biasA"] = biasA
    shared["biasD"] = biasD
    shared["lamc"] = lamc
    p = inputs["p"]
    maps = []
    for b in range(len(xs)):
        m = dict(shared)
        m["x"] = f32(xs[b])
        m["pT"] = np.ascontiguousarray(np.asarray(p[layers, b], dtype=np.float32).transpose(0, 2, 1))
        maps.append(m)
    return maps


FUSED = True


def kernel(**inputs):
    x = np.asarray(inputs["x"], dtype=np.float32)
    B, S, _ = x.shape
    NLT = inputs["w_in"].shape[0]
    xs = [x[b] for b in range(B)]
    if FUSED:
        nc = get_program(S, NLT)
        maps = make_in_maps(inputs, xs, list(range(NLT)), S)
        res = run_bass_kernel_spmd(nc, maps, core_ids=list(range(B)))
        xs = [np.asarray(r["out"]) for r in res.results]
    else:
        nc = get_program(S, 1)
        for l in range(NLT):
            maps = make_in_maps(inputs, xs, [l], S)
            res = run_bass_kernel_spmd(nc, maps, core_ids=list(range(B)))
            xs = [np.asarray(r["out"]) for r in res.results]
    return np.stack(xs, axis=0).astype(np.float32)
```

```python
import math
import numpy as np
from contextlib import ExitStack
import concourse.bass as bass
import concourse.mybir as mybir
from concourse.bass_utils import run_bass_kernel_spmd

F32 = mybir.dt.float32
BF16 = mybir.dt.bfloat16
AF = mybir.ActivationFunctionType
ALU = mybir.AluOpType
AX = mybir.AxisListType

D = 1024
DFF = 4096
PLE = 256
INC = 10752
A_Q, A_K, A_V = 0, 512, 1024
B_Q, B_K, B_V = 1536, 2048, 2560
C_Q, C_K, C_V = 3072, 4608, 6144
GATE = 7680
DIL = ((128, 1), (512, 4), (2048, 16))
DEPTH = 4
ALPHA = (2 * DEPTH) ** 0.25
LN_EPS = 1e-5
RMS_EPS = 1e-5
PADA = 384


class Buf:
    __slots__ = ("name", "w", "r")

    def __init__(self, name=""):
        self.name = name
        self.w = None
        self.r = {}


class Src:
    def __init__(self, name, sem, step):
        self.name = name
        self.sem = sem
        self.val = 0
        self.step = step


class Sched:
    ENGS = ("tensor", "vector", "scalar", "gpsimd", "sync")

    def __init__(self, nc, stack):
        self.nc = nc
        self.stack = stack
        self.ops = {e: [] for e in self.ENGS}
        self.src = {}
        self.seen = {e: {} for e in self.ENGS}
        self.chans = []
        for e in self.ENGS:
            self.src[e] = Src(e, stack.enter_context(nc.semaphore("s_" + e)), 1)
        self.nops = 0
        self.nroll = 0

    def chan(self, name):
        c = Src(name, self.stack.enter_context(self.nc.semaphore("c_" + name)), 16)
        self.chans.append(c)
        return c

    def _need(self, eng, deps):
        best = {}
        for s, v in deps:
            if best.get(s, 0) < v:
                best[s] = v
        for s, v in best.items():
            if self.seen[eng].get(s, 0) >= v:
                continue
            self.seen[eng][s] = v
            self.ops[eng].append(("wait", s.sem, v))

    @staticmethod
    def _deps(reads, writes):
        deps = []
        for b in reads:
            if b.w is not None:
                deps.append(b.w)
        for b in writes:
            if b.w is not None:
                deps.append(b.w)
            for rs, rv in b.r.items():
                deps.append((rs, rv))
        return deps

    LIMIT = 30000

    def _roll(self, s):
        if s.val < self.LIMIT:
            return s
        self.nroll += 1
        n = Src(s.name, self.stack.enter_context(self.nc.semaphore("r%d_%s" % (self.nroll, s.name))), s.step)
        return n

    def op(self, eng, fn, reads=(), writes=(), signal=True):
        s = self._roll(self.src[eng])
        self.src[eng] = s
        deps = self._deps(reads, writes)
        if eng == "tensor":
            deps = [d for d in deps if d[0] is not s]
        else:
            deps = [d for d in deps if not (d[0] is s and d[1] > s.val)]
        self._need(eng, deps)
        if signal:
            s.val += 1
            tag = (s, s.val)
            self.ops[eng].append(("op", fn, s.sem, 1))
        else:
            tag = (s, s.val + 1)
            self.ops[eng].append(("op", fn, None, 0))
        self.nops += 1
        for b in reads:
            if b.r.get(tag[0], 0) < tag[1]:
                b.r[tag[0]] = tag[1]
        for b in writes:
            b.w = tag
            b.r = {}
        return tag

    def dma(self, eng, ch, fn, reads=(), writes=()):
        deps = self._deps(reads, writes)
        self._need(eng, deps)
        ch.val += 16
        tag = (ch, ch.val)
        self.ops[eng].append(("op", fn, ch.sem, 16))
        self.nops += 1
        for b in reads:
            if b.r.get(ch, 0) < ch.val:
                b.r[ch] = ch.val
        for b in writes:
            b.w = tag
            b.r = {}
        return tag

    def barrier(self):
        allsrc = [(self.src[e], self.src[e].val) for e in self.ENGS if self.src[e].val > 0]
        allsrc += [(c, c.val) for c in self.chans if c.val > 0]
        for e in self.ENGS:
            self._need(e, [d for d in allsrc if d[0] is not self.src[e]])

    def wait_all(self, eng, tags):
        self._need(eng, tags)

    def prewait(self, eng, reads=(), writes=()):
        s = self.src[eng]
        deps = self._deps(reads, writes)
        if eng == "tensor":
            deps = [d for d in deps if d[0] is not s]
        else:
            deps = [d for d in deps if not (d[0] is s and d[1] > s.val)]
        self._need(eng, deps)

    def emit(self, block):
        def mk(eng):
            lst = self.ops[eng]

            def body(e):
                for it in lst:
                    if it[0] == "wait":
                        e.wait_ge(it[1], it[2])
                    else:
                        ins = it[1](e)
                        if it[2] is not None:
                            ins.then_inc(it[2], it[3])
            return body
        block.tensor(mk("tensor"))
        block.vector(mk("vector"))
        block.scalar(mk("scalar"))
        block.gpsimd(mk("gpsimd"))
        block.sync(mk("sync"))
        self.ops = {e: [] for e in self.ENGS}


def MM(out, lhsT, rhs, start, stop):
    return lambda e: e.matmul(out, lhsT=lhsT, rhs=rhs, start=start, stop=stop)


def TR(out, in_, ident):
    return lambda e: e.transpose(out, in_, ident)


def ACT(out, in_, func, scale=1.0, bias=0.0):
    return lambda e: e.activation(out=out, in_=in_, func=func, bias=bias, scale=scale)


def CP(out, in_):
    return lambda e: e.tensor_copy(out=out, in_=in_)


def TT(out, in0, in1, op):
    return lambda e: e.tensor_tensor(out=out, in0=in0, in1=in1, op=op)


def TS(out, in0, s1, s2, op0, op1=None):
    if op1 is None:
        return lambda e: e.tensor_scalar(out=out, in0=in0, scalar1=s1, scalar2=None, op0=op0)
    return lambda e: e.tensor_scalar(out=out, in0=in0, scalar1=s1, scalar2=s2, op0=op0, op1=op1)


def STT(out, in0, scalar, in1, op0, op1):
    return lambda e: e.scalar_tensor_tensor(out=out, in0=in0, scalar=scalar, in1=in1, op0=op0, op1=op1)


def DMA(out, in_):
    return lambda e: e.dma_start(out=out, in_=in_)


def ASEL(out, in_, pattern, cmp, fill, base, cm):
    return lambda e: e.affine_select(out=out, in_=in_, pattern=pattern, compare_op=cmp, fill=fill,
                                     base=base, channel_multiplier=cm)


class Rot:
    def __init__(self, items):
        self.items = items
        self.i = 0

    def next(self):
        it = self.items[self.i % len(self.items)]
        self.i += 1
        return it


def build_program(S, NL, dbg=False):
    T = S // 128
    G = S // 512
    WA = S + PADA
    nc = bass.Bass("TRN2", target_bir_lowering=False)

    def din(name, shape):
        return nc.dram_tensor(name, list(shape), F32, kind="ExternalInput").ap()

    x_d = din("x", [S, D])
    pT_d = din("pT", [NL, PLE, S])
    w_in_d = din("w_in", [NL, D, INC])
    lam_d = din("da_lambda", [NL, 256])
    dan_d = din("da_norm", [NL, 128])
    wb_d = [din("w_branch_da", [NL, 512, D]), din("w_branch_sb", [NL, 512, D]), din("w_branch_dil", [NL, 512, D])]
    wout_d = din("w_out", [NL, D, D])
    ln1g_d = din("ln1_g", [NL, D])
    ln1b_d = din("ln1_b", [NL, D])
    wup_d = din("w_up", [NL, D, DFF])
    wdn_d = din("w_down", [NL, DFF, D])
    wpg_d = din("w_ple_gate", [NL, D, D])
    wple_d = din("w_ple", [NL, PLE, D])
    ln2g_d = din("ln2_g", [NL, D])
    ln2b_d = din("ln2_b", [NL, D])
    biasA_d = din("biasA", [4, 128, WA])
    biasD_d = din("biasD", [12, 128, 256])
    lamc_d = din("lamc", [NL, 128, 2])
    out_d = nc.dram_tensor("out", [S, D], F32, kind="ExternalOutput").ap()
    okind = "ExternalOutput" if dbg else "Internal"
    res1_d = nc.dram_tensor("res1", [S, D], F32, kind=okind).ap()
    ot_d = nc.dram_tensor("ot", [12, 128, S], BF16, kind=okind).ap()
    ea_d = nc.dram_tensor("ea", [4, 128, WA], BF16, kind="Internal").ap()
    c1_d = nc.dram_tensor("c1", [S, D], F32, kind="Internal").ap()
    wsc = {
        "gate": nc.dram_tensor("wsc_gate", [D, 3072], BF16, kind="Internal").ap(),
        "br0": nc.dram_tensor("wsc_br0", [512, D], BF16, kind="Internal").ap(),
        "br1": nc.dram_tensor("wsc_br1", [512, D], BF16, kind="Internal").ap(),
        "br2": nc.dram_tensor("wsc_br2", [512, D], BF16, kind="Internal").ap(),
        "wo": nc.dram_tensor("wsc_wo", [D, D], BF16, kind="Internal").ap(),
        "up": nc.dram_tensor("wsc_up", [D, DFF], BF16, kind="Internal").ap(),
        "dn": nc.dram_tensor("wsc_dn", [DFF, D], BF16, kind="Internal").ap(),
        "pg": nc.dram_tensor("wsc_pg", [D, D], BF16, kind="Internal").ap(),
        "ple": nc.dram_tensor("wsc_ple", [PLE, D], BF16, kind="Internal").ap(),
    }

    with ExitStack() as st:
        Sc = Sched(nc, st)

        uid = [0]

        def sbuf(stack, name, shape, dt):
            uid[0] += 1
            return stack.enter_context(nc.sbuf_tensor("%s_u%d" % (name, uid[0]), list(shape), dt))

        banks = [st.enter_context(nc.psum_tensor("bank%d" % i, [128, 512], F32)) for i in range(8)]
        xT = sbuf(st, "xT", [128, 8, S], BF16)
        ident = sbuf(st, "ident", [128, 128], BF16)
        ones_bf = sbuf(st, "ones_bf", [128, 128], BF16)
        nones_bf = sbuf(st, "nones_bf", [128, 128], BF16)
        uneg = sbuf(st, "uneg", [128, 128], BF16)
        onesf = sbuf(st, "onesf", [128, 512], F32)

        class P:
            bank = [Buf("bank%d" % i) for i in range(8)]
            xT = [Buf("xT%d" % g) for g in range(G)]
            const = Buf("const")
            ED = Buf("ED")
            ea = [Buf("ea%d" % h) for h in range(4)]
            ot = [[Buf("ot%d_%d" % (i, g)) for g in range(G)] for i in range(12)]
            res1 = [Buf("res1_%d" % t) for t in range(T)]
            outb = [Buf("out_%d" % t) for t in range(T)]
            c1 = [Buf("c1_%d" % t) for t in range(T)]

        ch_out = Sc.chan("out")
        ch_res1 = Sc.chan("res1")
        ch_ot = Sc.chan("ot")
        ch_ea = Sc.chan("ea")
        ch_misc = [Sc.chan("misc%d" % i) for i in range(12)]
        ch_w = [Sc.chan("w%d" % i) for i in range(12)]
        ch_cast = {k: Sc.chan("cast_" + k) for k in wsc}
        wscb = {k: [] for k in wsc}

        def cast_jobs(l):
            jobs = []
            for r in range(0, D, 128):
                jobs.append(("gate", wsc["gate"][r:r + 128, :], w_in_d[l, r:r + 128, GATE:GATE + 3072]))
            for i in range(3):
                for r in range(0, 512, 128):
                    jobs.append(("br%d" % i, wsc["br%d" % i][r:r + 128, :], wb_d[i][l, r:r + 128, :]))
            for r in range(0, D, 128):
                jobs.append(("wo", wsc["wo"][r:r + 128, :], wout_d[l, r:r + 128, :]))
            for r in range(0, D, 128):
                jobs.append(("up", wsc["up"][r:r + 128, :], wup_d[l, r:r + 128, :]))
            for r in range(0, DFF, 512):
                jobs.append(("dn", wsc["dn"][r:r + 512, :], wdn_d[l, r:r + 512, :]))
            for r in range(0, D, 512):
                jobs.append(("pg", wsc["pg"][r:r + 512, :], wpg_d[l, r:r + 512, :]))
            jobs.append(("ple", wsc["ple"][:, :], wple_d[l, :, :]))
            return jobs

        def issue_casts(jobs):
            for (k, dst, src) in jobs:
                b_ = Buf("wsc_" + k)
                wscb[k].append(b_)
                Sc.dma("gpsimd", ch_cast[k], DMA(dst, src), writes=[b_])

        def end_phase(stack_unused=None):
            Sc.barrier()
            with nc.Block() as block:
                Sc.emit(block)

        def evac(i, out, in_, scale, reads, writes):
            if i % 2 == 0:
                Sc.op("scalar", ACT(out, in_, AF.Copy, scale=scale), reads=reads, writes=writes)
            else:
                Sc.op("vector", TS(out, in_, scale, None, ALU.mult), reads=reads, writes=writes)

        pj = Rot([(banks[6], P.bank[6]), (banks[7], P.bank[7])])
        cnt = {"ev": 0}

        def proj_feat(wt, wbuf, c0, dst_fn, dst_bufs, scale):
            for tg in range(G):
                bk, bb = pj.next()
                for kc in range(8):
                    Sc.op("tensor", MM(bk[:], wt[:, kc, c0:c0 + 128], xT[:, kc, tg * 512:(tg + 1) * 512], kc == 0, kc == 7),
                          reads=[wbuf, P.xT[tg]], writes=[bb], signal=(kc == 7))
                o, i_ = dst_fn(tg, bk)
                cnt["ev"] += 1
                evac(cnt["ev"], o, i_, scale, [bb], dst_bufs)

        def proj_tok(wt, wbuf, c0, ncols, vdst, vbuf, tok_ap_fn):
            per = 512 // ncols
            for t0 in range(0, T, per):
                bk, bb = pj.next()
                n = min(per, T - t0)
                for j in range(n):
                    for kc in range(8):
                        Sc.op("tensor", MM(bk[:, j * ncols:(j + 1) * ncols], tok_ap_fn(kc, t0 + j), wt[:, kc, c0:c0 + ncols], kc == 0, kc == 7),
                              reads=[wbuf] + P.xT, writes=[bb], signal=(kc == 7 and j == n - 1))
                cnt["ev"] += 1
                evac(cnt["ev"], vdst[:, t0:t0 + n, :], bk[:, 0:n * ncols].rearrange("p (t c) -> p t c", c=ncols), 1.0, [bb], [vbuf])

        def load_w_cols(stack_, wt, wbuf, ch, l, cols):
            o = 0
            for (c0, n) in cols:
                src = w_in_d[l, :, c0:c0 + n].rearrange("(kc p) c -> p kc c", p=128)
                Sc.dma("gpsimd", ch, DMA(wt[:, :, o:o + n], src), writes=[wbuf])
                o += n

        def transposes_to_xT(xb_ap, xb_buf, t):
            tp = banks[7][:].bitcast(BF16)
            for c in range(8):
                Sc.op("tensor", TR(tp[:, c * 128:(c + 1) * 128], xb_ap[:, c * 128:(c + 1) * 128], ident[:]),
                      reads=[xb_buf, P.const], writes=[P.bank[7]], signal=(c == 7))
            tg = t // 4
            Sc.op("scalar", ACT(xT[:, :, t * 128:(t + 1) * 128], tp.rearrange("p (c n) -> p c n", c=8), AF.Copy),
                  reads=[P.bank[7]], writes=[P.xT[tg]])

        with ExitStack() as ph:
            Sc.op("vector", lambda e: e.memset(onesf[:], 1.0), writes=[P.const])
            Sc.op("vector", CP(ones_bf[:], onesf[:, 0:128]), reads=[P.const], writes=[P.const])
            Sc.op("vector", TS(nones_bf[:], onesf[:, 0:128], -1.0, None, ALU.mult), reads=[P.const], writes=[P.const])
            Sc.op("gpsimd", ASEL(ident[:], ones_bf[:], [[1, 128]], ALU.is_equal, 0.0, 0, -1), reads=[P.const], writes=[P.const])
            Sc.op("gpsimd", ASEL(uneg[:], nones_bf[:], [[-1, 128]], ALU.is_ge, 0.0, 0, 1), reads=[P.const], writes=[P.const])
            CH = 1024
            rawA = [sbuf(ph, "rawA%d" % i, [128, CH], F32) for i in range(2)]
            rawAb = [Buf("rawA%d" % i) for i in range(2)]
            eab = [sbuf(ph, "eab%d" % i, [128, CH], BF16) for i in range(2)]
            eabb = [Buf("eab%d" % i) for i in range(2)]
            k = 0
            for h in range(4):
                for c0 in range(0, WA, CH):
                    n = min(CH, WA - c0)
                    i = k % 2
                    k += 1
                    Sc.dma("sync", ch_misc[1 + i], DMA(rawA[i][:, 0:n], biasA_d[h, :, c0:c0 + n]), writes=[rawAb[i]])
                    Sc.op("scalar", ACT(rawA[i][:, 0:n], rawA[i][:, 0:n], AF.Exp), reads=[rawAb[i]], writes=[rawAb[i]])
                    Sc.op("gpsimd", ASEL(eab[i][:, 0:n], rawA[i][:, 0:n], [[1, n]], ALU.is_ge, 0.0, c0 - PADA, -1),
                          reads=[rawAb[i]], writes=[eabb[i]])
                    Sc.dma("sync", ch_ea, DMA(ea_d[h, :, c0:c0 + n], eab[i][:, 0:n]), reads=[eabb[i]], writes=[P.ea[h]])
            xin = [sbuf(ph, "xin%d" % i, [128, D], F32) for i in range(2)]
            xinb = [Buf("xin%d" % i) for i in range(2)]
            xbf = [sbuf(ph, "xbf%d" % i, [128, D], BF16) for i in range(2)]
            xbfb = [Buf("xbf%d" % i) for i in range(2)]
            for t in range(T):
                i = t % 2
                Sc.dma("sync", ch_misc[3 + i], DMA(xin[i][:], x_d[t * 128:(t + 1) * 128, :]), writes=[xinb[i]])
                Sc.op("vector", CP(xbf[i][:], xin[i][:]), reads=[xinb[i]], writes=[xbfb[i]])
                transposes_to_xT(xbf[i], xbfb[i], t)
            end_phase()

        for l in range(NL):
            res_in = x_d if l == 0 else out_d
            res_in_bufs = None if l == 0 else P.outb

            with ExitStack() as ph:
                qk = [(sbuf(ph, "qA%d" % i, [128, S], BF16), sbuf(ph, "kA%d" % i, [128, S], BF16),
                       sbuf(ph, "vA%d" % i, [128, T, 128], BF16), Buf("qkvA%d" % i)) for i in range(2)]
                wA = [(sbuf(ph, "wA%d" % i, [128, 8, 384], BF16), Buf("wA%d" % i)) for i in range(2)]
                EAt = [(sbuf(ph, "EA%d" % i, [128, WA], BF16), Buf("EA%d" % i)) for i in range(2)]
                praw = Rot([(sbuf(ph, "praw%d" % i, [128, 512], BF16), Buf()) for i in range(6)])
                pTt = Rot([(sbuf(ph, "pT%d" % i, [128, 512], BF16), Buf()) for i in range(8)])
                lamt = sbuf(ph, "lamt", [128, 256], F32)
                lprod = sbuf(ph, "lprod", [128, 128], F32)
                lsum = sbuf(ph, "lsum", [128, 2], F32)
                lamc = sbuf(ph, "lamc", [128, 2], F32)
                nlam = sbuf(ph, "nlam", [128, 1], F32)
                gA = sbuf(ph, "gA", [128, 1], F32)
                lb = Buf("lam")
                r0 = sbuf(ph, "r0", [128, 512], F32)
                r1 = sbuf(ph, "r1", [128, 512], F32)
                t0 = sbuf(ph, "t0", [128, 512], F32)
                t1 = sbuf(ph, "t1", [128, 512], F32)
                oo = sbuf(ph, "oo", [128, 512], F32)
                sq = sbuf(ph, "sq", [128, 512], BF16)
                lnv = sbuf(ph, "lnv", [128, 512], F32)
                postb = Buf("post")
                ostage = Rot([(sbuf(ph, "ostA%d" % i, [128, 512], BF16), Buf()) for i in range(2)])

                Sc.dma("sync", ch_misc[0], DMA(lamt[:], lam_d[l, :].partition_broadcast(128)), writes=[lb])
                Sc.dma("sync", ch_misc[1], DMA(lamc[:], lamc_d[l, :, :]), writes=[lb])
                Sc.dma("sync", ch_misc[2], DMA(gA[:], dan_d[l, :].rearrange("(p o) -> p o", o=1)), writes=[lb])
                l4 = lamt[:].rearrange("p (a b d) -> p a b d", a=2, b=2)
                Sc.op("vector", TT(lprod[:].rearrange("p (a d) -> p a d", a=2), l4[:, :, 0, :], l4[:, :, 1, :], ALU.mult), reads=[lb], writes=[lb])
                Sc.op("vector", lambda e: e.tensor_reduce(out=lsum[:], in_=lprod[:].rearrange("p (a d) -> p a d", a=2), axis=AX.X, op=ALU.add),
                      reads=[lb], writes=[lb])
                Sc.op("scalar", ACT(lsum[:], lsum[:], AF.Exp), reads=[lb], writes=[lb])
                Sc.op("vector", TT(nlam[:], lsum[:, 1:2], lsum[:, 0:1], ALU.subtract), reads=[lb], writes=[lb])
                Sc.op("vector", TT(nlam[:], nlam[:], lamc[:, 0:1], ALU.subtract), reads=[lb], writes=[lb])
                Sc.op("vector", TT(gA[:], gA[:], lamc[:, 1:2], ALU.mult), reads=[lb], writes=[lb])

                def loadA(h):
                    wt, wbuf = wA[h % 2]
                    load_w_cols(ph, wt, wbuf, ch_w[h % 2], l, [(A_Q + h * 128, 128), (A_K + h * 128, 128), (A_V + h * 128, 128)])
                    et, eb = EAt[h % 2]
                    Sc.dma("sync", ch_w[2 + h % 2], DMA(et[:], ea_d[h, :, :]), reads=[P.ea[h]], writes=[eb])

                loadA(0)
                for h in range(4):
                    if h + 1 < 4:
                        loadA(h + 1)
                    wt, wbuf = wA[h % 2]
                    et, eb = EAt[h % 2]
                    qT, kT, vv, qb = qk[h % 2]
                    proj_feat(wt, wbuf, 0, lambda tg, bk: (qT[:, tg * 512:(tg + 1) * 512], bk[:]), [qb], 0.125)
                    proj_feat(wt, wbuf, 128, lambda tg, bk: (kT[:, tg * 512:(tg + 1) * 512], bk[:]), [qb], 1.0)
                    proj_tok(wt, wbuf, 256, 128, vv, qb, lambda kc, t: xT[:, kc, t * 128:(t + 1) * 128])
                    for g in range(G):
                        nk = 4 * g + 4
                        n = 2 * nk
                        Ub = [(banks[2], P.bank[2]), (banks[3], P.bank[3])]
                        Sb_ = [(banks[4], P.bank[4]), (banks[5], P.bank[5])]
                        sbk = [(banks[0], P.bank[0]), (banks[1], P.bank[1]), (banks[6], P.bank[6]), (banks[7], P.bank[7])]
                        pts = {}
                        prs = {}

                        def stage_S(kt):
                            for m in range(2):
                                bk, bb = sbk[(2 * kt + m) % 4]
                                Sc.op("tensor", MM(bk[:], kT[64 * m:64 * m + 64, kt * 128:(kt + 1) * 128],
                                                   qT[64 * m:64 * m + 64, g * 512:(g + 1) * 512], True, True),
                                      reads=[qb], writes=[bb], signal=(m == 1))

                        def stage_E(kt):
                            for m in range(2):
                                bk, bb = sbk[(2 * kt + m) % 4]
                                pr, prb = praw.next()
                                Sc.op("scalar", ACT(pr[:], bk[:], AF.Exp), reads=[bb], writes=[prb])
                                prs[(kt, m)] = (pr, prb)

                        def stage_M(kt):
                            for m in range(2):
                                pr, prb = prs.pop((kt, m))
                                pt, ptb = pTt.next()
                                off = PADA + g * 512 - kt * 128
                                Sc.op("vector", TT(pt[:], pr[:], et[:, off:off + 512], ALU.mult), reads=[prb, eb], writes=[ptb])
                                pts[(kt, m)] = (pt, ptb)

                        def stage_V(kt):
                            for m in range(2):
                                pt, ptb = pts.pop((kt, m))
                                ub, ubb = Ub[m]
                                sb_, sbb = Sb_[m]
                                Sc.op("tensor", MM(ub[:], vv[:, kt, :], pt[:], kt == 0, kt == nk - 1),
                                      reads=[qb, ptb], writes=[ubb], signal=False)
                                Sc.op("tensor", MM(sb_[:], ones_bf[:], pt[:], kt == 0, kt == nk - 1),
                                      reads=[P.const, ptb], writes=[sbb], signal=(m == 1))

                        for it in range(nk + 2):
                            rd = [qb, P.const]
                            wr = []
                            if it < nk:
                                wr += [sbk[(2 * it + m) % 4][1] for m in range(2)]
                            if it >= 2:
                                rd += [pts[(it - 2, m)][1] for m in range(2)]
                                wr += [Ub[0][1], Ub[1][1], Sb_[0][1], Sb_[1][1]]
                            Sc.prewait("tensor", rd, wr)
                            if it < nk:
                                stage_S(it)
                            if it >= 2:
                                stage_V(it - 2)
                            if it < nk:
                                stage_E(it)
                            if 1 <= it <= nk:
                                stage_M(it - 1)
                        Sc.op("scalar", ACT(r0[:], banks[4][:], AF.Ln), reads=[P.bank[4]], writes=[postb])
                        Sc.op("scalar", ACT(r1[:], banks[5][:], AF.Ln), reads=[P.bank[5]], writes=[postb])
                        Sc.op("scalar", ACT(r0[:], r0[:], AF.Exp, scale=-1.0), reads=[postb], writes=[postb])
                        Sc.op("scalar", ACT(r1[:], r1[:], AF.Exp, scale=-1.0), reads=[postb], writes=[postb])
                        Sc.op("vector", TT(t0[:], banks[2][:], r0[:], ALU.mult), reads=[P.bank[2], postb], writes=[postb])
                        Sc.op("vector", TT(t1[:], banks[3][:], r1[:], ALU.mult), reads=[P.bank[3], postb], writes=[postb])
                        Sc.op("vector", STT(oo[:], t1[:], nlam[:, 0:1], t0[:], ALU.mult, ALU.add), reads=[postb, lb], writes=[postb])
                        Sc.op("scalar", ACT(sq[:], oo[:], AF.Square), reads=[postb], writes=[postb])
                        Sc.op("tensor", MM(banks[0][:], ones_bf[:], sq[:], True, True), reads=[postb, P.const], writes=[P.bank[0]])
                        Sc.op("scalar", ACT(lnv[:], banks[0][:], AF.Ln, scale=1.0 / 128.0, bias=RMS_EPS), reads=[P.bank[0]], writes=[postb])
                        Sc.op("scalar", ACT(lnv[:], lnv[:], AF.Exp, scale=-0.5), reads=[postb], writes=[postb])
                        os_, osb = ostage.next()
                        Sc.op("vector", STT(os_[:], oo[:], gA[:, 0:1], lnv[:], ALU.mult, ALU.mult), reads=[postb, lb], writes=[osb])
                        Sc.dma("sync", ch_ot, DMA(ot_d[h, :, g * 512:(g + 1) * 512], os_[:]), reads=[osb], writes=[P.ot[h][g]])
                end_phase()

            with ExitStack() as ph:
                qk = [(sbuf(ph, "qB%d" % i, [128, S], BF16), sbuf(ph, "kB%d" % i, [128, S], BF16),
                       sbuf(ph, "vB%d" % i, [128, T, 128], BF16), Buf("qkvB%d" % i)) for i in range(2)]
                wB = [(sbuf(ph, "wB%d" % i, [128, 8, 384], BF16), Buf("wB%d" % i)) for i in range(2)]
                e32 = Rot([(sbuf(ph, "e32_%d" % i, [128, 512], F32), Buf()) for i in range(3)])
                spt = Rot([(sbuf(ph, "sp%d" % i, [128, 512], BF16), Buf()) for i in range(5)])
                tmpt = Rot([(sbuf(ph, "tmpB%d" % i, [128, 512], F32), Buf()) for i in range(3)])
                at = Rot([(sbuf(ph, "aB%d" % i, [128, 512], BF16), Buf()) for i in range(6)])
                csb = [(sbuf(ph, "csb%d" % i, [128, 512], F32), Buf()) for i in range(2)]
                ostage = Rot([(sbuf(ph, "ostB%d" % i, [128, 512], BF16), Buf()) for i in range(2)])
                maskB = sbuf(ph, "maskB", [128, 4, 512], BF16)
                mkb = Buf("maskB")
                for dd in range(4):
                    Sc.op("gpsimd", ASEL(maskB[:, dd, :], onesf[:], [[1, 512]], ALU.is_gt, 0.0, -128 * dd, -1),
                          reads=[P.const], writes=[mkb])

                def loadB(hp):
                    wt, wbuf = wB[hp % 2]
                    load_w_cols(ph, wt, wbuf, ch_w[hp % 2], l, [(B_Q + hp * 128, 128), (B_K + hp * 128, 128), (B_V + hp * 128, 128)])

                loadB(0)
                for k_ in wscb:
                    wscb[k_] = []
                cjobs = cast_jobs(l)
                cper = (len(cjobs) + 3) // 4
                for hp in range(4):
                    if hp + 1 < 4:
                        loadB(hp + 1)
                    issue_casts(cjobs[hp * cper:(hp + 1) * cper])
                    wt, wbuf = wB[hp % 2]
                    qT, kT, vv, qb = qk[hp % 2]
                    proj_feat(wt, wbuf, 0, lambda tg, bk: (qT[:, tg * 512:(tg + 1) * 512], bk[:]), [qb], 0.125)
                    proj_feat(wt, wbuf, 128, lambda tg, bk: (kT[:, tg * 512:(tg + 1) * 512], bk[:]), [qb], 1.0)
                    proj_tok(wt, wbuf, 256, 128, vv, qb, lambda kc, t: xT[:, kc, t * 128:(t + 1) * 128])
                    for g in range(G):
                        n = 4 * g + 4
                        for hh in range(2):
                            rs = slice(64 * hh, 64 * hh + 64)
                            zb = [(banks[0], P.bank[0]), (banks[1], P.bank[1])]
                            za = [(banks[2], P.bank[2]), (banks[3], P.bank[3])]
                            cbk = [(banks[4], P.bank[4]), (banks[7], P.bank[7])]
                            otb = [(banks[5], P.bank[5]), (banks[6], P.bank[6])]
                            NFILL = 0
                            st_e = {}
                            st_sp = {}
                            st_a = {}

                            def kt_of(i):
                                return 4 * g + 3 - i

                            def stZ1_pe(i):
                                kt = kt_of(i)
                                bk, bb = zb[i % 2]
                                Sc.op("tensor", MM(bk[:], kT[rs, kt * 128:(kt + 1) * 128], qT[rs, g * 512:(g + 1) * 512], True, True),
                                      reads=[qb], writes=[bb])

                            def stZ1_act(i):
                                bk, bb = zb[i % 2]
                                e_, eb_ = e32.next()
                                Sc.op("scalar", ACT(e_[:], bk[:], AF.Exp), reads=[bb], writes=[eb_])
                                st_e[i] = (e_, eb_)

                            def stZ2(i):
                                e_, eb_ = st_e.pop(i)
                                sp_, spb = spt.next()
                                Sc.op("scalar", ACT(sp_[:], e_[:], AF.Ln, bias=1.0), reads=[eb_], writes=[spb])
                                if i <= 3:
                                    Sc.op("gpsimd", TT(sp_[:], sp_[:], maskB[:, 3 - i, :], ALU.mult), reads=[spb, mkb], writes=[spb])
                                st_sp[i] = (sp_, spb)

                            def stA_pe(i):
                                sp_, spb = st_sp[i]
                                kt = kt_of(i)
                                bk, bb = za[i % 2]
                                Sc.op("tensor", MM(bk[:], uneg[:], sp_[:], True, False), reads=[spb, P.const], writes=[bb], signal=False)
                                Sc.op("tensor", MM(bk[:], kT[rs, kt * 128:(kt + 1) * 128], qT[rs, g * 512:(g + 1) * 512], False, True),
                                      reads=[qb], writes=[bb])
                                if i < n - 1:
                                    cb_, cbb_ = cbk[i % 2]
                                    Sc.op("tensor", MM(cb_[:], nones_bf[:], sp_[:], True, True), reads=[spb, P.const], writes=[cbb_])

                            def stA_dve(i):
                                st_sp.pop(i)
                                if i < n - 1:
                                    cb_, cbb_ = cbk[i % 2]
                                    cs, csbuf = csb[i % 2]
                                    if i == 0:
                                        Sc.op("vector", CP(cs[:], cb_[:]), reads=[cbb_], writes=[csbuf])
                                    else:
                                        pc, pcb = csb[(i - 1) % 2]
                                        Sc.op("vector", TT(cs[:], cb_[:], pc[:], ALU.add), reads=[cbb_, pcb], writes=[csbuf])

                            def stD(i):
                                bk, bb = za[i % 2]
                                a_, ab_ = at.next()
                                if i == 0:
                                    Sc.op("scalar", ACT(a_[:], bk[:], AF.Exp), reads=[bb], writes=[ab_])
                                else:
                                    pc, pcb = csb[(i - 1) % 2]
                                    tm, tmb = tmpt.next()
                                    Sc.op("vector", TT(tm[:], bk[:], pc[:], ALU.add), reads=[bb, pcb], writes=[tmb])
                                    Sc.op("scalar", ACT(a_[:], tm[:], AF.Exp), reads=[tmb], writes=[ab_])
                                if i <= 3:
                                    Sc.op("gpsimd", TT(a_[:], a_[:], maskB[:, 3 - i, :], ALU.mult), reads=[ab_, mkb], writes=[ab_])
                                st_a[i] = (a_, ab_)

                            def stV(i):
                                kt = kt_of(i)
                                a_, ab_ = st_a.pop(i)
                                Sc.op("tensor", MM(otb[hh][0][:], vv[:, kt, :], a_[:], i == 0, i == n - 1),
                                      reads=[qb, ab_], writes=[otb[hh][1]], signal=(i == n - 1))

                            for it in range(n + 4):
                                rd = [qb, P.const]
                                wr = []
                                if it < n:
                                    wr.append(zb[it % 2][1])
                                if 0 <= it - 2 < n:
                                    rd.append(st_sp[it - 2][1])
                                    wr.append(za[(it - 2) % 2][1])
                                    if it - 2 < n - 1:
                                        wr.append(cbk[(it - 2) % 2][1])
                                if 0 <= it - 4 < n:
                                    rd.append(st_a[it - 4][1])
                                    wr.append(otb[hh][1])
                                Sc.prewait("tensor", rd, wr)
                                if it < n:
                                    stZ1_pe(it)
                                if 0 <= it - 2 < n:
                                    stA_pe(it - 2)
                                if 0 <= it - 4 < n:
                                    stV(it - 4)
                                if it < n:
                                    for _f in range(NFILL):
                                        Sc.op("tensor", MM(banks[6][:], ones_bf[:], qT[:, g * 512:(g + 1) * 512], True, True),
                                              reads=[qb, P.const], writes=[P.bank[6]], signal=False)
                                if it < n:
                                    stZ1_act(it)
                                if 0 <= it - 2 < n:
                                    stD(it - 2)
                                if it < n:
                                    stZ2(it)
                                if 0 <= it - 2 < n:
                                    stA_dve(it - 2)
                        os_, osb = ostage.next()
                        Sc.op("vector", CP(os_[0:64, :], banks[5][0:64, :]), reads=[P.bank[5]], writes=[osb])
                        Sc.op("scalar", ACT(os_[64:128, :], banks[6][64:128, :], AF.Copy), reads=[P.bank[6]], writes=[osb])
                        Sc.dma("sync", ch_ot, DMA(ot_d[4 + hp, :, g * 512:(g + 1) * 512], os_[:]), reads=[osb], writes=[P.ot[4 + hp][g]])
                end_phase()

            with ExitStack() as ph:
                qk = [(sbuf(ph, "qC%d" % i, [128, S], BF16), sbuf(ph, "kC%d" % i, [128, S], BF16),
                       sbuf(ph, "vC%d" % i, [128, T, 128], BF16), Buf("qkvC%d" % i)) for i in range(2)]
                wC = [(sbuf(ph, "wC%d" % i, [128, 8, 384], BF16), Buf("wC%d" % i)) for i in range(2)]
                praw = Rot([(sbuf(ph, "prawC%d" % i, [128, 256], F32), Buf()) for i in range(3)])
                pTt = Rot([(sbuf(ph, "pTC%d" % i, [128, 256], BF16), Buf()) for i in range(3)])
                Uacc = sbuf(ph, "Uacc", [128, S], F32)
                Sacc = sbuf(ph, "Sacc", [128, S], F32)
                accb = Buf("acc")
                rc = sbuf(ph, "rcC", [128, 512], F32)
                rcb = Buf("rc")
                ostage = Rot([(sbuf(ph, "ostC%d" % i, [128, 512], BF16), Buf()) for i in range(2)])
                scale_c = 128.0 ** -0.5
                ED = sbuf(ph, "ED", [128, 12, 256], BF16)
                rawD = sbuf(ph, "rawD", [128, 12, 256], F32)
                rawDb = Buf("rawD")
                Sc.dma("sync", ch_misc[0], DMA(rawD[:], biasD_d.rearrange("h p f -> p h f")), writes=[rawDb])
                Sc.op("scalar", ACT(rawD[:], rawD[:], AF.Exp), reads=[rawDb], writes=[rawDb])
                for hh in range(12):
                    Sc.op("gpsimd", ASEL(rawD[:, hh, :], rawD[:, hh, :], [[1, 256]], ALU.is_ge, 0.0, 0, -1), reads=[rawDb], writes=[rawDb])
                    Sc.op("gpsimd", ASEL(ED[:, hh, :], rawD[:, hh, :], [[-1, 256]], ALU.is_ge, 0.0, 128, 1), reads=[rawDb], writes=[P.ED])

                def loadC(idx):
                    gi, hs = idx % 3, idx // 3
                    hd = gi * 4 + hs
                    wt, wbuf = wC[idx % 2]
                    load_w_cols(ph, wt, wbuf, ch_w[idx % 2], l, [(C_Q + hd * 128, 128), (C_K + hd * 128, 128), (C_V + hd * 128, 128)])

                loadC(0)
                for idx in range(12):
                    if idx + 1 < 12:
                        loadC(idx + 1)
                    gi, hs = idx % 3, idx // 3
                    hd = gi * 4 + hs
                    dl = DIL[gi][1]
                    L = S // dl
                    nblk = L // 128
                    wt, wbuf = wC[idx % 2]
                    qT, kT, vv, qb = qk[idx % 2]

                    def dstq(tg, bk, dst=None):
                        if dl == 1:
                            return dst[:, tg * 512:(tg + 1) * 512], bk[:]
                        m0 = tg * (512 // dl)
                        return (dst[:].rearrange("p (b m) -> p b m", b=dl)[:, :, m0:m0 + 512 // dl],
                                bk[:].rearrange("p (a b) -> p b a", b=dl))

                    proj_feat(wt, wbuf, 0, lambda tg, bk: dstq(tg, bk, qT), [qb], scale_c)
                    proj_feat(wt, wbuf, 128, lambda tg, bk: dstq(tg, bk, kT), [qb], 1.0)

                    def tokap(kc, pi):
                        r, j = pi // nblk, pi % nblk
                        s0 = r + dl * 128 * j
                        return xT[:, kc, s0:s0 + dl * 127 + 1:dl]

                    proj_tok(wt, wbuf, 256, 128, vv, qb, tokap)
                    ub = [(banks[2], P.bank[2]), (banks[3], P.bank[3])]
                    sb_ = [(banks[4], P.bank[4]), (banks[5], P.bank[5])]
                    sbk = [(banks[0], P.bank[0]), (banks[1], P.bank[1])]
                    pts = {}

                    def cS(pi):
                        r, j = pi // nblk, pi % nblk
                        ncol = 256 if j < nblk - 1 else 128
                        bk, bb = sbk[pi % 2]
                        Sc.op("tensor", MM(bk[:, 0:ncol], kT[:, pi * 128:(pi + 1) * 128], qT[:, pi * 128:pi * 128 + ncol], True, True),
                              reads=[qb], writes=[bb])
                        pr, prb = praw.next()
                        Sc.op("scalar", ACT(pr[:, 0:ncol], bk[:, 0:ncol], AF.Exp), reads=[bb], writes=[prb])
                        pt, ptb = pTt.next()
                        eng = "vector" if pi % 2 == 0 else "gpsimd"
                        Sc.op(eng, TT(pt[:, 0:ncol], pr[:, 0:ncol], ED[:, hd, 0:ncol], ALU.mult), reads=[prb, P.ED], writes=[ptb])
                        pts[pi] = (pt, ptb)

                    def cV(pi):
                        r, j = pi // nblk, pi % nblk
                        pt, ptb = pts.pop(pi)
                        u, ubb = ub[(pi // 4) % 2]
                        s_, sbb = sb_[(pi // 4) % 2]
                        c = pi % 4
                        Sc.op("tensor", MM(u[:, c * 128:(c + 1) * 128], vv[:, pi, :], pt[:, 0:128], j == 0, True),
                              reads=[qb, ptb], writes=[ubb], signal=False)
                        Sc.op("tensor", MM(s_[:, c * 128:(c + 1) * 128], ones_bf[:], pt[:, 0:128], j == 0, True),
                              reads=[P.const, ptb], writes=[sbb], signal=True)
                        if c == 3 or pi == T - 1:
                            flush(pi // 4)
                        if j < nblk - 1:
                            u2, ubb2 = ub[((pi + 1) // 4) % 2]
                            s2, sbb2 = sb_[((pi + 1) // 4) % 2]
                            c2 = (pi + 1) % 4
                            Sc.op("tensor", MM(u2[:, c2 * 128:(c2 + 1) * 128], vv[:, pi, :], pt[:, 128:256], True, False),
                                  reads=[qb, ptb], writes=[ubb2], signal=False)
                            Sc.op("tensor", MM(s2[:, c2 * 128:(c2 + 1) * 128], ones_bf[:], pt[:, 128:256], True, False),
                                  reads=[P.const, ptb], writes=[sbb2], signal=False)

                    def flush(bi):
                        u, ubb = ub[bi % 2]
                        s_, sbb = sb_[bi % 2]
                        for c in range(4):
                            pi = bi * 4 + c
                            if pi >= T:
                                break
                            r, j = pi // nblk, pi % nblk
                            s0 = r + dl * 128 * j
                            dU = Uacc[:, s0:s0 + dl * 127 + 1:dl]
                            dS = Sacc[:, s0:s0 + dl * 127 + 1:dl]
                            if gi == 0:
                                Sc.op("vector", CP(dU, u[:, c * 128:(c + 1) * 128]), reads=[ubb], writes=[accb])
                                Sc.op("vector", CP(dS, s_[:, c * 128:(c + 1) * 128]), reads=[sbb], writes=[accb])
                            else:
                                Sc.op("vector", TT(dU, u[:, c * 128:(c + 1) * 128], dU, ALU.add), reads=[ubb, accb], writes=[accb])
                                Sc.op("vector", TT(dS, s_[:, c * 128:(c + 1) * 128], dS, ALU.add), reads=[sbb, accb], writes=[accb])

                    for step in range(T + 1):
                        if step < T:
                            cS(step)
                        if step >= 1:
                            cV(step - 1)
                    if gi == 2:
                        for g in range(G):
                            Sc.op("vector", lambda e, g=g: e.reciprocal(out=rc[:], in_=Sacc[:, g * 512:(g + 1) * 512]), reads=[accb], writes=[rcb])
                            os_, osb = ostage.next()
                            Sc.op("vector", TT(os_[:], Uacc[:, g * 512:(g + 1) * 512], rc[:], ALU.mult), reads=[accb, rcb], writes=[osb])
                            Sc.dma("sync", ch_ot, DMA(ot_d[8 + hs, :, g * 512:(g + 1) * 512], os_[:]), reads=[osb], writes=[P.ot[8 + hs][g]])
                end_phase()

            def ln_tail(ph_t, hh, hb, gbc, bbc, lnb, t, dst_d, dst_buf, ch_dst, stats, mv, xn, xnb_, xnf_b, xnb_b):
                Sc.op("vector", lambda e: e.bn_stats(out=stats[:, 0:6], in_=hh[:, 0:512]), reads=[hb], writes=[lnb])
                Sc.op("vector", lambda e: e.bn_stats(out=stats[:, 6:12], in_=hh[:, 512:1024]), reads=[hb], writes=[lnb])
                Sc.op("vector", lambda e: e.bn_aggr(out=mv[:], in_=stats[:]), reads=[lnb], writes=[lnb])
                Sc.op("scalar", ACT(mv[:, 1:2], mv[:, 1:2], AF.Sqrt, bias=LN_EPS), reads=[lnb], writes=[lnb])
                Sc.op("vector", lambda e: e.reciprocal(out=mv[:, 1:2], in_=mv[:, 1:2]), reads=[lnb], writes=[lnb])
                Sc.op("vector", TS(xn[:], hh[:], mv[:, 0:1], mv[:, 1:2], ALU.subtract, ALU.mult), reads=[hb, lnb], writes=[xnf_b])
                Sc.op("gpsimd", TT(xn[:], xn[:], gbc[:], ALU.mult), reads=[xnf_b, P.const], writes=[xnf_b])
                Sc.op("gpsimd", TT(xn[:], xn[:], bbc[:], ALU.add), reads=[xnf_b, P.const], writes=[xnf_b])
                Sc.dma("sync", ch_dst, DMA(dst_d[t * 128:(t + 1) * 128, :], xn[:]), reads=[xnf_b], writes=[dst_buf])
                Sc.op("gpsimd", CP(xnb_[:], xn[:]), reads=[xnf_b], writes=[xnb_b])
                return lambda: transposes_to_xT(xnb_, xnb_b, t)

            with ExitStack() as ph:
                wg = [(sbuf(ph, "wg%d" % i, [128, 8, 512], BF16), Buf("wg%d" % i)) for i in range(2)]
                wbr = [(sbuf(ph, "wbr%d" % i, [128, 4, 1024], BF16), Buf("wbr%d" % i)) for i in range(3)]
                wo = sbuf(ph, "wo", [128, 8, 1024], BF16)
                wob = Buf("wo")
                gbc = sbuf(ph, "g1bc", [128, D], F32)
                bbc = sbuf(ph, "b1bc", [128, D], F32)
                otile = [(sbuf(ph, "otile%d" % i, [128, 12, 512], BF16), Buf("otile%d" % i)) for i in range(1)]
                merged = sbuf(ph, "merged", [128, 8, 512], BF16)
                mergb = Buf("merged")
                sg = Rot([(sbuf(ph, "sgM%d" % i, [128, 512], F32), Buf()) for i in range(2)])
                macc = sbuf(ph, "macc", [128, 4, 512], F32)
                maccb = Buf("macc")
                xres = [(sbuf(ph, "xres%d" % i, [128, D], F32), Buf()) for i in range(2)]
                hh = [(sbuf(ph, "hM%d" % i, [128, D], F32), Buf()) for i in range(2)]
                xn = [(sbuf(ph, "xnM%d" % i, [128, D], F32), Buf()) for i in range(2)]
                xnb = [(sbuf(ph, "xnbM%d" % i, [128, D], BF16), Buf()) for i in range(2)]
                stats = sbuf(ph, "statsM", [128, 12], F32)
                mv = sbuf(ph, "mvM", [128, 2], F32)
                lnb = Buf("lnM")

                for i in range(3):
                    Sc.dma("sync", ch_w[2 + i], DMA(wbr[i][0][:], wsc["br%d" % i].rearrange("(c p) n -> p c n", p=128)), reads=wscb["br%d" % i], writes=[wbr[i][1]])
                Sc.dma("sync", ch_w[5], DMA(wo[:], wsc["wo"].rearrange("(c p) n -> p c n", p=128)), reads=wscb["wo"], writes=[wob])
                Sc.dma("sync", ch_misc[0], DMA(gbc[:], ln1g_d[l, :].partition_broadcast(128)), writes=[P.const])
                Sc.dma("sync", ch_misc[1], DMA(bbc[:], ln1b_d[l, :].partition_broadcast(128)), writes=[P.const])
                gq = Rot([(banks[0], P.bank[0]), (banks[1], P.bank[1])])
                bq = Rot([(banks[2], P.bank[2]), (banks[3], P.bank[3])])
                kk = 0
                pend = [None]
                for tg in range(G):
                    ot_, otb = otile[0]
                    Sc.dma("sync", ch_misc[2 + tg % 2], DMA(ot_[:], ot_d[:, :, tg * 512:(tg + 1) * 512].rearrange("c p n -> p c n")),
                           reads=[P.ot[i][tg] for i in range(12)], writes=[otb])
                    for half2 in range(2):
                        for i in range(3):
                            wgt, wgb = wg[kk % 2]
                            c0 = i * 1024 + half2 * 512
                            Sc.dma("sync", ch_w[kk % 2], DMA(wgt[:], wsc["gate"][:, c0:c0 + 512].rearrange("(c p) n -> p c n", p=128)),
                                   reads=wscb["gate"], writes=[wgb])
                            kk += 1
                            for oc4 in range(4):
                                oc = half2 * 4 + oc4
                                gb_, gbb = gq.next()
                                for kc in range(8):
                                    Sc.op("tensor", MM(gb_[:], wgt[:, kc, oc4 * 128:(oc4 + 1) * 128], xT[:, kc, tg * 512:(tg + 1) * 512], kc == 0, kc == 7),
                                          reads=[wgb, P.xT[tg]], writes=[gbb], signal=(kc == 7))
                                bb_, bbb = bq.next()
                                for c4 in range(4):
                                    Sc.op("tensor", MM(bb_[:], wbr[i][0][:, c4, oc * 128:(oc + 1) * 128], ot_[:, 4 * i + c4, :], c4 == 0, c4 == 3),
                                          reads=[wbr[i][1], otb], writes=[bbb], signal=(c4 == 3))
                                s_, sb2 = sg.next()
                                Sc.op("scalar", ACT(s_[:], gb_[:], AF.Sigmoid), reads=[gbb], writes=[sb2])
                                if i == 0:
                                    Sc.op("vector", TT(macc[:, oc4, :], s_[:], bb_[:], ALU.mult), reads=[sb2, bbb], writes=[maccb])
                                elif i == 1:
                                    Sc.op("vector", TT(s_[:], s_[:], bb_[:], ALU.mult), reads=[sb2, bbb], writes=[sb2])
                                    Sc.op("gpsimd", TT(macc[:, oc4, :], macc[:, oc4, :], s_[:], ALU.add), reads=[sb2, maccb], writes=[maccb])
                                else:
                                    Sc.op("vector", TT(s_[:], s_[:], bb_[:], ALU.mult), reads=[sb2, bbb], writes=[sb2])
                                    Sc.op("gpsimd", TT(merged[:, oc, :], macc[:, oc4, :], s_[:], ALU.add), reads=[sb2, maccb], writes=[mergb])
                    for tt_ in range(4):
                        t = tg * 4 + tt_
                        xr, xrb = xres[t % 2]
                        rd = [] if res_in_bufs is None else [res_in_bufs[t]]
                        Sc.dma("sync", ch_misc[4 + t % 2], DMA(xr[:], res_in[t * 128:(t + 1) * 128, :]), reads=rd, writes=[xrb])
                        h_, hb = hh[t % 2]
                        for half in range(2):
                            yb, ybb = (banks[4], P.bank[4]) if half == 0 else (banks[5], P.bank[5])
                            for oc in range(8):
                                Sc.op("tensor", MM(yb[:], merged[:, oc, tt_ * 128:(tt_ + 1) * 128], wo[:, oc, half * 512:(half + 1) * 512], oc == 0, oc == 7),
                                      reads=[mergb, wob], writes=[ybb], signal=(oc == 7))
                            Sc.op("vector", STT(h_[:, half * 512:(half + 1) * 512], xr[:, half * 512:(half + 1) * 512], ALPHA, yb[:], ALU.mult, ALU.add),
                                  reads=[xrb, ybb], writes=[hb])
                        if pend[0] is not None:
                            pend[0]()
                        pend[0] = ln_tail(ph, h_, hb, gbc, bbc, lnb, t, res1_d, P.res1[t], ch_res1, stats, mv, xn[t % 2][0], xnb[t % 2][0], xn[t % 2][1], xnb[t % 2][1])
                if pend[0] is not None:
                    pend[0]()
                end_phase()

            for hp_ in range(2):
                with ExitStack() as ph:
                    wupr = sbuf(ph, "wupr", [128, 8, 2048], BF16)
                    wupb = [Buf() for _ in range(4)]
                    wdnr = sbuf(ph, "wdnr", [128, 16, 1024], BF16)
                    wdnb = [Buf() for _ in range(4)]
                    hidT = sbuf(ph, "hidT", [128, 16, 512], BF16)
                    hidb = Buf("hidT")
                    r32 = Rot([(sbuf(ph, "r32_%d" % i, [128, 512], F32), Buf()) for i in range(2)])
                    for q4 in range(4):
                        c0 = hp_ * 2048 + q4 * 512
                        Sc.dma("sync", ch_w[q4], DMA(wupr[:, :, q4 * 512:(q4 + 1) * 512], wsc["up"][:, c0:c0 + 512].rearrange("(c p) n -> p c n", p=128)),
                               reads=wscb["up"], writes=[wupb[q4]])
                    for q4 in range(4):
                        r0_ = hp_ * 2048 + q4 * 512
                        Sc.dma("sync", ch_w[4 + q4], DMA(wdnr[:, q4 * 4:(q4 + 1) * 4, :], wsc["dn"][r0_:r0_ + 512, :].rearrange("(c p) n -> p c n", p=128)),
                               reads=wscb["dn"], writes=[wdnb[q4]])
                    hq = Rot([(banks[0], P.bank[0]), (banks[1], P.bank[1])])
                    pendF = [None]
                    if hp_ == 0:
                        cpart = [(sbuf(ph, "cpart%d" % i, [128, D], F32), Buf()) for i in range(2)]
                    else:
                        wpg = sbuf(ph, "wpg", [128, 8, 1024], BF16)
                        wpgb = Buf("wpg")
                        wpl = sbuf(ph, "wpl", [128, 2, 1024], BF16)
                        wplb = Buf("wpl")
                        ptile = [(sbuf(ph, "ptile%d" % i, [128, 2, 512], BF16), Buf()) for i in range(1)]
                        gbc = sbuf(ph, "g2bc", [128, D], F32)
                        bbc = sbuf(ph, "b2bc", [128, D], F32)
                        x1 = [(sbuf(ph, "x1_%d" % i, [128, D], F32), Buf()) for i in range(2)]
                        c1 = [(sbuf(ph, "c1_%d" % i, [128, D], F32), Buf()) for i in range(1)]
                        sgt = Rot([(sbuf(ph, "sgF%d" % i, [128, 512], F32), Buf()) for i in range(2)])
                        xn = [(sbuf(ph, "xnF%d" % i, [128, D], F32), Buf()) for i in range(1)]
                        xnb = [(sbuf(ph, "xnbF%d" % i, [128, D], BF16), Buf()) for i in range(2)]
                        stats = sbuf(ph, "statsF", [128, 12], F32)
                        mv = sbuf(ph, "mvF", [128, 2], F32)
                        lnb = Buf("lnF")
                        Sc.dma("sync", ch_w[8], DMA(wpg[:], wsc["pg"].rearrange("(c p) n -> p c n", p=128)), reads=wscb["pg"], writes=[wpgb])
                        Sc.dma("sync", ch_w[9], DMA(wpl[:], wsc["ple"].rearrange("(c p) n -> p c n", p=128)), reads=wscb["ple"], writes=[wplb])
                        Sc.dma("sync", ch_misc[0], DMA(gbc[:], ln2g_d[l, :].partition_broadcast(128)), writes=[P.const])
                        Sc.dma("sync", ch_misc[1], DMA(bbc[:], ln2b_d[l, :].partition_broadcast(128)), writes=[P.const])
                    for tg in range(G):
                        if hp_ == 1:
                            pt_, ptb = ptile[0]
                            Sc.dma("gpsimd", ch_w[10], DMA(pt_[:], pT_d[l, :, tg * 512:(tg + 1) * 512].rearrange("(c p) n -> p c n", p=128)), writes=[ptb])
                        for hc in range(16):
                            hb_, hbb = hq.next()
                            for kc in range(8):
                                Sc.op("tensor", MM(hb_[:], wupr[:, kc, hc * 128:(hc + 1) * 128], xT[:, kc, tg * 512:(tg + 1) * 512], kc == 0, kc == 7),
                                      reads=[wupb[hc // 4], P.xT[tg]], writes=[hbb], signal=(kc == 7))
                            r_, rb = r32.next()
                            Sc.op("scalar", ACT(r_[:], hb_[:], AF.Relu), reads=[hbb], writes=[rb])
                            Sc.op("vector", STT(hidT[:, hc, :], hb_[:], 0.0, r_[:], ALU.max, ALU.mult), reads=[hbb, rb], writes=[hidb])
                        for tt_ in range(4):
                            t = tg * 4 + tt_
                            cb2 = [(banks[2], P.bank[2]), (banks[3], P.bank[3])] if tt_ % 2 == 0 else [(banks[4], P.bank[4]), (banks[5], P.bank[5])]
                            for half in range(2):
                                cs = slice(half * 512, (half + 1) * 512)
                                cb, cbb = cb2[half]
                                for hc in range(16):
                                    Sc.op("tensor", MM(cb[:], hidT[:, hc, tt_ * 128:(tt_ + 1) * 128], wdnr[:, hc, cs], hc == 0, hc == 15),
                                          reads=[hidb, wdnb[hc // 4]], writes=[cbb], signal=(hc == 15))
                            if hp_ == 0:
                                cp_, cpb = cpart[t % 2]
                                Sc.op("scalar", ACT(cp_[:, 0:512], cb2[0][0][:], AF.Copy), reads=[cb2[0][1]], writes=[cpb])
                                Sc.op("vector", CP(cp_[:, 512:1024], cb2[1][0][:]), reads=[cb2[1][1]], writes=[cpb])
                                Sc.dma("sync", ch_res1, DMA(c1_d[t * 128:(t + 1) * 128, :], cp_[:]), reads=[cpb], writes=[P.c1[t]])
                            else:
                                h_, hb = x1[t % 2]
                                c1t, c1b = c1[0]
                                Sc.dma("sync", ch_misc[2 + t % 2], DMA(h_[:], res1_d[t * 128:(t + 1) * 128, :]), reads=[P.res1[t]], writes=[hb])
                                Sc.dma("sync", ch_misc[4], DMA(c1t[:], c1_d[t * 128:(t + 1) * 128, :]), reads=[P.c1[t]], writes=[c1b])
                                for half in range(2):
                                    cs = slice(half * 512, (half + 1) * 512)
                                    cb, cbb = cb2[half]
                                    gbk, gbkb = (banks[6], P.bank[6])
                                    for kc in range(8):
                                        Sc.op("tensor", MM(gbk[:], xT[:, kc, t * 128:(t + 1) * 128], wpg[:, kc, cs], kc == 0, kc == 7),
                                              reads=[wpgb, P.xT[tg]], writes=[gbkb], signal=(kc == 7))
                                    pbk, pbkb = (banks[0], P.bank[0]) if half == 0 else (banks[1], P.bank[1])
                                    for c2 in range(2):
                                        Sc.op("tensor", MM(pbk[:], pt_[:, c2, tt_ * 128:(tt_ + 1) * 128], wpl[:, c2, cs], c2 == 0, c2 == 1),
                                              reads=[wplb, ptb], writes=[pbkb], signal=(c2 == 1))
                                    s_, sb2 = sgt.next()
                                    Sc.op("scalar", ACT(s_[:], gbk[:], AF.Sigmoid), reads=[gbkb], writes=[sb2])
                                    Sc.op("vector", TT(s_[:], s_[:], pbk[:], ALU.mult), reads=[sb2, pbkb], writes=[sb2])
                                    Sc.op("vector", STT(h_[:, cs], h_[:, cs], ALPHA, s_[:], ALU.mult, ALU.add), reads=[hb, sb2], writes=[hb])
                                    Sc.op("vector", TT(h_[:, cs], h_[:, cs], cb[:], ALU.add), reads=[hb, cbb], writes=[hb])
                                Sc.op("gpsimd", TT(h_[:], h_[:], c1t[:], ALU.add), reads=[hb, c1b], writes=[hb])
                                if pendF[0] is not None:
                                    pendF[0]()
                                pendF[0] = ln_tail(ph, h_, hb, gbc, bbc, lnb, t, out_d, P.outb[t], ch_out, stats, mv, xn[0][0], xnb[t % 2][0], xn[0][1], xnb[t % 2][1])
                    if pendF[0] is not None:
                        pendF[0]()
                    end_phase()

        Sc.wait_all("sync", [(ch_out, ch_out.val)])
        if dbg:
            Sc.wait_all("sync", [(ch_res1, ch_res1.val), (ch_ot, ch_ot.val)])
        with nc.Block() as block:
            Sc.emit(block)
    return nc


def _bucket(d):
    d = np.maximum(d, 0).astype(np.int64)
    dm = np.maximum(d, 1).astype(np.float32)
    lr = np.log(dm / np.float32(16.0)) / np.float32(math.log(2048 / 16))
    large = 16 + (lr * np.float32(16.0)).astype(np.int32)
    return np.where(d < 16, d, np.minimum(large, 31)).astype(np.int64)


def bias_tables(rel_bias, S):
    WA = S + PADA
    p = np.arange(128)[:, None]
    j = np.arange(WA)[None, :]
    bA = _bucket(j - PADA - p)
    biasA = np.ascontiguousarray(rel_bias[bA][:, :, 0:4].transpose(2, 0, 1)).astype(np.float32)
    f = np.arange(256)[None, :]
    step = np.clip(f - p, 0, 128)
    biasD = np.zeros((12, 128, 256), np.float32)
    for gi, (_, dl) in enumerate(DIL):
        bD = _bucket(step * dl)
        for hs in range(4):
            biasD[gi * 4 + hs] = rel_bias[bD, 4 + gi * 4 + hs]
    return biasA, biasD


def lam_consts(layers):
    out = np.zeros((len(layers), 128, 2), np.float32)
    for i, l in enumerate(layers):
        li = 0.8 - 0.6 * math.exp(-0.3 * l)
        out[i, :, 0] = li
        out[i, :, 1] = 1.0 - li
    return out


_WNAMES = ["w_in", "da_lambda", "da_norm", "w_branch_da", "w_branch_sb", "w_branch_dil", "w_out", "ln1_g", "ln1_b",
           "w_up", "w_down", "w_ple_gate", "w_ple", "ln2_g", "ln2_b"]
_PROG = {}


def get_program(S, NL):
    key = (S, NL)
    if key not in _PROG:
        _PROG[key] = build_program(S, NL)
    return _PROG[key]


def make_in_maps(inputs, xs, layers, S):
    f32 = lambda a: np.ascontiguousarray(np.asarray(a, dtype=np.float32))
    biasA, biasD = bias_tables(f32(inputs["rel_bias"]), S)
    lamc = lam_consts(layers)
    shared = {}
    for n in _WNAMES:
        a = f32(inputs[n])[layers]
        if n == "da_lambda":
            a = a.reshape(len(layers), 256)
        shared[n] = np.ascontiguousarray(a)
    shared["biasA"] = biasA
    shared["biasD"] = biasD
    shared["lamc"] = lamc
    p = inputs["p"]
    maps = []
    for b in range(len(xs)):
        m = dict(shared)
        m["x"] = f32(xs[b])
        m["pT"] = np.ascontiguousarray(np.asarray(p[layers, b], dtype=np.float32).transpose(0, 2, 1))
        maps.append(m)
    return maps


FUSED = True


def kernel(**inputs):
    x = np.asarray(inputs["x"], dtype=np.float32)
    B, S, _ = x.shape
    NLT = inputs["w_in"].shape[0]
    xs = [x[b] for b in range(B)]
    if FUSED:
        nc = get_program(S, NLT)
        maps = make_in_maps(inputs, xs, list(range(NLT)), S)
        res = run_bass_kernel_spmd(nc, maps, core_ids=list(range(B)))
        xs = [np.asarray(r["out"]) for r in res.results]
    else:
        nc = get_program(S, 1)
        for l in range(NLT):
            maps = make_in_maps(inputs, xs, [l], S)
            res = run_bass_kernel_spmd(nc, maps, core_ids=list(range(B)))
            xs = [np.asarray(r["out"]) for r in res.results]
    return np.stack(xs, axis=0).astype(np.float32)
```

```python
import math
import numpy as np
from contextlib import ExitStack
import concourse.bass as bass
import concourse.mybir as mybir
from concourse.bass_utils import run_bass_kernel_spmd

F32 = mybir.dt.float32
BF16 = mybir.dt.bfloat16
AF = mybir.ActivationFunctionType
ALU = mybir.AluOpType
AX = mybir.AxisListType

D = 1024
DFF = 4096
PLE = 256
INC = 10752
A_Q, A_K, A_V = 0, 512, 1024
B_Q, B_K, B_V = 1536, 2048, 2560
C_Q, C_K, C_V = 3072, 4608, 6144
GATE = 7680
DIL = ((128, 1), (512, 4), (2048, 16))
DEPTH = 4
ALPHA = (2 * DEPTH) ** 0.25
LN_EPS = 1e-5
RMS_EPS = 1e-5
PADA = 384


class Buf:
    __slots__ = ("name", "w", "r")

    def __init__(self, name=""):
        self.name = name
        self.w = None
        self.r = {}


class Src:
    def __init__(self, name, sem, step):
        self.name = name
        self.sem = sem
        self.val = 0
        self.step = step


class Sched:
    ENGS = ("tensor", "vector", "scalar", "gpsimd", "sync")

    def __init__(self, nc, stack):
        self.nc = nc
        self.stack = stack
        self.ops = {e: [] for e in self.ENGS}
        self.src = {}
        self.seen = {e: {} for e in self.ENGS}
        self.chans = []
        for e in self.ENGS:
            self.src[e] = Src(e, stack.enter_context(nc.semaphore("s_" + e)), 1)
        self.nops = 0
        self.nroll = 0

    def chan(self, name):
        c = Src(name, self.stack.enter_context(self.nc.semaphore("c_" + name)), 16)
        self.chans.append(c)
        return c

    def _need(self, eng, deps):
        best = {}
        for s, v in deps:
            if best.get(s, 0) < v:
                best[s] = v
        for s, v in best.items():
            if self.seen[eng].get(s, 0) >= v:
                continue
            self.seen[eng][s] = v
            self.ops[eng].append(("wait", s.sem, v))

    @staticmethod
    def _deps(reads, writes):
        deps = []
        for b in reads:
            if b.w is not None:
                deps.append(b.w)
        for b in writes:
            if b.w is not None:
                deps.append(b.w)
            for rs, rv in b.r.items():
                deps.append((rs, rv))
        return deps

    LIMIT = 30000

    def _roll(self, s):
        if s.val < self.LIMIT:
            return s
        self.nroll += 1
        n = Src(s.name, self.stack.enter_context(self.nc.semaphore("r%d_%s" % (self.nroll, s.name))), s.step)
        return n

    def op(self, eng, fn, reads=(), writes=(), signal=True):
        s = self._roll(self.src[eng])
        self.src[eng] = s
        deps = self._deps(reads, writes)
        if eng == "tensor":
            deps = [d for d in deps if d[0] is not s]
        else:
            deps = [d for d in deps if not (d[0] is s and d[1] > s.val)]
        self._need(eng, deps)
        if signal:
            s.val += 1
            tag = (s, s.val)
            self.ops[eng].append(("op", fn, s.sem, 1))
        else:
            tag = (s, s.val + 1)
            self.ops[eng].append(("op", fn, None, 0))
        self.nops += 1
        for b in reads:
            if b.r.get(tag[0], 0) < tag[1]:
                b.r[tag[0]] = tag[1]
        for b in writes:
            b.w = tag
            b.r = {}
        return tag

    def dma(self, eng, ch, fn, reads=(), writes=()):
        deps = self._deps(reads, writes)
        self._need(eng, deps)
        ch.val += 16
        tag = (ch, ch.val)
        self.ops[eng].append(("op", fn, ch.sem, 16))
        self.nops += 1
        for b in reads:
            if b.r.get(ch, 0) < ch.val:
                b.r[ch] = ch.val
        for b in writes:
            b.w = tag
            b.r = {}
        return tag

    def barrier(self):
        allsrc = [(self.src[e], self.src[e].val) for e in self.ENGS if self.src[e].val > 0]
        allsrc += [(c, c.val) for c in self.chans if c.val > 0]
        for e in self.ENGS:
            self._need(e, [d for d in allsrc if d[0] is not self.src[e]])

    def wait_all(self, eng, tags):
        self._need(eng, tags)

    def prewait(self, eng, reads=(), writes=()):
        s = self.src[eng]
        deps = self._deps(reads, writes)
        if eng == "tensor":
            deps = [d for d in deps if d[0] is not s]
        else:
            deps = [d for d in deps if not (d[0] is s and d[1] > s.val)]
        self._need(eng, deps)

    def emit(self, block):
        def mk(eng):
            lst = self.ops[eng]

            def body(e):
                for it in lst:
                    if it[0] == "wait":
                        e.wait_ge(it[1], it[2])
                    else:
                        ins = it[1](e)
                        if it[2] is not None:
                            ins.then_inc(it[2], it[3])
            return body
        block.tensor(mk("tensor"))
        block.vector(mk("vector"))
        block.scalar(mk("scalar"))
        block.gpsimd(mk("gpsimd"))
        block.sync(mk("sync"))
        self.ops = {e: [] for e in self.ENGS}


def MM(out, lhsT, rhs, start, stop):
    return lambda e: e.matmul(out, lhsT=lhsT, rhs=rhs, start=start, stop=stop)


def TR(out, in_, ident):
    return lambda e: e.transpose(out, in_, ident)


def ACT(out, in_, func, scale=1.0, bias=0.0):
    return lambda e: e.activation(out=out, in_=in_, func=func, bias=bias, scale=scale)


def CP(out, in_):
    return lambda e: e.tensor_copy(out=out, in_=in_)


def TT(out, in0, in1, op):
    return lambda e: e.tensor_tensor(out=out, in0=in0, in1=in1, op=op)


def TS(out, in0, s1, s2, op0, op1=None):
    if op1 is None:
        return lambda e: e.tensor_scalar(out=out, in0=in0, scalar1=s1, scalar2=None, op0=op0)
    return lambda e: e.tensor_scalar(out=out, in0=in0, scalar1=s1, scalar2=s2, op0=op0, op1=op1)


def STT(out, in0, scalar, in1, op0, op1):
    return lambda e: e.scalar_tensor_tensor(out=out, in0=in0, scalar=scalar, in1=in1, op0=op0, op1=op1)


def DMA(out, in_):
    return lambda e: e.dma_start(out=out, in_=in_)


def ASEL(out, in_, pattern, cmp, fill, base, cm):
    return lambda e: e.affine_select(out=out, in_=in_, pattern=pattern, compare_op=cmp, fill=fill,
                                     base=base, channel_multiplier=cm)


class Rot:
    def __init__(self, items):
        self.items = items
        self.i = 0

    def next(self):
        it = self.items[self.i % len(self.items)]
        self.i += 1
        return it


def build_program(S, NL, dbg=False):
    T = S // 128
    G = S // 512
    WA = S + PADA
    nc = bass.Bass("TRN2", target_bir_lowering=False)

    def din(name, shape):
        return nc.dram_tensor(name, list(shape), F32, kind="ExternalInput").ap()

    x_d = din("x", [S, D])
    pT_d = din("pT", [NL, PLE, S])
    w_in_d = din("w_in", [NL, D, INC])
    lam_d = din("da_lambda", [NL, 256])
    dan_d = din("da_norm", [NL, 128])
    wb_d = [din("w_branch_da", [NL, 512, D]), din("w_branch_sb", [NL, 512, D]), din("w_branch_dil", [NL, 512, D])]
    wout_d = din("w_out", [NL, D, D])
    ln1g_d = din("ln1_g", [NL, D])
    ln1b_d = din("ln1_b", [NL, D])
    wup_d = din("w_up", [NL, D, DFF])
    wdn_d = din("w_down", [NL, DFF, D])
    wpg_d = din("w_ple_gate", [NL, D, D])
    wple_d = din("w_ple", [NL, PLE, D])
    ln2g_d = din("ln2_g", [NL, D])
    ln2b_d = din("ln2_b", [NL, D])
    biasA_d = din("biasA", [4, 128, WA])
    biasD_d = din("biasD", [12, 128, 256])
    lamc_d = din("lamc", [NL, 128, 2])
    out_d = nc.dram_tensor("out", [S, D], F32, kind="ExternalOutput").ap()
    okind = "ExternalOutput" if dbg else "Internal"
    res1_d = nc.dram_tensor("res1", [S, D], F32, kind=okind).ap()
    ot_d = nc.dram_tensor("ot", [12, 128, S], BF16, kind=okind).ap()
    ea_d = nc.dram_tensor("ea", [4, 128, WA], BF16, kind="Internal").ap()
    c1_d = nc.dram_tensor("c1", [S, D], F32, kind="Internal").ap()
    wsc = {
        "gate": nc.dram_tensor("wsc_gate", [D, 3072], BF16, kind="Internal").ap(),
        "br0": nc.dram_tensor("wsc_br0", [512, D], BF16, kind="Internal").ap(),
        "br1": nc.dram_tensor("wsc_br1", [512, D], BF16, kind="Internal").ap(),
        "br2": nc.dram_tensor("wsc_br2", [512, D], BF16, kind="Internal").ap(),
        "wo": nc.dram_tensor("wsc_wo", [D, D], BF16, kind="Internal").ap(),
        "up": nc.dram_tensor("wsc_up", [D, DFF], BF16, kind="Internal").ap(),
        "dn": nc.dram_tensor("wsc_dn", [DFF, D], BF16, kind="Internal").ap(),
        "pg": nc.dram_tensor("wsc_pg", [D, D], BF16, kind="Internal").ap(),
        "ple": nc.dram_tensor("wsc_ple", [PLE, D], BF16, kind="Internal").ap(),
    }

    with ExitStack() as st:
        Sc = Sched(nc, st)

        uid = [0]

        def sbuf(stack, name, shape, dt):
            uid[0] += 1
            return stack.enter_context(nc.sbuf_tensor("%s_u%d" % (name, uid[0]), list(shape), dt))

        banks = [st.enter_context(nc.psum_tensor("bank%d" % i, [128, 512], F32)) for i in range(8)]
        xT = sbuf(st, "xT", [128, 8, S], BF16)
        ident = sbuf(st, "ident", [128, 128], BF16)
        ones_bf = sbuf(st, "ones_bf", [128, 128], BF16)
        nones_bf = sbuf(st, "nones_bf", [128, 128], BF16)
        uneg = sbuf(st, "uneg", [128, 128], BF16)
        onesf = sbuf(st, "onesf", [128, 512], F32)

        class P:
            bank = [Buf("bank%d" % i) for i in range(8)]
            xT = [Buf("xT%d" % g) for g in range(G)]
            const = Buf("const")
            ED = Buf("ED")
            ea = [Buf("ea%d" % h) for h in range(4)]
            ot = [[Buf("ot%d_%d" % (i, g)) for g in range(G)] for i in range(12)]
            res1 = [Buf("res1_%d" % t) for t in range(T)]
            outb = [Buf("out_%d" % t) for t in range(T)]
            c1 = [Buf("c1_%d" % t) for t in range(T)]

        ch_out = Sc.chan("out")
        ch_res1 = Sc.chan("res1")
        ch_ot = Sc.chan("ot")
        ch_ea = Sc.chan("ea")
        ch_misc = [Sc.chan("misc%d" % i) for i in range(12)]
        ch_w = [Sc.chan("w%d" % i) for i in range(12)]
        ch_cast = {k: Sc.chan("cast_" + k) for k in wsc}
        wscb = {k: [] for k in wsc}

        def cast_jobs(l):
            jobs = []
            for r in range(0, D, 128):
                jobs.append(("gate", wsc["gate"][r:r + 128, :], w_in_d[l, r:r + 128, GATE:GATE + 3072]))
            for i in range(3):
                for r in range(0, 512, 128):
                    jobs.append(("br%d" % i, wsc["br%d" % i][r:r + 128, :], wb_d[i][l, r:r + 128, :]))
            for r in range(0, D, 128):
                jobs.append(("wo", wsc["wo"][r:r + 128, :], wout_d[l, r:r + 128, :]))
            for r in range(0, D, 128):
                jobs.append(("up", wsc["up"][r:r + 128, :], wup_d[l, r:r + 128, :]))
            for r in range(0, DFF, 512):
                jobs.append(("dn", wsc["dn"][r:r + 512, :], wdn_d[l, r:r + 512, :]))
            for r in range(0, D, 512):
                jobs.append(("pg", wsc["pg"][r:r + 512, :], wpg_d[l, r:r + 512, :]))
            jobs.append(("ple", wsc["ple"][:, :], wple_d[l, :, :]))
            return jobs

        def issue_casts(jobs):
            for (k, dst, src) in jobs:
                b_ = Buf("wsc_" + k)
                wscb[k].append(b_)
                Sc.dma("gpsimd", ch_cast[k], DMA(dst, src), writes=[b_])

        def end_phase(stack_unused=None):
            Sc.barrier()
            with nc.Block() as block:
                Sc.emit(block)

        def evac(i, out, in_, scale, reads, writes):
            if i % 2 == 0:
                Sc.op("scalar", ACT(out, in_, AF.Copy, scale=scale), reads=reads, writes=writes)
            else:
                Sc.op("vector", TS(out, in_, scale, None, ALU.mult), reads=reads, writes=writes)

        pj = Rot([(banks[6], P.bank[6]), (banks[7], P.bank[7])])
        cnt = {"ev": 0}

        def proj_feat(wt, wbuf, c0, dst_fn, dst_bufs, scale):
            for tg in range(G):
                bk, bb = pj.next()
                for kc in range(8):
                    Sc.op("tensor", MM(bk[:], wt[:, kc, c0:c0 + 128], xT[:, kc, tg * 512:(tg + 1) * 512], kc == 0, kc == 7),
                          reads=[wbuf, P.xT[tg]], writes=[bb], signal=(kc == 7))
                o, i_ = dst_fn(tg, bk)
                cnt["ev"] += 1
                evac(cnt["ev"], o, i_, scale, [bb], dst_bufs)

        def proj_tok(wt, wbuf, c0, ncols, vdst, vbuf, tok_ap_fn):
            per = 512 // ncols
            for t0 in range(0, T, per):
                bk, bb = pj.next()
                n = min(per, T - t0)
                for j in range(n):
                    for kc in range(8):
                        Sc.op("tensor", MM(bk[:, j * ncols:(j + 1) * ncols], tok_ap_fn(kc, t0 + j), wt[:, kc, c0:c0 + ncols], kc == 0, kc == 7),
                              reads=[wbuf] + P.xT, writes=[bb], signal=(kc == 7 and j == n - 1))
                cnt["ev"] += 1
                evac(cnt["ev"], vdst[:, t0:t0 + n, :], bk[:, 0:n * ncols].rearrange("p (t c) -> p t c", c=ncols), 1.0, [bb], [vbuf])

        def load_w_cols(stack_, wt, wbuf, ch, l, cols):
            o = 0
            for (c0, n) in cols:
                src = w_in_d[l, :, c0:c0 + n].rearrange("(kc p) c -> p kc c", p=128)
                Sc.dma("gpsimd", ch, DMA(wt[:, :, o:o + n], src), writes=[wbuf])
                o += n

        def transposes_to_xT(xb_ap, xb_buf, t):
            tp = banks[7][:].bitcast(BF16)
            for c in range(8):
                Sc.op("tensor", TR(tp[:, c * 128:(c + 1) * 128], xb_ap[:, c * 128:(c + 1) * 128], ident[:]),
                      reads=[xb_buf, P.const], writes=[P.bank[7]], signal=(c == 7))
            tg = t // 4
            Sc.op("scalar", ACT(xT[:, :, t * 128:(t + 1) * 128], tp.rearrange("p (c n) -> p c n", c=8), AF.Copy),
                  reads=[P.bank[7]], writes=[P.xT[tg]])

        with ExitStack() as ph:
            Sc.op("vector", lambda e: e.memset(onesf[:], 1.0), writes=[P.const])
            Sc.op("vector", CP(ones_bf[:], onesf[:, 0:128]), reads=[P.const], writes=[P.const])
            Sc.op("vector", TS(nones_bf[:], onesf[:, 0:128], -1.0, None, ALU.mult), reads=[P.const], writes=[P.const])
            Sc.op("gpsimd", ASEL(ident[:], ones_bf[:], [[1, 128]], ALU.is_equal, 0.0, 0, -1), reads=[P.const], writes=[P.const])
            Sc.op("gpsimd", ASEL(uneg[:], nones_bf[:], [[-1, 128]], ALU.is_ge, 0.0, 0, 1), reads=[P.const], writes=[P.const])
            CH = 1024
            rawA = [sbuf(ph, "rawA%d" % i, [128, CH], F32) for i in range(2)]
            rawAb = [Buf("rawA%d" % i) for i in range(2)]
            eab = [sbuf(ph, "eab%d" % i, [128, CH], BF16) for i in range(2)]
            eabb = [Buf("eab%d" % i) for i in range(2)]
            k = 0
            for h in range(4):
                for c0 in range(0, WA, CH):
                    n = min(CH, WA - c0)
                    i = k % 2
                    k += 1
                    Sc.dma("sync", ch_misc[1 + i], DMA(rawA[i][:, 0:n], biasA_d[h, :, c0:c0 + n]), writes=[rawAb[i]])
                    Sc.op("scalar", ACT(rawA[i][:, 0:n], rawA[i][:, 0:n], AF.Exp), reads=[rawAb[i]], writes=[rawAb[i]])
                    Sc.op("gpsimd", ASEL(eab[i][:, 0:n], rawA[i][:, 0:n], [[1, n]], ALU.is_ge, 0.0, c0 - PADA, -1),
                          reads=[rawAb[i]], writes=[eabb[i]])
                    Sc.dma("sync", ch_ea, DMA(ea_d[h, :, c0:c0 + n], eab[i][:, 0:n]), reads=[eabb[i]], writes=[P.ea[h]])
            xin = [sbuf(ph, "xin%d" % i, [128, D], F32) for i in range(2)]
            xinb = [Buf("xin%d" % i) for i in range(2)]
            xbf = [sbuf(ph, "xbf%d" % i, [128, D], BF16) for i in range(2)]
            xbfb = [Buf("xbf%d" % i) for i in range(2)]
            for t in range(T):
                i = t % 2
                Sc.dma("sync", ch_misc[3 + i], DMA(xin[i][:], x_d[t * 128:(t + 1) * 128, :]), writes=[xinb[i]])
                Sc.op("vector", CP(xbf[i][:], xin[i][:]), reads=[xinb[i]], writes=[xbfb[i]])
                transposes_to_xT(xbf[i], xbfb[i], t)
            end_phase()

        for l in range(NL):
            res_in = x_d if l == 0 else out_d
            res_in_bufs = None if l == 0 else P.outb

            with ExitStack() as ph:
                qk = [(sbuf(ph, "qA%d" % i, [128, S], BF16), sbuf(ph, "kA%d" % i, [128, S], BF16),
                       sbuf(ph, "vA%d" % i, [128, T, 128], BF16), Buf("qkvA%d" % i)) for i in range(2)]
                wA = [(sbuf(ph, "wA%d" % i, [128, 8, 384], BF16), Buf("wA%d" % i)) for i in range(2)]
                EAt = [(sbuf(ph, "EA%d" % i, [128, WA], BF16), Buf("EA%d" % i)) for i in range(2)]
                praw = Rot([(sbuf(ph, "praw%d" % i, [128, 512], BF16), Buf()) for i in range(6)])
                pTt = Rot([(sbuf(ph, "pT%d" % i, [128, 512], BF16), Buf()) for i in range(8)])
                lamt = sbuf(ph, "lamt", [128, 256], F32)
                lprod = sbuf(ph, "lprod", [128, 128], F32)
                lsum = sbuf(ph, "lsum", [128, 2], F32)
                lamc = sbuf(ph, "lamc", [128, 2], F32)
                nlam = sbuf(ph, "nlam", [128, 1], F32)
                gA = sbuf(ph, "gA", [128, 1], F32)
                lb = Buf("lam")
                r0 = sbuf(ph, "r0", [128, 512], F32)
                r1 = sbuf(ph, "r1", [128, 512], F32)
                t0 = sbuf(ph, "t0", [128, 512], F32)
                t1 = sbuf(ph, "t1", [128, 512], F32)
                oo = sbuf(ph, "oo", [128, 512], F32)
                sq = sbuf(ph, "sq", [128, 512], BF16)
                lnv = sbuf(ph, "lnv", [128, 512], F32)
                postb = Buf("post")
                ostage = Rot([(sbuf(ph, "ostA%d" % i, [128, 512], BF16), Buf()) for i in range(2)])

                Sc.dma("sync", ch_misc[0], DMA(lamt[:], lam_d[l, :].partition_broadcast(128)), writes=[lb])
                Sc.dma("sync", ch_misc[1], DMA(lamc[:], lamc_d[l, :, :]), writes=[lb])
                Sc.dma("sync", ch_misc[2], DMA(gA[:], dan_d[l, :].rearrange("(p o) -> p o", o=1)), writes=[lb])
                l4 = lamt[:].rearrange("p (a b d) -> p a b d", a=2, b=2)
                Sc.op("vector", TT(lprod[:].rearrange("p (a d) -> p a d", a=2), l4[:, :, 0, :], l4[:, :, 1, :], ALU.mult), reads=[lb], writes=[lb])
                Sc.op("vector", lambda e: e.tensor_reduce(out=lsum[:], in_=lprod[:].rearrange("p (a d) -> p a d", a=2), axis=AX.X, op=ALU.add),
                      reads=[lb], writes=[lb])
                Sc.op("scalar", ACT(lsum[:], lsum[:], AF.Exp), reads=[lb], writes=[lb])
                Sc.op("vector", TT(nlam[:], lsum[:, 1:2], lsum[:, 0:1], ALU.subtract), reads=[lb], writes=[lb])
                Sc.op("vector", TT(nlam[:], nlam[:], lamc[:, 0:1], ALU.subtract), reads=[lb], writes=[lb])
                Sc.op("vector", TT(gA[:], gA[:], lamc[:, 1:2], ALU.mult), reads=[lb], writes=[lb])

                def loadA(h):
                    wt, wbuf = wA[h % 2]
                    load_w_cols(ph, wt, wbuf, ch_w[h % 2], l, [(A_Q + h * 128, 128), (A_K + h * 128, 128), (A_V + h * 128, 128)])
                    et, eb = EAt[h % 2]
                    Sc.dma("sync", ch_w[2 + h % 2], DMA(et[:], ea_d[h, :, :]), reads=[P.ea[h]], writes=[eb])

                loadA(0)
                for h in range(4):
                    if h + 1 < 4:
                        loadA(h + 1)
                    wt, wbuf = wA[h % 2]
                    et, eb = EAt[h % 2]
                    qT, kT, vv, qb = qk[h % 2]
                    proj_feat(wt, wbuf, 0, lambda tg, bk: (qT[:, tg * 512:(tg + 1) * 512], bk[:]), [qb], 0.125)
                    proj_feat(wt, wbuf, 128, lambda tg, bk: (kT[:, tg * 512:(tg + 1) * 512], bk[:]), [qb], 1.0)
                    proj_tok(wt, wbuf, 256, 128, vv, qb, lambda kc, t: xT[:, kc, t * 128:(t + 1) * 128])
                    for g in range(G):
                        nk = 4 * g + 4
                        n = 2 * nk
                        Ub = [(banks[2], P.bank[2]), (banks[3], P.bank[3])]
                        Sb_ = [(banks[4], P.bank[4]), (banks[5], P.bank[5])]
                        sbk = [(banks[0], P.bank[0]), (banks[1], P.bank[1]), (banks[6], P.bank[6]), (banks[7], P.bank[7])]
                        pts = {}
                        prs = {}

                        def stage_S(kt):
                            for m in range(2):
                                bk, bb = sbk[(2 * kt + m) % 4]
                                Sc.op("tensor", MM(bk[:], kT[64 * m:64 * m + 64, kt * 128:(kt + 1) * 128],
                                                   qT[64 * m:64 * m + 64, g * 512:(g + 1) * 512], True, True),
                                      reads=[qb], writes=[bb], signal=(m == 1))

                        def stage_E(kt):
                            for m in range(2):
                                bk, bb = sbk[(2 * kt + m) % 4]
                                pr, prb = praw.next()
                                Sc.op("scalar", ACT(pr[:], bk[:], AF.Exp), reads=[bb], writes=[prb])
                                prs[(kt, m)] = (pr, prb)

                        def stage_M(kt):
                            for m in range(2):
                                pr, prb = prs.pop((kt, m))
                                pt, ptb = pTt.next()
                                off = PADA + g * 512 - kt * 128
                                Sc.op("vector", TT(pt[:], pr[:], et[:, off:off + 512], ALU.mult), reads=[prb, eb], writes=[ptb])
                                pts[(kt, m)] = (pt, ptb)

                        def stage_V(kt):
                            for m in range(2):
                                pt, ptb = pts.pop((kt, m))
                                ub, ubb = Ub[m]
                                sb_, sbb = Sb_[m]
                                Sc.op("tensor", MM(ub[:], vv[:, kt, :], pt[:], kt == 0, kt == nk - 1),
                                      reads=[qb, ptb], writes=[ubb], signal=False)
                                Sc.op("tensor", MM(sb_[:], ones_bf[:], pt[:], kt == 0, kt == nk - 1),
                                      reads=[P.const, ptb], writes=[sbb], signal=(m == 1))

                        for it in range(nk + 2):
                            rd = [qb, P.const]
                            wr = []
                            if it < nk:
                                wr += [sbk[(2 * it + m) % 4][1] for m in range(2)]
                            if it >= 2:
                                rd += [pts[(it - 2, m)][1] for m in range(2)]
                                wr += [Ub[0][1], Ub[1][1], Sb_[0][1], Sb_[1][1]]
                            Sc.prewait("tensor", rd, wr)
                            if it < nk:
                                stage_S(it)
                            if it >= 2:
                                stage_V(it - 2)
                            if it < nk:
                                stage_E(it)
                            if 1 <= it <= nk:
                                stage_M(it - 1)
                        Sc.op("scalar", ACT(r0[:], banks[4][:], AF.Ln), reads=[P.bank[4]], writes=[postb])
                        Sc.op("scalar", ACT(r1[:], banks[5][:], AF.Ln), reads=[P.bank[5]], writes=[postb])
                        Sc.op("scalar", ACT(r0[:], r0[:], AF.Exp, scale=-1.0), reads=[postb], writes=[postb])
                        Sc.op("scalar", ACT(r1[:], r1[:], AF.Exp, scale=-1.0), reads=[postb], writes=[postb])
                        Sc.op("vector", TT(t0[:], banks[2][:], r0[:], ALU.mult), reads=[P.bank[2], postb], writes=[postb])
                        Sc.op("vector", TT(t1[:], banks[3][:], r1[:], ALU.mult), reads=[P.bank[3], postb], writes=[postb])
                        Sc.op("vector", STT(oo[:], t1[:], nlam[:, 0:1], t0[:], ALU.mult, ALU.add), reads=[postb, lb], writes=[postb])
                        Sc.op("scalar", ACT(sq[:], oo[:], AF.Square), reads=[postb], writes=[postb])
                        Sc.op("tensor", MM(banks[0][:], ones_bf[:], sq[:], True, True), reads=[postb, P.const], writes=[P.bank[0]])
                        Sc.op("scalar", ACT(lnv[:], banks[0][:], AF.Ln, scale=1.0 / 128.0, bias=RMS_EPS), reads=[P.bank[0]], writes=[postb])
                        Sc.op("scalar", ACT(lnv[:], lnv[:], AF.Exp, scale=-0.5), reads=[postb], writes=[postb])
                        os_, osb = ostage.next()
                        Sc.op("vector", STT(os_[:], oo[:], gA[:, 0:1], lnv[:], ALU.mult, ALU.mult), reads=[postb, lb], writes=[osb])
                        Sc.dma("sync", ch_ot, DMA(ot_d[h, :, g * 512:(g + 1) * 512], os_[:]), reads=[osb], writes=[P.ot[h][g]])
                end_phase()

            with ExitStack() as ph:
                qk = [(sbuf(ph, "qB%d" % i, [128, S], BF16), sbuf(ph, "kB%d" % i, [128, S], BF16),
                       sbuf(ph, "vB%d" % i, [128, T, 128], BF16), Buf("qkvB%d" % i)) for i in range(2)]
                wB = [(sbuf(ph, "wB%d" % i, [128, 8, 384], BF16), Buf("wB%d" % i)) for i in range(2)]
                e32 = Rot([(sbuf(ph, "e32_%d" % i, [128, 512], F32), Buf()) for i in range(3)])
                spt = Rot([(sbuf(ph, "sp%d" % i, [128, 512], BF16), Buf()) for i in range(5)])
                tmpt = Rot([(sbuf(ph, "tmpB%d" % i, [128, 512], F32), Buf()) for i in range(3)])
                at = Rot([(sbuf(ph, "aB%d" % i, [128, 512], BF16), Buf()) for i in range(6)])
                csb = [(sbuf(ph, "csb%d" % i, [128, 512], F32), Buf()) for i in range(2)]
                zct = Rot([(sbuf(ph, "zcB%d" % i, [128, 512], F32), Buf()) for i in range(3)])
                ostage = Rot([(sbuf(ph, "ostB%d" % i, [128, 512], BF16), Buf()) for i in range(2)])
                maskB = sbuf(ph, "maskB", [128, 4, 512], BF16)
                mkb = Buf("maskB")
                for dd in range(4):
                    Sc.op("gpsimd", ASEL(maskB[:, dd, :], onesf[:], [[1, 512]], ALU.is_gt, 0.0, -128 * dd, -1),
                          reads=[P.const], writes=[mkb])

                def loadB(hp):
                    wt, wbuf = wB[hp % 2]
                    load_w_cols(ph, wt, wbuf, ch_w[hp % 2], l, [(B_Q + hp * 128, 128), (B_K + hp * 128, 128), (B_V + hp * 128, 128)])

                loadB(0)
                for k_ in wscb:
                    wscb[k_] = []
                cjobs = cast_jobs(l)
                cper = (len(cjobs) + 3) // 4
                for hp in range(4):
                    if hp + 1 < 4:
                        loadB(hp + 1)
                    issue_casts(cjobs[hp * cper:(hp + 1) * cper])
                    wt, wbuf = wB[hp % 2]
                    qT, kT, vv, qb = qk[hp % 2]
                    proj_feat(wt, wbuf, 0, lambda tg, bk: (qT[:, tg * 512:(tg + 1) * 512], bk[:]), [qb], 0.125)
                    proj_feat(wt, wbuf, 128, lambda tg, bk: (kT[:, tg * 512:(tg + 1) * 512], bk[:]), [qb], 1.0)
                    proj_tok(wt, wbuf, 256, 128, vv, qb, lambda kc, t: xT[:, kc, t * 128:(t + 1) * 128])
                    for g in range(G):
                        n = 4 * g + 4
                        for hh in range(2):
                            rs = slice(64 * hh, 64 * hh + 64)
                            zb = [(banks[0], P.bank[0]), (banks[1], P.bank[1]), (banks[6], P.bank[6])]
                            za = [(banks[2], P.bank[2]), (banks[3], P.bank[3])]
                            cbk = [(banks[4], P.bank[4]), (banks[7], P.bank[7])]
                            NFILL = 0
                            st_zc = {}
                            st_e1 = {}
                            st_e = {}
                            st_sp = {}
                            st_a = {}

                            def kt_of(i):
                                return 4 * g + 3 - i

                            def stZ1_pe(i):
                                kt = kt_of(i)
                                bk, bb = zb[i % 3]
                                Sc.op("tensor", MM(bk[:], kT[rs, kt * 128:(kt + 1) * 128], qT[rs, g * 512:(g + 1) * 512], True, True),
                                      reads=[qb], writes=[bb])

                            def stZ1_act(i):
                                bk, bb = zb[i % 3]
                                e_, eb_ = e32.next()
                                Sc.op("scalar", ACT(e_[:], bk[:], AF.Exp), reads=[bb], writes=[eb_])
                                st_e[i] = (e_, eb_)
                                st_e1[i] = eb_

                            def stZ2(i):
                                e_, eb_ = st_e.pop(i)
                                sp_, spb = spt.next()
                                Sc.op("scalar", ACT(sp_[:], e_[:], AF.Ln, bias=1.0), reads=[eb_], writes=[spb])
                                if i <= 3:
                                    Sc.op("gpsimd", TT(sp_[:], sp_[:], maskB[:, 3 - i, :], ALU.mult), reads=[spb, mkb], writes=[spb])
                                st_sp[i] = (sp_, spb)

                            def stA_pe(i):
                                sp_, spb = st_sp[i]
                                kt = kt_of(i)
                                bk, bb = za[i % 2]
                                Sc.op("tensor", MM(bk[:], uneg[:], sp_[:], True, True), reads=[spb, P.const], writes=[bb])
                                if i < n - 1:
                                    cb_, cbb_ = cbk[i % 2]
                                    Sc.op("tensor", MM(cb_[:], nones_bf[:], sp_[:], True, True), reads=[spb, P.const], writes=[cbb_])

                            def stA_dve(i):
                                st_sp.pop(i)
                                if i < n - 1:
                                    cb_, cbb_ = cbk[i % 2]
                                    cs, csbuf = csb[i % 2]
                                    if i == 0:
                                        Sc.op("vector", CP(cs[:], cb_[:]), reads=[cbb_], writes=[csbuf])
                                    else:
                                        pc, pcb = csb[(i - 1) % 2]
                                        Sc.op("vector", TT(cs[:], cb_[:], pc[:], ALU.add), reads=[cbb_, pcb], writes=[csbuf])

                            def stZC(i):
                                zbk, zbb = zb[i % 3]
                                zc_, zcb = zct.next()
                                e1b = st_e1.pop(i)
                                if i == 0:
                                    Sc.op("vector", CP(zc_[:], zbk[:]), reads=[zbb, e1b], writes=[zcb])
                                else:
                                    pc, pcb = csb[(i - 1) % 2]
                                    Sc.op("vector", TT(zc_[:], zbk[:], pc[:], ALU.add), reads=[zbb, pcb, e1b], writes=[zcb])
                                st_zc[i] = (zc_, zcb)

                            def stD(i):
                                bk, bb = za[i % 2]
                                a_, ab_ = at.next()
                                zc_, zcb = st_zc.pop(i)
                                tm, tmb = tmpt.next()
                                Sc.op("vector", TT(tm[:], bk[:], zc_[:], ALU.add), reads=[bb, zcb], writes=[tmb])
                                Sc.op("scalar", ACT(a_[:], tm[:], AF.Exp), reads=[tmb], writes=[ab_])
                                if i <= 3:
                                    Sc.op("gpsimd", TT(a_[:], a_[:], maskB[:, 3 - i, :], ALU.mult), reads=[ab_, mkb], writes=[ab_])
                                st_a[i] = (a_, ab_)

                            def stV(i):
                                kt = kt_of(i)
                                a_, ab_ = st_a.pop(i)
                                Sc.op("tensor", MM(banks[5][rs, :], vv[:, kt, rs], a_[:], i == 0, i == n - 1),
                                      reads=[qb, ab_], writes=[P.bank[5]], signal=(i == n - 1))

                            for it in range(n + 4):
                                rd = [qb, P.const]
                                wr = []
                                if it < n:
                                    wr.append(zb[it % 3][1])
                                if 0 <= it - 2 < n:
                                    rd.append(st_sp[it - 2][1])
                                    wr.append(za[(it - 2) % 2][1])
                                    if it - 2 < n - 1:
                                        wr.append(cbk[(it - 2) % 2][1])
                                if 0 <= it - 4 < n:
                                    rd.append(st_a[it - 4][1])
                                    wr.append(P.bank[5])
                                Sc.prewait("tensor", rd, wr)
                                if it < n:
                                    stZ1_pe(it)
                                if 0 <= it - 2 < n:
                                    stA_pe(it - 2)
                                if 0 <= it - 4 < n:
                                    stV(it - 4)
                                if it < n:
                                    for _f in range(NFILL):
                                        Sc.op("tensor", MM(banks[6][:], ones_bf[:], qT[:, g * 512:(g + 1) * 512], True, True),
                                              reads=[qb, P.const], writes=[P.bank[6]], signal=False)
                                if it < n:
                                    stZ1_act(it)
                                if 0 <= it - 2 < n:
                                    stD(it - 2)
                                if it < n:
                                    stZ2(it)
                                if 0 <= it - 2 < n:
                                    stA_dve(it - 2)
                                if 0 <= it - 1 < n:
                                    stZC(it - 1)
                        os_, osb = ostage.next()
                        Sc.op("scalar", ACT(os_[:], banks[5][:], AF.Copy), reads=[P.bank[5]], writes=[osb])
                        Sc.dma("sync", ch_ot, DMA(ot_d[4 + hp, :, g * 512:(g + 1) * 512], os_[:]), reads=[osb], writes=[P.ot[4 + hp][g]])
                end_phase()

            with ExitStack() as ph:
                qk = [(sbuf(ph, "qC%d" % i, [128, S], BF16), sbuf(ph, "kC%d" % i, [128, S], BF16),
                       sbuf(ph, "vC%d" % i, [128, T, 128], BF16), Buf("qkvC%d" % i)) for i in range(2)]
                wC = [(sbuf(ph, "wC%d" % i, [128, 8, 384], BF16), Buf("wC%d" % i)) for i in range(2)]
                praw = Rot([(sbuf(ph, "prawC%d" % i, [128, 256], F32), Buf()) for i in range(3)])
                pTt = Rot([(sbuf(ph, "pTC%d" % i, [128, 256], BF16), Buf()) for i in range(3)])
                Uacc = sbuf(ph, "Uacc", [128, S], F32)
                Sacc = sbuf(ph, "Sacc", [128, S], F32)
                accb = Buf("acc")
                rc = sbuf(ph, "rcC", [128, 512], F32)
                rcb = Buf("rc")
                ostage = Rot([(sbuf(ph, "ostC%d" % i, [128, 512], BF16), Buf()) for i in range(2)])
                scale_c = 128.0 ** -0.5
                ED = sbuf(ph, "ED", [128, 12, 256], BF16)
                rawD = sbuf(ph, "rawD", [128, 12, 256], F32)
                rawDb = Buf("rawD")
                Sc.dma("sync", ch_misc[0], DMA(rawD[:], biasD_d.rearrange("h p f -> p h f")), writes=[rawDb])
                Sc.op("scalar", ACT(rawD[:], rawD[:], AF.Exp), reads=[rawDb], writes=[rawDb])
                for hh in range(12):
                    Sc.op("gpsimd", ASEL(rawD[:, hh, :], rawD[:, hh, :], [[1, 256]], ALU.is_ge, 0.0, 0, -1), reads=[rawDb], writes=[rawDb])
                    Sc.op("gpsimd", ASEL(ED[:, hh, :], rawD[:, hh, :], [[-1, 256]], ALU.is_ge, 0.0, 128, 1), reads=[rawDb], writes=[P.ED])

                def loadC(idx):
                    gi, hs = idx % 3, idx // 3
                    hd = gi * 4 + hs
                    wt, wbuf = wC[idx % 2]
                    load_w_cols(ph, wt, wbuf, ch_w[idx % 2], l, [(C_Q + hd * 128, 128), (C_K + hd * 128, 128), (C_V + hd * 128, 128)])

                loadC(0)
                for idx in range(12):
                    if idx + 1 < 12:
                        loadC(idx + 1)
                    gi, hs = idx % 3, idx // 3
                    hd = gi * 4 + hs
                    dl = DIL[gi][1]
                    L = S // dl
                    nblk = L // 128
                    wt, wbuf = wC[idx % 2]
                    qT, kT, vv, qb = qk[idx % 2]

                    def dstq(tg, bk, dst=None):
                        if dl == 1:
                            return dst[:, tg * 512:(tg + 1) * 512], bk[:]
                        m0 = tg * (512 // dl)
                        return (dst[:].rearrange("p (b m) -> p b m", b=dl)[:, :, m0:m0 + 512 // dl],
                                bk[:].rearrange("p (a b) -> p b a", b=dl))

                    proj_feat(wt, wbuf, 0, lambda tg, bk: dstq(tg, bk, qT), [qb], scale_c)
                    proj_feat(wt, wbuf, 128, lambda tg, bk: dstq(tg, bk, kT), [qb], 1.0)

                    def tokap(kc, pi):
                        r, j = pi // nblk, pi % nblk
                        s0 = r + dl * 128 * j
                        return xT[:, kc, s0:s0 + dl * 127 + 1:dl]

                    proj_tok(wt, wbuf, 256, 128, vv, qb, tokap)
                    ub = [(banks[2], P.bank[2]), (banks[3], P.bank[3])]
                    sb_ = [(banks[4], P.bank[4]), (banks[5], P.bank[5])]
                    sbk = [(banks[0], P.bank[0]), (banks[1], P.bank[1])]
                    pts = {}

                    def cS(pi):
                        r, j = pi // nblk, pi % nblk
                        ncol = 256 if j < nblk - 1 else 128
                        bk, bb = sbk[pi % 2]
                        Sc.op("tensor", MM(bk[:, 0:ncol], kT[:, pi * 128:(pi + 1) * 128], qT[:, pi * 128:pi * 128 + ncol], True, True),
                              reads=[qb], writes=[bb])
                        pr, prb = praw.next()
                        Sc.op("scalar", ACT(pr[:, 0:ncol], bk[:, 0:ncol], AF.Exp), reads=[bb], writes=[prb])
                        pt, ptb = pTt.next()
                        eng = "vector" if pi % 2 == 0 else "gpsimd"
                        Sc.op(eng, TT(pt[:, 0:ncol], pr[:, 0:ncol], ED[:, hd, 0:ncol], ALU.mult), reads=[prb, P.ED], writes=[ptb])
                        pts[pi] = (pt, ptb)

                    def cV(pi):
                        r, j = pi // nblk, pi % nblk
                        pt, ptb = pts.pop(pi)
                        u, ubb = ub[(pi // 4) % 2]
                        s_, sbb = sb_[(pi // 4) % 2]
                        c = pi % 4
                        Sc.op("tensor", MM(u[:, c * 128:(c + 1) * 128], vv[:, pi, :], pt[:, 0:128], j == 0, True),
                              reads=[qb, ptb], writes=[ubb], signal=False)
                        Sc.op("tensor", MM(s_[:, c * 128:(c + 1) * 128], ones_bf[:], pt[:, 0:128], j == 0, True),
                              reads=[P.const, ptb], writes=[sbb], signal=True)
                        if c == 3 or pi == T - 1:
                            flush(pi // 4)
                        if j < nblk - 1:
                            u2, ubb2 = ub[((pi + 1) // 4) % 2]
                            s2, sbb2 = sb_[((pi + 1) // 4) % 2]
                            c2 = (pi + 1) % 4
                            Sc.op("tensor", MM(u2[:, c2 * 128:(c2 + 1) * 128], vv[:, pi, :], pt[:, 128:256], True, False),
                                  reads=[qb, ptb], writes=[ubb2], signal=False)
                            Sc.op("tensor", MM(s2[:, c2 * 128:(c2 + 1) * 128], ones_bf[:], pt[:, 128:256], True, False),
                                  reads=[P.const, ptb], writes=[sbb2], signal=False)

                    def flush(bi):
                        u, ubb = ub[bi % 2]
                        s_, sbb = sb_[bi % 2]
                        for c in range(4):
                            pi = bi * 4 + c
                            if pi >= T:
                                break
                            r, j = pi // nblk, pi % nblk
                            s0 = r + dl * 128 * j
                            dU = Uacc[:, s0:s0 + dl * 127 + 1:dl]
                            dS = Sacc[:, s0:s0 + dl * 127 + 1:dl]
                            if gi == 0:
                                Sc.op("vector", CP(dU, u[:, c * 128:(c + 1) * 128]), reads=[ubb], writes=[accb])
                                Sc.op("vector", CP(dS, s_[:, c * 128:(c + 1) * 128]), reads=[sbb], writes=[accb])
                            else:
                                Sc.op("vector", TT(dU, u[:, c * 128:(c + 1) * 128], dU, ALU.add), reads=[ubb, accb], writes=[accb])
                                Sc.op("vector", TT(dS, s_[:, c * 128:(c + 1) * 128], dS, ALU.add), reads=[sbb, accb], writes=[accb])

                    for step in range(T + 1):
                        if step < T:
                            cS(step)
                        if step >= 1:
                            cV(step - 1)
                    if gi == 2:
                        for g in range(G):
                            Sc.op("vector", lambda e, g=g: e.reciprocal(out=rc[:], in_=Sacc[:, g * 512:(g + 1) * 512]), reads=[accb], writes=[rcb])
                            os_, osb = ostage.next()
                            Sc.op("vector", TT(os_[:], Uacc[:, g * 512:(g + 1) * 512], rc[:], ALU.mult), reads=[accb, rcb], writes=[osb])
                            Sc.dma("sync", ch_ot, DMA(ot_d[8 + hs, :, g * 512:(g + 1) * 512], os_[:]), reads=[osb], writes=[P.ot[8 + hs][g]])
                end_phase()

            def ln_tail(ph_t, hh, hb, gbc, bbc, lnb, t, dst_d, dst_buf, ch_dst, stats, mv, xn, xnb_, xnf_b, xnb_b):
                Sc.op("vector", lambda e: e.bn_stats(out=stats[:, 0:6], in_=hh[:, 0:512]), reads=[hb], writes=[lnb])
                Sc.op("vector", lambda e: e.bn_stats(out=stats[:, 6:12], in_=hh[:, 512:1024]), reads=[hb], writes=[lnb])
                Sc.op("vector", lambda e: e.bn_aggr(out=mv[:], in_=stats[:]), reads=[lnb], writes=[lnb])
                Sc.op("scalar", ACT(mv[:, 1:2], mv[:, 1:2], AF.Sqrt, bias=LN_EPS), reads=[lnb], writes=[lnb])
                Sc.op("vector", lambda e: e.reciprocal(out=mv[:, 1:2], in_=mv[:, 1:2]), reads=[lnb], writes=[lnb])
                Sc.op("vector", TS(xn[:], hh[:], mv[:, 0:1], mv[:, 1:2], ALU.subtract, ALU.mult), reads=[hb, lnb], writes=[xnf_b])
                Sc.op("gpsimd", TT(xn[:], xn[:], gbc[:], ALU.mult), reads=[xnf_b, P.const], writes=[xnf_b])
                Sc.op("gpsimd", TT(xn[:], xn[:], bbc[:], ALU.add), reads=[xnf_b, P.const], writes=[xnf_b])
                Sc.dma("sync", ch_dst, DMA(dst_d[t * 128:(t + 1) * 128, :], xn[:]), reads=[xnf_b], writes=[dst_buf])
                Sc.op("gpsimd", CP(xnb_[:], xn[:]), reads=[xnf_b], writes=[xnb_b])
                return lambda: transposes_to_xT(xnb_, xnb_b, t)

            with ExitStack() as ph:
                wg = [(sbuf(ph, "wg%d" % i, [128, 8, 512], BF16), Buf("wg%d" % i)) for i in range(2)]
                wbr = [(sbuf(ph, "wbr%d" % i, [128, 4, 1024], BF16), Buf("wbr%d" % i)) for i in range(3)]
                wo = sbuf(ph, "wo", [128, 8, 1024], BF16)
                wob = Buf("wo")
                gbc = sbuf(ph, "g1bc", [128, D], F32)
                bbc = sbuf(ph, "b1bc", [128, D], F32)
                otile = [(sbuf(ph, "otile%d" % i, [128, 12, 512], BF16), Buf("otile%d" % i)) for i in range(1)]
                merged = sbuf(ph, "merged", [128, 8, 512], BF16)
                mergb = Buf("merged")
                sg = Rot([(sbuf(ph, "sgM%d" % i, [128, 512], F32), Buf()) for i in range(2)])
                macc = sbuf(ph, "macc", [128, 4, 512], F32)
                maccb = Buf("macc")
                xres = [(sbuf(ph, "xres%d" % i, [128, D], F32), Buf()) for i in range(2)]
                hh = [(sbuf(ph, "hM%d" % i, [128, D], F32), Buf()) for i in range(2)]
                xn = [(sbuf(ph, "xnM%d" % i, [128, D], F32), Buf()) for i in range(2)]
                xnb = [(sbuf(ph, "xnbM%d" % i, [128, D], BF16), Buf()) for i in range(2)]
                stats = sbuf(ph, "statsM", [128, 12], F32)
                mv = sbuf(ph, "mvM", [128, 2], F32)
                lnb = Buf("lnM")

                for i in range(3):
                    Sc.dma("sync", ch_w[2 + i], DMA(wbr[i][0][:], wsc["br%d" % i].rearrange("(c p) n -> p c n", p=128)), reads=wscb["br%d" % i], writes=[wbr[i][1]])
                Sc.dma("sync", ch_w[5], DMA(wo[:], wsc["wo"].rearrange("(c p) n -> p c n", p=128)), reads=wscb["wo"], writes=[wob])
                Sc.dma("sync", ch_misc[0], DMA(gbc[:], ln1g_d[l, :].partition_broadcast(128)), writes=[P.const])
                Sc.dma("sync", ch_misc[1], DMA(bbc[:], ln1b_d[l, :].partition_broadcast(128)), writes=[P.const])
                gq = Rot([(banks[0], P.bank[0]), (banks[1], P.bank[1])])
                bq = Rot([(banks[2], P.bank[2]), (banks[3], P.bank[3])])
                kk = 0
                pend = [None]
                for tg in range(G):
                    ot_, otb = otile[0]
                    Sc.dma("sync", ch_misc[2 + tg % 2], DMA(ot_[:], ot_d[:, :, tg * 512:(tg + 1) * 512].rearrange("c p n -> p c n")),
                           reads=[P.ot[i][tg] for i in range(12)], writes=[otb])
                    for half2 in range(2):
                        for i in range(3):
                            wgt, wgb = wg[kk % 2]
                            c0 = i * 1024 + half2 * 512
                            Sc.dma("sync", ch_w[kk % 2], DMA(wgt[:], wsc["gate"][:, c0:c0 + 512].rearrange("(c p) n -> p c n", p=128)),
                                   reads=wscb["gate"], writes=[wgb])
                            kk += 1
                            for oc4 in range(4):
                                oc = half2 * 4 + oc4
                                gb_, gbb = gq.next()
                                for kc in range(8):
                                    Sc.op("tensor", MM(gb_[:], wgt[:, kc, oc4 * 128:(oc4 + 1) * 128], xT[:, kc, tg * 512:(tg + 1) * 512], kc == 0, kc == 7),
                                          reads=[wgb, P.xT[tg]], writes=[gbb], signal=(kc == 7))
                                bb_, bbb = bq.next()
                                for c4 in range(4):
                                    Sc.op("tensor", MM(bb_[:], wbr[i][0][:, c4, oc * 128:(oc + 1) * 128], ot_[:, 4 * i + c4, :], c4 == 0, c4 == 3),
                                          reads=[wbr[i][1], otb], writes=[bbb], signal=(c4 == 3))
                                s_, sb2 = sg.next()
                                Sc.op("scalar", ACT(s_[:], gb_[:], AF.Sigmoid), reads=[gbb], writes=[sb2])
                                if i == 0:
                                    Sc.op("vector", TT(macc[:, oc4, :], s_[:], bb_[:], ALU.mult), reads=[sb2, bbb], writes=[maccb])
                                elif i == 1:
                                    Sc.op("vector", TT(s_[:], s_[:], bb_[:], ALU.mult), reads=[sb2, bbb], writes=[sb2])
                                    Sc.op("gpsimd", TT(macc[:, oc4, :], macc[:, oc4, :], s_[:], ALU.add), reads=[sb2, maccb], writes=[maccb])
                                else:
                                    Sc.op("vector", TT(s_[:], s_[:], bb_[:], ALU.mult), reads=[sb2, bbb], writes=[sb2])
                                    Sc.op("gpsimd", TT(merged[:, oc, :], macc[:, oc4, :], s_[:], ALU.add), reads=[sb2, maccb], writes=[mergb])
                    for tt_ in range(4):
                        t = tg * 4 + tt_
                        xr, xrb = xres[t % 2]
                        rd = [] if res_in_bufs is None else [res_in_bufs[t]]
                        Sc.dma("sync", ch_misc[4 + t % 2], DMA(xr[:], res_in[t * 128:(t + 1) * 128, :]), reads=rd, writes=[xrb])
                        h_, hb = hh[t % 2]
                        for half in range(2):
                            yb, ybb = (banks[4], P.bank[4]) if half == 0 else (banks[5], P.bank[5])
                            for oc in range(8):
                                Sc.op("tensor", MM(yb[:], merged[:, oc, tt_ * 128:(tt_ + 1) * 128], wo[:, oc, half * 512:(half + 1) * 512], oc == 0, oc == 7),
                                      reads=[mergb, wob], writes=[ybb], signal=(oc == 7))
                            Sc.op("vector", STT(h_[:, half * 512:(half + 1) * 512], xr[:, half * 512:(half + 1) * 512], ALPHA, yb[:], ALU.mult, ALU.add),
                                  reads=[xrb, ybb], writes=[hb])
                        if pend[0] is not None:
                            pend[0]()
                        pend[0] = ln_tail(ph, h_, hb, gbc, bbc, lnb, t, res1_d, P.res1[t], ch_res1, stats, mv, xn[t % 2][0], xnb[t % 2][0], xn[t % 2][1], xnb[t % 2][1])
                if pend[0] is not None:
                    pend[0]()
                end_phase()

            for hp_ in range(2):
                with ExitStack() as ph:
                    wupr = sbuf(ph, "wupr", [128, 8, 2048], BF16)
                    wupb = [Buf() for _ in range(4)]
                    wdnr = sbuf(ph, "wdnr", [128, 16, 1024], BF16)
                    wdnb = [Buf() for _ in range(4)]
                    hidT = sbuf(ph, "hidT", [128, 16, 512], BF16)
                    hidb = Buf("hidT")
                    r32 = Rot([(sbuf(ph, "r32_%d" % i, [128, 512], F32), Buf()) for i in range(2)])
                    for q4 in range(4):
                        c0 = hp_ * 2048 + q4 * 512
                        Sc.dma("sync", ch_w[q4], DMA(wupr[:, :, q4 * 512:(q4 + 1) * 512], wsc["up"][:, c0:c0 + 512].rearrange("(c p) n -> p c n", p=128)),
                               reads=wscb["up"], writes=[wupb[q4]])
                    for q4 in range(4):
                        r0_ = hp_ * 2048 + q4 * 512
                        Sc.dma("sync", ch_w[4 + q4], DMA(wdnr[:, q4 * 4:(q4 + 1) * 4, :], wsc["dn"][r0_:r0_ + 512, :].rearrange("(c p) n -> p c n", p=128)),
                               reads=wscb["dn"], writes=[wdnb[q4]])
                    hq = Rot([(banks[0], P.bank[0]), (banks[1], P.bank[1])])
                    pendF = [None]
                    if hp_ == 0:
                        cpart = [(sbuf(ph, "cpart%d" % i, [128, D], F32), Buf()) for i in range(2)]
                    else:
                        wpg = sbuf(ph, "wpg", [128, 8, 1024], BF16)
                        wpgb = Buf("wpg")
                        wpl = sbuf(ph, "wpl", [128, 2, 1024], BF16)
                        wplb = Buf("wpl")
                        ptile = [(sbuf(ph, "ptile%d" % i, [128, 2, 512], BF16), Buf()) for i in range(1)]
                        gbc = sbuf(ph, "g2bc", [128, D], F32)
                        bbc = sbuf(ph, "b2bc", [128, D], F32)
                        x1 = [(sbuf(ph, "x1_%d" % i, [128, D], F32), Buf()) for i in range(2)]
                        c1 = [(sbuf(ph, "c1_%d" % i, [128, D], F32), Buf()) for i in range(1)]
                        sgt = Rot([(sbuf(ph, "sgF%d" % i, [128, 512], F32), Buf()) for i in range(2)])
                        xn = [(sbuf(ph, "xnF%d" % i, [128, D], F32), Buf()) for i in range(1)]
                        xnb = [(sbuf(ph, "xnbF%d" % i, [128, D], BF16), Buf()) for i in range(2)]
                        stats = sbuf(ph, "statsF", [128, 12], F32)
                        mv = sbuf(ph, "mvF", [128, 2], F32)
                        lnb = Buf("lnF")
                        Sc.dma("sync", ch_w[8], DMA(wpg[:], wsc["pg"].rearrange("(c p) n -> p c n", p=128)), reads=wscb["pg"], writes=[wpgb])
                        Sc.dma("sync", ch_w[9], DMA(wpl[:], wsc["ple"].rearrange("(c p) n -> p c n", p=128)), reads=wscb["ple"], writes=[wplb])
                        Sc.dma("sync", ch_misc[0], DMA(gbc[:], ln2g_d[l, :].partition_broadcast(128)), writes=[P.const])
                        Sc.dma("sync", ch_misc[1], DMA(bbc[:], ln2b_d[l, :].partition_broadcast(128)), writes=[P.const])
                    for tg in range(G):
                        if hp_ == 1:
                            pt_, ptb = ptile[0]
                            Sc.dma("gpsimd", ch_w[10], DMA(pt_[:], pT_d[l, :, tg * 512:(tg + 1) * 512].rearrange("(c p) n -> p c n", p=128)), writes=[ptb])
                        for hc in range(16):
                            hb_, hbb = hq.next()
                            for kc in range(8):
                                Sc.op("tensor", MM(hb_[:], wupr[:, kc, hc * 128:(hc + 1) * 128], xT[:, kc, tg * 512:(tg + 1) * 512], kc == 0, kc == 7),
                                      reads=[wupb[hc // 4], P.xT[tg]], writes=[hbb], signal=(kc == 7))
                            r_, rb = r32.next()
                            Sc.op("scalar", ACT(r_[:], hb_[:], AF.Relu), reads=[hbb], writes=[rb])
                            Sc.op("vector", STT(hidT[:, hc, :], hb_[:], 0.0, r_[:], ALU.max, ALU.mult), reads=[hbb, rb], writes=[hidb])
                        for tt_ in range(4):
                            t = tg * 4 + tt_
                            cb2 = [(banks[2], P.bank[2]), (banks[3], P.bank[3])] if tt_ % 2 == 0 else [(banks[4], P.bank[4]), (banks[5], P.bank[5])]
                            for half in range(2):
                                cs = slice(half * 512, (half + 1) * 512)
                                cb, cbb = cb2[half]
                                for hc in range(16):
                                    Sc.op("tensor", MM(cb[:], hidT[:, hc, tt_ * 128:(tt_ + 1) * 128], wdnr[:, hc, cs], hc == 0, hc == 15),
                                          reads=[hidb, wdnb[hc // 4]], writes=[cbb], signal=(hc == 15))
                            if hp_ == 0:
                                cp_, cpb = cpart[t % 2]
                                Sc.op("scalar", ACT(cp_[:, 0:512], cb2[0][0][:], AF.Copy), reads=[cb2[0][1]], writes=[cpb])
                                Sc.op("vector", CP(cp_[:, 512:1024], cb2[1][0][:]), reads=[cb2[1][1]], writes=[cpb])
                                Sc.dma("sync", ch_res1, DMA(c1_d[t * 128:(t + 1) * 128, :], cp_[:]), reads=[cpb], writes=[P.c1[t]])
                            else:
                                h_, hb = x1[t % 2]
                                c1t, c1b = c1[0]
                                Sc.dma("sync", ch_misc[2 + t % 2], DMA(h_[:], res1_d[t * 128:(t + 1) * 128, :]), reads=[P.res1[t]], writes=[hb])
                                Sc.dma("sync", ch_misc[4], DMA(c1t[:], c1_d[t * 128:(t + 1) * 128, :]), reads=[P.c1[t]], writes=[c1b])
                                for half in range(2):
                                    cs = slice(half * 512, (half + 1) * 512)
                                    cb, cbb = cb2[half]
                                    gbk, gbkb = (banks[6], P.bank[6])
                                    for kc in range(8):
                                        Sc.op("tensor", MM(gbk[:], xT[:, kc, t * 128:(t + 1) * 128], wpg[:, kc, cs], kc == 0, kc == 7),
                                              reads=[wpgb, P.xT[tg]], writes=[gbkb], signal=(kc == 7))
                                    pbk, pbkb = (banks[0], P.bank[0]) if half == 0 else (banks[1], P.bank[1])
                                    for c2 in range(2):
                                        Sc.op("tensor", MM(pbk[:], pt_[:, c2, tt_ * 128:(tt_ + 1) * 128], wpl[:, c2, cs], c2 == 0, c2 == 1),
                                              reads=[wplb, ptb], writes=[pbkb], signal=(c2 == 1))
                                    s_, sb2 = sgt.next()
                                    Sc.op("scalar", ACT(s_[:], gbk[:], AF.Sigmoid), reads=[gbkb], writes=[sb2])
                                    Sc.op("vector", TT(s_[:], s_[:], pbk[:], ALU.mult), reads=[sb2, pbkb], writes=[sb2])
                                    Sc.op("vector", STT(h_[:, cs], h_[:, cs], ALPHA, s_[:], ALU.mult, ALU.add), reads=[hb, sb2], writes=[hb])
                                    Sc.op("vector", TT(h_[:, cs], h_[:, cs], cb[:], ALU.add), reads=[hb, cbb], writes=[hb])
                                Sc.op("gpsimd", TT(h_[:], h_[:], c1t[:], ALU.add), reads=[hb, c1b], writes=[hb])
                                if pendF[0] is not None:
                                    pendF[0]()
                                pendF[0] = ln_tail(ph, h_, hb, gbc, bbc, lnb, t, out_d, P.outb[t], ch_out, stats, mv, xn[0][0], xnb[t % 2][0], xn[0][1], xnb[t % 2][1])
                    if pendF[0] is not None:
                        pendF[0]()
                    end_phase()

        Sc.wait_all("sync", [(ch_out, ch_out.val)])
        if dbg:
            Sc.wait_all("sync", [(ch_res1, ch_res1.val), (ch_ot, ch_ot.val)])
        with nc.Block() as block:
            Sc.emit(block)
    return nc


def _bucket(d):
    d = np.maximum(d, 0).astype(np.int64)
    dm = np.maximum(d, 1).astype(np.float32)
    lr = np.log(dm / np.float32(16.0)) / np.float32(math.log(2048 / 16))
    large = 16 + (lr * np.float32(16.0)).astype(np.int32)
    return np.where(d < 16, d, np.minimum(large, 31)).astype(np.int64)


def bias_tables(rel_bias, S):
    WA = S + PADA
    p = np.arange(128)[:, None]
    j = np.arange(WA)[None, :]
    bA = _bucket(j - PADA - p)
    biasA = np.ascontiguousarray(rel_bias[bA][:, :, 0:4].transpose(2, 0, 1)).astype(np.float32)
    f = np.arange(256)[None, :]
    step = np.clip(f - p, 0, 128)
    biasD = np.zeros((12, 128, 256), np.float32)
    for gi, (_, dl) in enumerate(DIL):
        bD = _bucket(step * dl)
        for hs in range(4):
            biasD[gi * 4 + hs] = rel_bias[bD, 4 + gi * 4 + hs]
    return biasA, biasD


def lam_consts(layers):
    out = np.zeros((len(layers), 128, 2), np.float32)
    for i, l in enumerate(layers):
        li = 0.8 - 0.6 * math.exp(-0.3 * l)
        out[i, :, 0] = li
        out[i, :, 1] = 1.0 - li
    return out


_WNAMES = ["w_in", "da_lambda", "da_norm", "w_branch_da", "w_branch_sb", "w_branch_dil", "w_out", "ln1_g", "ln1_b",
           "w_up", "w_down", "w_ple_gate", "w_ple", "ln2_g", "ln2_b"]
_PROG = {}


def get_program(S, NL):
    key = (S, NL)
    if key not in _PROG:
        _PROG[key] = build_program(S, NL)
    return _PROG[key]


def make_in_maps(inputs, xs, layers, S):
    f32 = lambda a: np.ascontiguousarray(np.asarray(a, dtype=np.float32))
    biasA, biasD = bias_tables(f32(inputs["rel_bias"]), S)
    lamc = lam_consts(layers)
    shared = {}
    for n in _WNAMES:
        a = f32(inputs[n])[layers]
        if n == "da_lambda":
            a = a.reshape(len(layers), 256)
        shared[n] = np.ascontiguousarray(a)
    shared["biasA"] = biasA
    shared["biasD"] = biasD
    shared["lamc"] = lamc
    p = inputs["p"]
    maps = []
    for b in range(len(xs)):
        m = dict(shared)
        m["x"] = f32(xs[b])
        m["pT"] = np.ascontiguousarray(np.asarray(p[layers, b], dtype=np.float32).transpose(0, 2, 1))
        maps.append(m)
    return maps


FUSED = True


def kernel(**inputs):
    x = np.asarray(inputs["x"], dtype=np.float32)
    B, S, _ = x.shape
    NLT = inputs["w_in"].shape[0]
    xs = [x[b] for b in range(B)]
    if FUSED:
        nc = get_program(S, NLT)
        maps = make_in_maps(inputs, xs, list(range(NLT)), S)
        res = run_bass_kernel_spmd(nc, maps, core_ids=list(range(B)))
        xs = [np.asarray(r["out"]) for r in res.results]
    else:
        nc = get_program(S, 1)
        for l in range(NLT):
            maps = make_in_maps(inputs, xs, [l], S)
            res = run_bass_kernel_spmd(nc, maps, core_ids=list(range(B)))
            xs = [np.asarray(r["out"]) for r in res.results]
    return np.stack(xs, axis=0).astype(np.float32)
```

```python
import math
import numpy as np
from contextlib import ExitStack
import concourse.bass as bass
import concourse.mybir as mybir
from concourse.bass_utils import run_bass_kernel_spmd

F32 = mybir.dt.float32
BF16 = mybir.dt.bfloat16
AF = mybir.ActivationFunctionType
ALU = mybir.AluOpType
AX = mybir.AxisListType

D = 1024
DFF = 4096
PLE = 256
INC = 10752
A_Q, A_K, A_V = 0, 512, 1024
B_Q, B_K, B_V = 1536, 2048, 2560
C_Q, C_K, C_V = 3072, 4608, 6144
GATE = 7680
DIL = ((128, 1), (512, 4), (2048, 16))
DEPTH = 4
ALPHA = (2 * DEPTH) ** 0.25
LN_EPS = 1e-5
RMS_EPS = 1e-5
PADA = 384


class Buf:
    __slots__ = ("name", "w", "r")

    def __init__(self, name=""):
        self.name = name
        self.w = None
        self.r = {}


class Src:
    def __init__(self, name, sem, step):
        self.name = name
        self.sem = sem
        self.val = 0
        self.step = step


class Sched:
    ENGS = ("tensor", "vector", "scalar", "gpsimd", "sync")

    def __init__(self, nc, stack):
        self.nc = nc
        self.stack = stack
        self.ops = {e: [] for e in self.ENGS}
        self.src = {}
        self.seen = {e: {} for e in self.ENGS}
        self.chans = []
        for e in self.ENGS:
            self.src[e] = Src(e, stack.enter_context(nc.semaphore("s_" + e)), 1)
        self.nops = 0
        self.nroll = 0

    def chan(self, name):
        c = Src(name, self.stack.enter_context(self.nc.semaphore("c_" + name)), 16)
        self.chans.append(c)
        return c

    def _need(self, eng, deps):
        best = {}
        for s, v in deps:
            if best.get(s, 0) < v:
                best[s] = v
        for s, v in best.items():
            if self.seen[eng].get(s, 0) >= v:
                continue
            self.seen[eng][s] = v
            self.ops[eng].append(("wait", s.sem, v))

    @staticmethod
    def _deps(reads, writes):
        deps = []
        for b in reads:
            if b.w is not None:
                deps.append(b.w)
        for b in writes:
            if b.w is not None:
                deps.append(b.w)
            for rs, rv in b.r.items():
                deps.append((rs, rv))
        return deps

    LIMIT = 30000

    def _roll(self, s):
        if s.val < self.LIMIT:
            return s
        self.nroll += 1
        n = Src(s.name, self.stack.enter_context(self.nc.semaphore("r%d_%s" % (self.nroll, s.name))), s.step)
        return n

    def op(self, eng, fn, reads=(), writes=(), signal=True):
        s = self._roll(self.src[eng])
        self.src[eng] = s
        deps = self._deps(reads, writes)
        if eng == "tensor":
            deps = [d for d in deps if d[0] is not s]
        else:
            deps = [d for d in deps if not (d[0] is s and d[1] > s.val)]
        self._need(eng, deps)
        if signal:
            s.val += 1
            tag = (s, s.val)
            self.ops[eng].append(("op", fn, s.sem, 1))
        else:
            tag = (s, s.val + 1)
            self.ops[eng].append(("op", fn, None, 0))
        self.nops += 1
        for b in reads:
            if b.r.get(tag[0], 0) < tag[1]:
                b.r[tag[0]] = tag[1]
        for b in writes:
            b.w = tag
            b.r = {}
        return tag

    def dma(self, eng, ch, fn, reads=(), writes=()):
        deps = self._deps(reads, writes)
        self._need(eng, deps)
        ch.val += 16
        tag = (ch, ch.val)
        self.ops[eng].append(("op", fn, ch.sem, 16))
        self.nops += 1
        for b in reads:
            if b.r.get(ch, 0) < ch.val:
                b.r[ch] = ch.val
        for b in writes:
            b.w = tag
            b.r = {}
        return tag

    def barrier(self):
        allsrc = [(self.src[e], self.src[e].val) for e in self.ENGS if self.src[e].val > 0]
        allsrc += [(c, c.val) for c in self.chans if c.val > 0]
        for e in self.ENGS:
            self._need(e, [d for d in allsrc if d[0] is not self.src[e]])

    def wait_all(self, eng, tags):
        self._need(eng, tags)

    def prewait(self, eng, reads=(), writes=()):
        s = self.src[eng]
        deps = self._deps(reads, writes)
        if eng == "tensor":
            deps = [d for d in deps if d[0] is not s]
        else:
            deps = [d for d in deps if not (d[0] is s and d[1] > s.val)]
        self._need(eng, deps)

    def emit(self, block):
        def mk(eng):
            lst = self.ops[eng]

            def body(e):
                for it in lst:
                    if it[0] == "wait":
                        e.wait_ge(it[1], it[2])
                    else:
                        ins = it[1](e)
                        if it[2] is not None:
                            ins.then_inc(it[2], it[3])
            return body
        block.tensor(mk("tensor"))
        block.vector(mk("vector"))
        block.scalar(mk("scalar"))
        block.gpsimd(mk("gpsimd"))
        block.sync(mk("sync"))
        self.ops = {e: [] for e in self.ENGS}


def MM(out, lhsT, rhs, start, stop):
    return lambda e: e.matmul(out, lhsT=lhsT, rhs=rhs, start=start, stop=stop)


def TR(out, in_, ident):
    return lambda e: e.transpose(out, in_, ident)


def ACT(out, in_, func, scale=1.0, bias=0.0):
    return lambda e: e.activation(out=out, in_=in_, func=func, bias=bias, scale=scale)


def CP(out, in_):
    return lambda e: e.tensor_copy(out=out, in_=in_)


def TT(out, in0, in1, op):
    return lambda e: e.tensor_tensor(out=out, in0=in0, in1=in1, op=op)


def TS(out, in0, s1, s2, op0, op1=None):
    if op1 is None:
        return lambda e: e.tensor_scalar(out=out, in0=in0, scalar1=s1, scalar2=None, op0=op0)
    return lambda e: e.tensor_scalar(out=out, in0=in0, scalar1=s1, scalar2=s2, op0=op0, op1=op1)


def STT(out, in0, scalar, in1, op0, op1):
    return lambda e: e.scalar_tensor_tensor(out=out, in0=in0, scalar=scalar, in1=in1, op0=op0, op1=op1)


def DMA(out, in_):
    return lambda e: e.dma_start(out=out, in_=in_)


def ASEL(out, in_, pattern, cmp, fill, base, cm):
    return lambda e: e.affine_select(out=out, in_=in_, pattern=pattern, compare_op=cmp, fill=fill,
                                     base=base, channel_multiplier=cm)


class Rot:
    def __init__(self, items):
        self.items = items
        self.i = 0

    def next(self):
        it = self.items[self.i % len(self.items)]
        self.i += 1
        return it


def build_program(S, NL, dbg=False):
    T = S // 128
    G = S // 512
    WA = S + PADA
    nc = bass.Bass("TRN2", target_bir_lowering=False)

    def din(name, shape):
        return nc.dram_tensor(name, list(shape), F32, kind="ExternalInput").ap()

    x_d = din("x", [S, D])
    pT_d = din("pT", [NL, PLE, S])
    w_in_d = din("w_in", [NL, D, INC])
    lam_d = din("da_lambda", [NL, 256])
    dan_d = din("da_norm", [NL, 128])
    wb_d = [din("w_branch_da", [NL, 512, D]), din("w_branch_sb", [NL, 512, D]), din("w_branch_dil", [NL, 512, D])]
    wout_d = din("w_out", [NL, D, D])
    ln1g_d = din("ln1_g", [NL, D])
    ln1b_d = din("ln1_b", [NL, D])
    wup_d = din("w_up", [NL, D, DFF])
    wdn_d = din("w_down", [NL, DFF, D])
    wpg_d = din("w_ple_gate", [NL, D, D])
    wple_d = din("w_ple", [NL, PLE, D])
    ln2g_d = din("ln2_g", [NL, D])
    ln2b_d = din("ln2_b", [NL, D])
    biasA_d = din("biasA", [4, 128, WA])
    biasD_d = din("biasD", [12, 128, 256])
    lamc_d = din("lamc", [NL, 128, 2])
    out_d = nc.dram_tensor("out", [S, D], F32, kind="ExternalOutput").ap()
    okind = "ExternalOutput" if dbg else "Internal"
    res1_d = nc.dram_tensor("res1", [S, D], F32, kind=okind).ap()
    ot_d = nc.dram_tensor("ot", [12, 128, S], BF16, kind=okind).ap()
    ea_d = nc.dram_tensor("ea", [4, 128, WA], BF16, kind="Internal").ap()
    c1_d = nc.dram_tensor("c1", [S, D], F32, kind="Internal").ap()
    wsc = {
        "gate": nc.dram_tensor("wsc_gate", [D, 3072], BF16, kind="Internal").ap(),
        "br0": nc.dram_tensor("wsc_br0", [512, D], BF16, kind="Internal").ap(),
        "br1": nc.dram_tensor("wsc_br1", [512, D], BF16, kind="Internal").ap(),
        "br2": nc.dram_tensor("wsc_br2", [512, D], BF16, kind="Internal").ap(),
        "wo": nc.dram_tensor("wsc_wo", [D, D], BF16, kind="Internal").ap(),
        "up": nc.dram_tensor("wsc_up", [D, DFF], BF16, kind="Internal").ap(),
        "dn": nc.dram_tensor("wsc_dn", [DFF, D], BF16, kind="Internal").ap(),
        "pg": nc.dram_tensor("wsc_pg", [D, D], BF16, kind="Internal").ap(),
        "ple": nc.dram_tensor("wsc_ple", [PLE, D], BF16, kind="Internal").ap(),
    }

    with ExitStack() as st:
        Sc = Sched(nc, st)

        uid = [0]

        def sbuf(stack, name, shape, dt):
            uid[0] += 1
            return stack.enter_context(nc.sbuf_tensor("%s_u%d" % (name, uid[0]), list(shape), dt))

        banks = [st.enter_context(nc.psum_tensor("bank%d" % i, [128, 512], F32)) for i in range(8)]
        xT = sbuf(st, "xT", [128, 8, S], BF16)
        ident = sbuf(st, "ident", [128, 128], BF16)
        ones_bf = sbuf(st, "ones_bf", [128, 128], BF16)
        nones_bf = sbuf(st, "nones_bf", [128, 128], BF16)
        uneg = sbuf(st, "uneg", [128, 128], BF16)
        onesf = sbuf(st, "onesf", [128, 512], F32)

        class P:
            bank = [Buf("bank%d" % i) for i in range(8)]
            xT = [Buf("xT%d" % g) for g in range(G)]
            const = Buf("const")
            ED = Buf("ED")
            ea = [Buf("ea%d" % h) for h in range(4)]
            ot = [[Buf("ot%d_%d" % (i, g)) for g in range(G)] for i in range(12)]
            res1 = [Buf("res1_%d" % t) for t in range(T)]
            outb = [Buf("out_%d" % t) for t in range(T)]
            c1 = [Buf("c1_%d" % t) for t in range(T)]

        ch_out = Sc.chan("out")
        ch_res1 = Sc.chan("res1")
        ch_ot = Sc.chan("ot")
        ch_ea = Sc.chan("ea")
        ch_misc = [Sc.chan("misc%d" % i) for i in range(12)]
        ch_w = [Sc.chan("w%d" % i) for i in range(12)]
        ch_cast = {k: Sc.chan("cast_" + k) for k in wsc}
        wscb = {k: [] for k in wsc}

        def cast_jobs(l):
            jobs = []
            for r in range(0, D, 128):
                jobs.append(("gate", wsc["gate"][r:r + 128, :], w_in_d[l, r:r + 128, GATE:GATE + 3072]))
            for i in range(3):
                for r in range(0, 512, 128):
                    jobs.append(("br%d" % i, wsc["br%d" % i][r:r + 128, :], wb_d[i][l, r:r + 128, :]))
            for r in range(0, D, 128):
                jobs.append(("wo", wsc["wo"][r:r + 128, :], wout_d[l, r:r + 128, :]))
            for r in range(0, D, 128):
                jobs.append(("up", wsc["up"][r:r + 128, :], wup_d[l, r:r + 128, :]))
            for r in range(0, DFF, 512):
                jobs.append(("dn", wsc["dn"][r:r + 512, :], wdn_d[l, r:r + 512, :]))
            for r in range(0, D, 512):
                jobs.append(("pg", wsc["pg"][r:r + 512, :], wpg_d[l, r:r + 512, :]))
            jobs.append(("ple", wsc["ple"][:, :], wple_d[l, :, :]))
            return jobs

        def issue_casts(jobs):
            for (k, dst, src) in jobs:
                b_ = Buf("wsc_" + k)
                wscb[k].append(b_)
                Sc.dma("gpsimd", ch_cast[k], DMA(dst, src), writes=[b_])

        def end_phase(stack_unused=None):
            Sc.barrier()
            with nc.Block() as block:
                Sc.emit(block)

        def evac(i, out, in_, scale, reads, writes):
            if i % 2 == 0:
                Sc.op("scalar", ACT(out, in_, AF.Copy, scale=scale), reads=reads, writes=writes)
            else:
                Sc.op("vector", TS(out, in_, scale, None, ALU.mult), reads=reads, writes=writes)

        pj = Rot([(banks[6], P.bank[6]), (banks[7], P.bank[7])])
        cnt = {"ev": 0}

        def proj_feat(wt, wbuf, c0, dst_fn, dst_bufs, scale):
            for tg in range(G):
                bk, bb = pj.next()
                for kc in range(8):
                    Sc.op("tensor", MM(bk[:], wt[:, kc, c0:c0 + 128], xT[:, kc, tg * 512:(tg + 1) * 512], kc == 0, kc == 7),
                          reads=[wbuf, P.xT[tg]], writes=[bb], signal=(kc == 7))
                o, i_ = dst_fn(tg, bk)
                cnt["ev"] += 1
                evac(cnt["ev"], o, i_, scale, [bb], dst_bufs)

        def proj_tok(wt, wbuf, c0, ncols, vdst, vbuf, tok_ap_fn):
            per = 512 // ncols
            for t0 in range(0, T, per):
                bk, bb = pj.next()
                n = min(per, T - t0)
                for j in range(n):
                    for kc in range(8):
                        Sc.op("tensor", MM(bk[:, j * ncols:(j + 1) * ncols], tok_ap_fn(kc, t0 + j), wt[:, kc, c0:c0 + ncols], kc == 0, kc == 7),
                              reads=[wbuf] + P.xT, writes=[bb], signal=(kc == 7 and j == n - 1))
                cnt["ev"] += 1
                evac(cnt["ev"], vdst[:, t0:t0 + n, :], bk[:, 0:n * ncols].rearrange("p (t c) -> p t c", c=ncols), 1.0, [bb], [vbuf])

        def load_w_cols(stack_, wt, wbuf, ch, l, cols):
            o = 0
            for (c0, n) in cols:
                src = w_in_d[l, :, c0:c0 + n].rearrange("(kc p) c -> p kc c", p=128)
                Sc.dma("gpsimd", ch, DMA(wt[:, :, o:o + n], src), writes=[wbuf])
                o += n

        def transposes_to_xT(xb_ap, xb_buf, t):
            tp = banks[7][:].bitcast(BF16)
            for c in range(8):
                Sc.op("tensor", TR(tp[:, c * 128:(c + 1) * 128], xb_ap[:, c * 128:(c + 1) * 128], ident[:]),
                      reads=[xb_buf, P.const], writes=[P.bank[7]], signal=(c == 7))
            tg = t // 4
            Sc.op("scalar", ACT(xT[:, :, t * 128:(t + 1) * 128], tp.rearrange("p (c n) -> p c n", c=8), AF.Copy),
                  reads=[P.bank[7]], writes=[P.xT[tg]])

        with ExitStack() as ph:
            Sc.op("vector", lambda e: e.memset(onesf[:], 1.0), writes=[P.const])
            Sc.op("vector", CP(ones_bf[:], onesf[:, 0:128]), reads=[P.const], writes=[P.const])
            Sc.op("vector", TS(nones_bf[:], onesf[:, 0:128], -1.0, None, ALU.mult), reads=[P.const], writes=[P.const])
            Sc.op("gpsimd", ASEL(ident[:], ones_bf[:], [[1, 128]], ALU.is_equal, 0.0, 0, -1), reads=[P.const], writes=[P.const])
            Sc.op("gpsimd", ASEL(uneg[:], nones_bf[:], [[-1, 128]], ALU.is_ge, 0.0, 0, 1), reads=[P.const], writes=[P.const])
            CH = 1024
            rawA = [sbuf(ph, "rawA%d" % i, [128, CH], F32) for i in range(2)]
            rawAb = [Buf("rawA%d" % i) for i in range(2)]
            eab = [sbuf(ph, "eab%d" % i, [128, CH], BF16) for i in range(2)]
            eabb = [Buf("eab%d" % i) for i in range(2)]
            k = 0
            for h in range(4):
                for c0 in range(0, WA, CH):
                    n = min(CH, WA - c0)
                    i = k % 2
                    k += 1
                    Sc.dma("sync", ch_misc[1 + i], DMA(rawA[i][:, 0:n], biasA_d[h, :, c0:c0 + n]), writes=[rawAb[i]])
                    Sc.op("scalar", ACT(rawA[i][:, 0:n], rawA[i][:, 0:n], AF.Exp), reads=[rawAb[i]], writes=[rawAb[i]])
                    Sc.op("gpsimd", ASEL(eab[i][:, 0:n], rawA[i][:, 0:n], [[1, n]], ALU.is_ge, 0.0, c0 - PADA, -1),
                          reads=[rawAb[i]], writes=[eabb[i]])
                    Sc.dma("sync", ch_ea, DMA(ea_d[h, :, c0:c0 + n], eab[i][:, 0:n]), reads=[eabb[i]], writes=[P.ea[h]])
            xin = [sbuf(ph, "xin%d" % i, [128, D], F32) for i in range(2)]
            xinb = [Buf("xin%d" % i) for i in range(2)]
            xbf = [sbuf(ph, "xbf%d" % i, [128, D], BF16) for i in range(2)]
            xbfb = [Buf("xbf%d" % i) for i in range(2)]
            for t in range(T):
                i = t % 2
                Sc.dma("sync", ch_misc[3 + i], DMA(xin[i][:], x_d[t * 128:(t + 1) * 128, :]), writes=[xinb[i]])
                Sc.op("vector", CP(xbf[i][:], xin[i][:]), reads=[xinb[i]], writes=[xbfb[i]])
                transposes_to_xT(xbf[i], xbfb[i], t)
            end_phase()

        for l in range(NL):
            res_in = x_d if l == 0 else out_d
            res_in_bufs = None if l == 0 else P.outb

            with ExitStack() as ph:
                qk = [(sbuf(ph, "qA%d" % i, [128, S], BF16), sbuf(ph, "kA%d" % i, [128, S], BF16),
                       sbuf(ph, "vA%d" % i, [128, T, 128], BF16), Buf("qkvA%d" % i)) for i in range(2)]
                wA = [(sbuf(ph, "wA%d" % i, [128, 8, 384], BF16), Buf("wA%d" % i)) for i in range(2)]
                EAt = [(sbuf(ph, "EA%d" % i, [128, WA], BF16), Buf("EA%d" % i)) for i in range(2)]
                praw = Rot([(sbuf(ph, "praw%d" % i, [128, 512], BF16), Buf()) for i in range(6)])
                pTt = Rot([(sbuf(ph, "pT%d" % i, [128, 512], BF16), Buf()) for i in range(8)])
                lamt = sbuf(ph, "lamt", [128, 256], F32)
                lprod = sbuf(ph, "lprod", [128, 128], F32)
                lsum = sbuf(ph, "lsum", [128, 2], F32)
                lamc = sbuf(ph, "lamc", [128, 2], F32)
                nlam = sbuf(ph, "nlam", [128, 1], F32)
                gA = sbuf(ph, "gA", [128, 1], F32)
                lb = Buf("lam")
                r0 = sbuf(ph, "r0", [128, 512], F32)
                r1 = sbuf(ph, "r1", [128, 512], F32)
                t0 = sbuf(ph, "t0", [128, 512], F32)
                t1 = sbuf(ph, "t1", [128, 512], F32)
                oo = sbuf(ph, "oo", [128, 512], F32)
                sq = sbuf(ph, "sq", [128, 512], BF16)
                lnv = sbuf(ph, "lnv", [128, 512], F32)
                postb = Buf("post")
                ostage = Rot([(sbuf(ph, "ostA%d" % i, [128, 512], BF16), Buf()) for i in range(2)])

                Sc.dma("sync", ch_misc[0], DMA(lamt[:], lam_d[l, :].partition_broadcast(128)), writes=[lb])
                Sc.dma("sync", ch_misc[1], DMA(lamc[:], lamc_d[l, :, :]), writes=[lb])
                Sc.dma("sync", ch_misc[2], DMA(gA[:], dan_d[l, :].rearrange("(p o) -> p o", o=1)), writes=[lb])
                l4 = lamt[:].rearrange("p (a b d) -> p a b d", a=2, b=2)
                Sc.op("vector", TT(lprod[:].rearrange("p (a d) -> p a d", a=2), l4[:, :, 0, :], l4[:, :, 1, :], ALU.mult), reads=[lb], writes=[lb])
                Sc.op("vector", lambda e: e.tensor_reduce(out=lsum[:], in_=lprod[:].rearrange("p (a d) -> p a d", a=2), axis=AX.X, op=ALU.add),
                      reads=[lb], writes=[lb])
                Sc.op("scalar", ACT(lsum[:], lsum[:], AF.Exp), reads=[lb], writes=[lb])
                Sc.op("vector", TT(nlam[:], lsum[:, 1:2], lsum[:, 0:1], ALU.subtract), reads=[lb], writes=[lb])
                Sc.op("vector", TT(nlam[:], nlam[:], lamc[:, 0:1], ALU.subtract), reads=[lb], writes=[lb])
                Sc.op("vector", TT(gA[:], gA[:], lamc[:, 1:2], ALU.mult), reads=[lb], writes=[lb])

                def loadA(h):
                    wt, wbuf = wA[h % 2]
                    load_w_cols(ph, wt, wbuf, ch_w[h % 2], l, [(A_Q + h * 128, 128), (A_K + h * 128, 128), (A_V + h * 128, 128)])
                    et, eb = EAt[h % 2]
                    Sc.dma("sync", ch_w[2 + h % 2], DMA(et[:], ea_d[h, :, :]), reads=[P.ea[h]], writes=[eb])

                loadA(0)
                for h in range(4):
                    if h + 1 < 4:
                        loadA(h + 1)
                    wt, wbuf = wA[h % 2]
                    et, eb = EAt[h % 2]
                    qT, kT, vv, qb = qk[h % 2]
                    proj_feat(wt, wbuf, 0, lambda tg, bk: (qT[:, tg * 512:(tg + 1) * 512], bk[:]), [qb], 0.125)
                    proj_feat(wt, wbuf, 128, lambda tg, bk: (kT[:, tg * 512:(tg + 1) * 512], bk[:]), [qb], 1.0)
                    proj_tok(wt, wbuf, 256, 128, vv, qb, lambda kc, t: xT[:, kc, t * 128:(t + 1) * 128])
                    for g in range(G):
                        nk = 4 * g + 4
                        n = 2 * nk
                        Ub = [(banks[2], P.bank[2]), (banks[3], P.bank[3])]
                        Sb_ = [(banks[4], P.bank[4]), (banks[5], P.bank[5])]
                        sbk = [(banks[0], P.bank[0]), (banks[1], P.bank[1]), (banks[6], P.bank[6]), (banks[7], P.bank[7])]
                        pts = {}
                        prs = {}

                        def stage_S(kt):
                            for m in range(2):
                                bk, bb = sbk[(2 * kt + m) % 4]
                                Sc.op("tensor", MM(bk[:], kT[64 * m:64 * m + 64, kt * 128:(kt + 1) * 128],
                                                   qT[64 * m:64 * m + 64, g * 512:(g + 1) * 512], True, True),
                                      reads=[qb], writes=[bb], signal=(m == 1))

                        def stage_E(kt):
                            for m in range(2):
                                bk, bb = sbk[(2 * kt + m) % 4]
                                pr, prb = praw.next()
                                Sc.op("scalar", ACT(pr[:], bk[:], AF.Exp), reads=[bb], writes=[prb])
                                prs[(kt, m)] = (pr, prb)

                        def stage_M(kt):
                            for m in range(2):
                                pr, prb = prs.pop((kt, m))
                                pt, ptb = pTt.next()
                                off = PADA + g * 512 - kt * 128
                                Sc.op("vector", TT(pt[:], pr[:], et[:, off:off + 512], ALU.mult), reads=[prb, eb], writes=[ptb])
                                pts[(kt, m)] = (pt, ptb)

                        def stage_V(kt):
                            for m in range(2):
                                pt, ptb = pts.pop((kt, m))
                                ub, ubb = Ub[m]
                                sb_, sbb = Sb_[m]
                                Sc.op("tensor", MM(ub[:], vv[:, kt, :], pt[:], kt == 0, kt == nk - 1),
                                      reads=[qb, ptb], writes=[ubb], signal=False)
                                Sc.op("tensor", MM(sb_[:], ones_bf[:], pt[:], kt == 0, kt == nk - 1),
                                      reads=[P.const, ptb], writes=[sbb], signal=(m == 1))

                        for it in range(nk + 2):
                            rd = [qb, P.const]
                            wr = []
                            if it < nk:
                                wr += [sbk[(2 * it + m) % 4][1] for m in range(2)]
                            if it >= 2:
                                rd += [pts[(it - 2, m)][1] for m in range(2)]
                                wr += [Ub[0][1], Ub[1][1], Sb_[0][1], Sb_[1][1]]
                            Sc.prewait("tensor", rd, wr)
                            if it < nk:
                                stage_S(it)
                            if it >= 2:
                                stage_V(it - 2)
                            if it < nk:
                                stage_E(it)
                            if 1 <= it <= nk:
                                stage_M(it - 1)
                        Sc.op("scalar", ACT(r0[:], banks[4][:], AF.Ln), reads=[P.bank[4]], writes=[postb])
                        Sc.op("scalar", ACT(r1[:], banks[5][:], AF.Ln), reads=[P.bank[5]], writes=[postb])
                        Sc.op("scalar", ACT(r0[:], r0[:], AF.Exp, scale=-1.0), reads=[postb], writes=[postb])
                        Sc.op("scalar", ACT(r1[:], r1[:], AF.Exp, scale=-1.0), reads=[postb], writes=[postb])
                        Sc.op("vector", TT(t0[:], banks[2][:], r0[:], ALU.mult), reads=[P.bank[2], postb], writes=[postb])
                        Sc.op("vector", TT(t1[:], banks[3][:], r1[:], ALU.mult), reads=[P.bank[3], postb], writes=[postb])
                        Sc.op("vector", STT(oo[:], t1[:], nlam[:, 0:1], t0[:], ALU.mult, ALU.add), reads=[postb, lb], writes=[postb])
                        Sc.op("scalar", ACT(sq[:], oo[:], AF.Square), reads=[postb], writes=[postb])
                        Sc.op("tensor", MM(banks[0][:], ones_bf[:], sq[:], True, True), reads=[postb, P.const], writes=[P.bank[0]])
                        Sc.op("scalar", ACT(lnv[:], banks[0][:], AF.Ln, scale=1.0 / 128.0, bias=RMS_EPS), reads=[P.bank[0]], writes=[postb])
                        Sc.op("scalar", ACT(lnv[:], lnv[:], AF.Exp, scale=-0.5), reads=[postb], writes=[postb])
                        os_, osb = ostage.next()
                        Sc.op("vector", STT(os_[:], oo[:], gA[:, 0:1], lnv[:], ALU.mult, ALU.mult), reads=[postb, lb], writes=[osb])
                        Sc.dma("sync", ch_ot, DMA(ot_d[h, :, g * 512:(g + 1) * 512], os_[:]), reads=[osb], writes=[P.ot[h][g]])
                end_phase()

            with ExitStack() as ph:
                qk = [(sbuf(ph, "qB%d" % i, [128, S], BF16), sbuf(ph, "kB%d" % i, [128, S], BF16),
                       sbuf(ph, "vB%d" % i, [128, T, 128], BF16), Buf("qkvB%d" % i)) for i in range(2)]
                wB = [(sbuf(ph, "wB%d" % i, [128, 8, 384], BF16), Buf("wB%d" % i)) for i in range(2)]
                e32 = Rot([(sbuf(ph, "e32_%d" % i, [128, 512], F32), Buf()) for i in range(3)])
                spt = Rot([(sbuf(ph, "sp%d" % i, [128, 512], BF16), Buf()) for i in range(5)])
                tmpt = Rot([(sbuf(ph, "tmpB%d" % i, [128, 512], F32), Buf()) for i in range(3)])
                at = Rot([(sbuf(ph, "aB%d" % i, [128, 512], BF16), Buf()) for i in range(6)])
                csb = [(sbuf(ph, "csb%d" % i, [128, 512], F32), Buf()) for i in range(2)]
                zct = Rot([(sbuf(ph, "zcB%d" % i, [128, 512], F32), Buf()) for i in range(3)])
                ostage = Rot([(sbuf(ph, "ostB%d" % i, [128, 512], BF16), Buf()) for i in range(2)])
                maskB = sbuf(ph, "maskB", [128, 4, 512], BF16)
                mkb = Buf("maskB")
                for dd in range(4):
                    Sc.op("gpsimd", ASEL(maskB[:, dd, :], onesf[:], [[1, 512]], ALU.is_gt, 0.0, -128 * dd, -1),
                          reads=[P.const], writes=[mkb])

                def loadB(hp):
                    wt, wbuf = wB[hp % 2]
                    load_w_cols(ph, wt, wbuf, ch_w[hp % 2], l, [(B_Q + hp * 128, 128), (B_K + hp * 128, 128), (B_V + hp * 128, 128)])

                loadB(0)
                for k_ in wscb:
                    wscb[k_] = []
                cjobs = cast_jobs(l)
                cper = (len(cjobs) + 3) // 4
                for hp in range(4):
                    if hp + 1 < 4:
                        loadB(hp + 1)
                    issue_casts(cjobs[hp * cper:(hp + 1) * cper])
                    wt, wbuf = wB[hp % 2]
                    qT, kT, vv, qb = qk[hp % 2]
                    proj_feat(wt, wbuf, 0, lambda tg, bk: (qT[:, tg * 512:(tg + 1) * 512], bk[:]), [qb], 0.125)
                    proj_feat(wt, wbuf, 128, lambda tg, bk: (kT[:, tg * 512:(tg + 1) * 512], bk[:]), [qb], 1.0)
                    proj_tok(wt, wbuf, 256, 128, vv, qb, lambda kc, t: xT[:, kc, t * 128:(t + 1) * 128])
                    for g in range(G):
                        n = 4 * g + 4
                        for hh in range(2):
                            rs = slice(64 * hh, 64 * hh + 64)
                            zb = [(banks[0], P.bank[0]), (banks[1], P.bank[1]), (banks[6], P.bank[6])]
                            za = [(banks[2], P.bank[2]), (banks[3], P.bank[3])]
                            cbk = [(banks[4], P.bank[4]), (banks[7], P.bank[7])]
                            NFILL = 0
                            st_zc = {}
                            st_e1 = {}
                            st_e = {}
                            st_sp = {}
                            st_a = {}

                            def kt_of(i):
                                return 4 * g + 3 - i

                            def stZ1_pe(i):
                                kt = kt_of(i)
                                bk, bb = zb[i % 3]
                                Sc.op("tensor", MM(bk[:], kT[rs, kt * 128:(kt + 1) * 128], qT[rs, g * 512:(g + 1) * 512], True, True),
                                      reads=[qb], writes=[bb])

                            def stZ1_act(i):
                                bk, bb = zb[i % 3]
                                e_, eb_ = e32.next()
                                Sc.op("scalar", ACT(e_[:], bk[:], AF.Exp), reads=[bb], writes=[eb_])
                                st_e[i] = (e_, eb_)
                                st_e1[i] = eb_

                            def stZ2(i):
                                e_, eb_ = st_e.pop(i)
                                sp_, spb = spt.next()
                                Sc.op("scalar", ACT(sp_[:], e_[:], AF.Ln, bias=1.0), reads=[eb_], writes=[spb])
                                if i <= 3:
                                    Sc.op("gpsimd", TT(sp_[:], sp_[:], maskB[:, 3 - i, :], ALU.mult), reads=[spb, mkb], writes=[spb])
                                st_sp[i] = (sp_, spb)

                            def stA_pe(i):
                                sp_, spb = st_sp[i]
                                kt = kt_of(i)
                                bk, bb = za[i % 2]
                                Sc.op("tensor", MM(bk[:], uneg[:], sp_[:], True, True), reads=[spb, P.const], writes=[bb])
                                if i < n - 1:
                                    cb_, cbb_ = cbk[i % 2]
                                    Sc.op("tensor", MM(cb_[:], nones_bf[:], sp_[:], True, True), reads=[spb, P.const], writes=[cbb_])

                            def stA_dve(i):
                                st_sp.pop(i)
                                if i < n - 1:
                                    cb_, cbb_ = cbk[i % 2]
                                    cs, csbuf = csb[i % 2]
                                    if i == 0:
                                        Sc.op("vector", CP(cs[:], cb_[:]), reads=[cbb_], writes=[csbuf])
                                    else:
                                        pc, pcb = csb[(i - 1) % 2]
                                        Sc.op("vector", TT(cs[:], cb_[:], pc[:], ALU.add), reads=[cbb_, pcb], writes=[csbuf])

                            def stZC(i):
                                zbk, zbb = zb[i % 3]
                                zc_, zcb = zct.next()
                                e1b = st_e1.pop(i)
                                if i == 0:
                                    Sc.op("vector", CP(zc_[:], zbk[:]), reads=[zbb, e1b], writes=[zcb])
                                else:
                                    pc, pcb = csb[(i - 1) % 2]
                                    Sc.op("vector", TT(zc_[:], zbk[:], pc[:], ALU.add), reads=[zbb, pcb, e1b], writes=[zcb])
                                st_zc[i] = (zc_, zcb)

                            def stD(i):
                                bk, bb = za[i % 2]
                                a_, ab_ = at.next()
                                zc_, zcb = st_zc.pop(i)
                                tm, tmb = tmpt.next()
                                Sc.op("vector", TT(tm[:], bk[:], zc_[:], ALU.add), reads=[bb, zcb], writes=[tmb])
                                Sc.op("scalar", ACT(a_[:], tm[:], AF.Exp), reads=[tmb], writes=[ab_])
                                if i <= 3:
                                    Sc.op("gpsimd", TT(a_[:], a_[:], maskB[:, 3 - i, :], ALU.mult), reads=[ab_, mkb], writes=[ab_])
                                st_a[i] = (a_, ab_)

                            def stV(i):
                                kt = kt_of(i)
                                a_, ab_ = st_a.pop(i)
                                Sc.op("tensor", MM(banks[5][rs, :], vv[:, kt, rs], a_[:], i == 0, i == n - 1),
                                      reads=[qb, ab_], writes=[P.bank[5]], signal=(i == n - 1))

                            for it in range(n + 4):
                                rd = [qb, P.const]
                                wr = []
                                if it < n:
                                    wr.append(zb[it % 3][1])
                                if 0 <= it - 2 < n:
                                    rd.append(st_sp[it - 2][1])
                                    wr.append(za[(it - 2) % 2][1])
                                    if it - 2 < n - 1:
                                        wr.append(cbk[(it - 2) % 2][1])
                                if 0 <= it - 4 < n:
                                    rd.append(st_a[it - 4][1])
                                    wr.append(P.bank[5])
                                Sc.prewait("tensor", rd, wr)
                                if it < n:
                                    stZ1_pe(it)
                                if 0 <= it - 2 < n:
                                    stA_pe(it - 2)
                                if 0 <= it - 4 < n:
                                    stV(it - 4)
                                if it < n:
                                    for _f in range(NFILL):
                                        Sc.op("tensor", MM(banks[6][:], ones_bf[:], qT[:, g * 512:(g + 1) * 512], True, True),
                                              reads=[qb, P.const], writes=[P.bank[6]], signal=False)
                                if it < n:
                                    stZ1_act(it)
                                if 0 <= it - 2 < n:
                                    stD(it - 2)
                                if it < n:
                                    stZ2(it)
                                if 0 <= it - 2 < n:
                                    stA_dve(it - 2)
                                if 0 <= it - 1 < n:
                                    stZC(it - 1)
                        os_, osb = ostage.next()
                        Sc.op("scalar", ACT(os_[:], banks[5][:], AF.Copy), reads=[P.bank[5]], writes=[osb])
                        Sc.dma("sync", ch_ot, DMA(ot_d[4 + hp, :, g * 512:(g + 1) * 512], os_[:]), reads=[osb], writes=[P.ot[4 + hp][g]])
                end_phase()

            with ExitStack() as ph:
                qk = [(sbuf(ph, "qC%d" % i, [128, S], BF16), sbuf(ph, "kC%d" % i, [128, S], BF16),
                       sbuf(ph, "vC%d" % i, [128, T, 128], BF16), Buf("qkvC%d" % i)) for i in range(2)]
                wC = [(sbuf(ph, "wC%d" % i, [128, 8, 384], BF16), Buf("wC%d" % i)) for i in range(2)]
                praw = Rot([(sbuf(ph, "prawC%d" % i, [128, 256], F32), Buf()) for i in range(3)])
                pTt = Rot([(sbuf(ph, "pTC%d" % i, [128, 256], BF16), Buf()) for i in range(3)])
                Uacc = sbuf(ph, "Uacc", [128, S], F32)
                Sacc = sbuf(ph, "Sacc", [128, S], F32)
                accb = Buf("acc")
                rc = sbuf(ph, "rcC", [128, 512], F32)
                rcb = Buf("rc")
                ostage = Rot([(sbuf(ph, "ostC%d" % i, [128, 512], BF16), Buf()) for i in range(2)])
                scale_c = 128.0 ** -0.5
                ED = sbuf(ph, "ED", [128, 12, 256], BF16)
                rawD = sbuf(ph, "rawD", [128, 12, 256], F32)
                rawDb = Buf("rawD")
                Sc.dma("sync", ch_misc[0], DMA(rawD[:], biasD_d.rearrange("h p f -> p h f")), writes=[rawDb])
                Sc.op("scalar", ACT(rawD[:], rawD[:], AF.Exp), reads=[rawDb], writes=[rawDb])
                for hh in range(12):
                    Sc.op("gpsimd", ASEL(rawD[:, hh, :], rawD[:, hh, :], [[1, 256]], ALU.is_ge, 0.0, 0, -1), reads=[rawDb], writes=[rawDb])
                    Sc.op("gpsimd", ASEL(ED[:, hh, :], rawD[:, hh, :], [[-1, 256]], ALU.is_ge, 0.0, 128, 1), reads=[rawDb], writes=[P.ED])

                def loadC(idx):
                    gi, hs = idx % 3, idx // 3
                    hd = gi * 4 + hs
                    wt, wbuf = wC[idx % 2]
                    load_w_cols(ph, wt, wbuf, ch_w[idx % 2], l, [(C_Q + hd * 128, 128), (C_K + hd * 128, 128), (C_V + hd * 128, 128)])

                loadC(0)
                for idx in range(12):
                    if idx + 1 < 12:
                        loadC(idx + 1)
                    gi, hs = idx % 3, idx // 3
                    hd = gi * 4 + hs
                    dl = DIL[gi][1]
                    L = S // dl
                    nblk = L // 128
                    wt, wbuf = wC[idx % 2]
                    qT, kT, vv, qb = qk[idx % 2]

                    def dstq(tg, bk, dst=None):
                        if dl == 1:
                            return dst[:, tg * 512:(tg + 1) * 512], bk[:]
                        m0 = tg * (512 // dl)
                        return (dst[:].rearrange("p (b m) -> p b m", b=dl)[:, :, m0:m0 + 512 // dl],
                                bk[:].rearrange("p (a b) -> p b a", b=dl))

                    proj_feat(wt, wbuf, 0, lambda tg, bk: dstq(tg, bk, qT), [qb], scale_c)
                    proj_feat(wt, wbuf, 128, lambda tg, bk: dstq(tg, bk, kT), [qb], 1.0)

                    def tokap(kc, pi):
                        r, j = pi // nblk, pi % nblk
                        s0 = r + dl * 128 * j
                        return xT[:, kc, s0:s0 + dl * 127 + 1:dl]

                    proj_tok(wt, wbuf, 256, 128, vv, qb, tokap)
                    ub = [(banks[2], P.bank[2]), (banks[3], P.bank[3])]
                    sb_ = [(banks[4], P.bank[4]), (banks[5], P.bank[5])]
                    sbk = [(banks[0], P.bank[0]), (banks[1], P.bank[1])]
                    pts = {}

                    def cS(pi):
                        r, j = pi // nblk, pi % nblk
                        ncol = 256 if j < nblk - 1 else 128
                        bk, bb = sbk[pi % 2]
                        Sc.op("tensor", MM(bk[:, 0:ncol], kT[:, pi * 128:(pi + 1) * 128], qT[:, pi * 128:pi * 128 + ncol], True, True),
                              reads=[qb], writes=[bb])
                        pr, prb = praw.next()
                        Sc.op("scalar", ACT(pr[:, 0:ncol], bk[:, 0:ncol], AF.Exp), reads=[bb], writes=[prb])
                        pt, ptb = pTt.next()
                        eng = "vector" if pi % 2 == 0 else "gpsimd"
                        Sc.op(eng, TT(pt[:, 0:ncol], pr[:, 0:ncol], ED[:, hd, 0:ncol], ALU.mult), reads=[prb, P.ED], writes=[ptb])
                        pts[pi] = (pt, ptb)

                    def cV(pi):
                        r, j = pi // nblk, pi % nblk
                        pt, ptb = pts.pop(pi)
                        u, ubb = ub[(pi // 4) % 2]
                        s_, sbb = sb_[(pi // 4) % 2]
                        c = pi % 4
                        Sc.op("tensor", MM(u[:, c * 128:(c + 1) * 128], vv[:, pi, :], pt[:, 0:128], j == 0, True),
                              reads=[qb, ptb], writes=[ubb], signal=False)
                        Sc.op("tensor", MM(s_[:, c * 128:(c + 1) * 128], ones_bf[:], pt[:, 0:128], j == 0, True),
                              reads=[P.const, ptb], writes=[sbb], signal=True)
                        if c == 3 or pi == T - 1:
                            flush(pi // 4)
                        if j < nblk - 1:
                            u2, ubb2 = ub[((pi + 1) // 4) % 2]
                            s2, sbb2 = sb_[((pi + 1) // 4) % 2]
                            c2 = (pi + 1) % 4
                            Sc.op("tensor", MM(u2[:, c2 * 128:(c2 + 1) * 128], vv[:, pi, :], pt[:, 128:256], True, False),
                                  reads=[qb, ptb], writes=[ubb2], signal=False)
                            Sc.op("tensor", MM(s2[:, c2 * 128:(c2 + 1) * 128], ones_bf[:], pt[:, 128:256], True, False),
                                  reads=[P.const, ptb], writes=[sbb2], signal=False)

                    def flush(bi):
                        u, ubb = ub[bi % 2]
                        s_, sbb = sb_[bi % 2]
                        pi0 = bi * 4
                        r0_, j0 = pi0 // nblk, pi0 % nblk
                        if dl == 1:
                            dU = Uacc[:, pi0 * 128:(pi0 + 4) * 128]
                            dS = Sacc[:, pi0 * 128:(pi0 + 4) * 128]
                            sU = u[:]
                            sS = s_[:]
                        elif nblk >= 4:
                            dU = Uacc[:].rearrange("p (m b) -> p b m", b=dl)[:, r0_, j0 * 128:(j0 + 4) * 128]
                            dS = Sacc[:].rearrange("p (m b) -> p b m", b=dl)[:, r0_, j0 * 128:(j0 + 4) * 128]
                            sU = u[:]
                            sS = s_[:]
                        else:
                            nres = 4 // nblk
                            dU = Uacc[:].rearrange("p (m b) -> p b m", b=dl)[:, r0_:r0_ + nres, 0:nblk * 128]
                            dS = Sacc[:].rearrange("p (m b) -> p b m", b=dl)[:, r0_:r0_ + nres, 0:nblk * 128]
                            sU = u[:].rearrange("p (r m) -> p r m", r=nres)
                            sS = s_[:].rearrange("p (r m) -> p r m", r=nres)
                        if gi == 0:
                            Sc.op("vector", CP(dU, sU), reads=[ubb], writes=[accb])
                            Sc.op("vector", CP(dS, sS), reads=[sbb], writes=[accb])
                        else:
                            Sc.op("vector", TT(dU, sU, dU, ALU.add), reads=[ubb, accb], writes=[accb])
                            Sc.op("vector", TT(dS, sS, dS, ALU.add), reads=[sbb, accb], writes=[accb])

                    for step in range(T + 1):
                        if step < T:
                            cS(step)
                        if step >= 1:
                            cV(step - 1)
                    if gi == 2:
                        for g in range(G):
                            Sc.op("vector", lambda e, g=g: e.reciprocal(out=rc[:], in_=Sacc[:, g * 512:(g + 1) * 512]), reads=[accb], writes=[rcb])
                            os_, osb = ostage.next()
                            Sc.op("vector", TT(os_[:], Uacc[:, g * 512:(g + 1) * 512], rc[:], ALU.mult), reads=[accb, rcb], writes=[osb])
                            Sc.dma("sync", ch_ot, DMA(ot_d[8 + hs, :, g * 512:(g + 1) * 512], os_[:]), reads=[osb], writes=[P.ot[8 + hs][g]])
                end_phase()

            def ln_tail(ph_t, hh, hb, gbc, bbc, lnb, t, dst_d, dst_buf, ch_dst, stats, mv, xn, xnb_, xnf_b, xnb_b):
                Sc.op("vector", lambda e: e.bn_stats(out=stats[:, 0:6], in_=hh[:, 0:512]), reads=[hb], writes=[lnb])
                Sc.op("vector", lambda e: e.bn_stats(out=stats[:, 6:12], in_=hh[:, 512:1024]), reads=[hb], writes=[lnb])
                Sc.op("vector", lambda e: e.bn_aggr(out=mv[:], in_=stats[:]), reads=[lnb], writes=[lnb])
                Sc.op("scalar", ACT(mv[:, 1:2], mv[:, 1:2], AF.Sqrt, bias=LN_EPS), reads=[lnb], writes=[lnb])
                Sc.op("vector", lambda e: e.reciprocal(out=mv[:, 1:2], in_=mv[:, 1:2]), reads=[lnb], writes=[lnb])
                Sc.op("vector", TS(xn[:], hh[:], mv[:, 0:1], mv[:, 1:2], ALU.subtract, ALU.mult), reads=[hb, lnb], writes=[xnf_b])
                Sc.op("gpsimd", TT(xn[:], xn[:], gbc[:], ALU.mult), reads=[xnf_b, P.const], writes=[xnf_b])
                Sc.op("gpsimd", TT(xn[:], xn[:], bbc[:], ALU.add), reads=[xnf_b, P.const], writes=[xnf_b])
                Sc.dma("sync", ch_dst, DMA(dst_d[t * 128:(t + 1) * 128, :], xn[:]), reads=[xnf_b], writes=[dst_buf])
                Sc.op("gpsimd", CP(xnb_[:], xn[:]), reads=[xnf_b], writes=[xnb_b])
                return lambda: transposes_to_xT(xnb_, xnb_b, t)

            with ExitStack() as ph:
                wg = [(sbuf(ph, "wg%d" % i, [128, 8, 512], BF16), Buf("wg%d" % i)) for i in range(2)]
                wbr = [(sbuf(ph, "wbr%d" % i, [128, 4, 1024], BF16), Buf("wbr%d" % i)) for i in range(3)]
                wo = sbuf(ph, "wo", [128, 8, 1024], BF16)
                wob = Buf("wo")
                gbc = sbuf(ph, "g1bc", [128, D], F32)
                bbc = sbuf(ph, "b1bc", [128, D], F32)
                otile = [(sbuf(ph, "otile%d" % i, [128, 12, 512], BF16), Buf("otile%d" % i)) for i in range(1)]
                merged = sbuf(ph, "merged", [128, 8, 512], BF16)
                mergb = Buf("merged")
                sg = Rot([(sbuf(ph, "sgM%d" % i, [128, 512], F32), Buf()) for i in range(2)])
                macc = sbuf(ph, "macc", [128, 4, 512], F32)
                maccb = Buf("macc")
                xres = [(sbuf(ph, "xres%d" % i, [128, D], F32), Buf()) for i in range(2)]
                hh = [(sbuf(ph, "hM%d" % i, [128, D], F32), Buf()) for i in range(2)]
                xn = [(sbuf(ph, "xnM%d" % i, [128, D], F32), Buf()) for i in range(2)]
                xnb = [(sbuf(ph, "xnbM%d" % i, [128, D], BF16), Buf()) for i in range(2)]
                stats = sbuf(ph, "statsM", [128, 12], F32)
                mv = sbuf(ph, "mvM", [128, 2], F32)
                lnb = Buf("lnM")

                for i in range(3):
                    Sc.dma("sync", ch_w[2 + i], DMA(wbr[i][0][:], wsc["br%d" % i].rearrange("(c p) n -> p c n", p=128)), reads=wscb["br%d" % i], writes=[wbr[i][1]])
                Sc.dma("sync", ch_w[5], DMA(wo[:], wsc["wo"].rearrange("(c p) n -> p c n", p=128)), reads=wscb["wo"], writes=[wob])
                Sc.dma("sync", ch_misc[0], DMA(gbc[:], ln1g_d[l, :].partition_broadcast(128)), writes=[P.const])
                Sc.dma("sync", ch_misc[1], DMA(bbc[:], ln1b_d[l, :].partition_broadcast(128)), writes=[P.const])
                gq = Rot([(banks[0], P.bank[0]), (banks[1], P.bank[1])])
                bq = Rot([(banks[2], P.bank[2]), (banks[3], P.bank[3])])
                kk = 0
                pend = [None]
                for tg in range(G):
                    ot_, otb = otile[0]
                    Sc.dma("sync", ch_misc[2 + tg % 2], DMA(ot_[:], ot_d[:, :, tg * 512:(tg + 1) * 512].rearrange("c p n -> p c n")),
                           reads=[P.ot[i][tg] for i in range(12)], writes=[otb])
                    for half2 in range(2):
                        for i in range(3):
                            wgt, wgb = wg[kk % 2]
                            c0 = i * 1024 + half2 * 512
                            Sc.dma("sync", ch_w[kk % 2], DMA(wgt[:], wsc["gate"][:, c0:c0 + 512].rearrange("(c p) n -> p c n", p=128)),
                                   reads=wscb["gate"], writes=[wgb])
                            kk += 1
                            for oc4 in range(4):
                                oc = half2 * 4 + oc4
                                gb_, gbb = gq.next()
                                for kc in range(8):
                                    Sc.op("tensor", MM(gb_[:], wgt[:, kc, oc4 * 128:(oc4 + 1) * 128], xT[:, kc, tg * 512:(tg + 1) * 512], kc == 0, kc == 7),
                                          reads=[wgb, P.xT[tg]], writes=[gbb], signal=(kc == 7))
                                bb_, bbb = bq.next()
                                for c4 in range(4):
                                    Sc.op("tensor", MM(bb_[:], wbr[i][0][:, c4, oc * 128:(oc + 1) * 128], ot_[:, 4 * i + c4, :], c4 == 0, c4 == 3),
                                          reads=[wbr[i][1], otb], writes=[bbb], signal=(c4 == 3))
                                s_, sb2 = sg.next()
                                Sc.op("scalar", ACT(s_[:], gb_[:], AF.Sigmoid), reads=[gbb], writes=[sb2])
                                if i == 0:
                                    Sc.op("vector", TT(macc[:, oc4, :], s_[:], bb_[:], ALU.mult), reads=[sb2, bbb], writes=[maccb])
                                elif i == 1:
                                    Sc.op("vector", TT(s_[:], s_[:], bb_[:], ALU.mult), reads=[sb2, bbb], writes=[sb2])
                                    Sc.op("gpsimd", TT(macc[:, oc4, :], macc[:, oc4, :], s_[:], ALU.add), reads=[sb2, maccb], writes=[maccb])
                                else:
                                    Sc.op("vector", TT(s_[:], s_[:], bb_[:], ALU.mult), reads=[sb2, bbb], writes=[sb2])
                                    Sc.op("gpsimd", TT(merged[:, oc, :], macc[:, oc4, :], s_[:], ALU.add), reads=[sb2, maccb], writes=[mergb])
                    for tt_ in range(4):
                        t = tg * 4 + tt_
                        xr, xrb = xres[t % 2]
                        rd = [] if res_in_bufs is None else [res_in_bufs[t]]
                        Sc.dma("sync", ch_misc[4 + t % 2], DMA(xr[:], res_in[t * 128:(t + 1) * 128, :]), reads=rd, writes=[xrb])
                        h_, hb = hh[t % 2]
                        for half in range(2):
                            yb, ybb = (banks[4], P.bank[4]) if half == 0 else (banks[5], P.bank[5])
                            for oc in range(8):
                                Sc.op("tensor", MM(yb[:], merged[:, oc, tt_ * 128:(tt_ + 1) * 128], wo[:, oc, half * 512:(half + 1) * 512], oc == 0, oc == 7),
                                      reads=[mergb, wob], writes=[ybb], signal=(oc == 7))
                            Sc.op("vector", STT(h_[:, half * 512:(half + 1) * 512], xr[:, half * 512:(half + 1) * 512], ALPHA, yb[:], ALU.mult, ALU.add),
                                  reads=[xrb, ybb], writes=[hb])
                        if pend[0] is not None:
                            pend[0]()
                        pend[0] = ln_tail(ph, h_, hb, gbc, bbc, lnb, t, res1_d, P.res1[t], ch_res1, stats, mv, xn[t % 2][0], xnb[t % 2][0], xn[t % 2][1], xnb[t % 2][1])
                if pend[0] is not None:
                    pend[0]()
                end_phase()

            for hp_ in range(2):
                with ExitStack() as ph:
                    wupr = sbuf(ph, "wupr", [128, 8, 2048], BF16)
                    wupb = [Buf() for _ in range(4)]
                    wdnr = sbuf(ph, "wdnr", [128, 16, 1024], BF16)
                    wdnb = [Buf() for _ in range(4)]
                    hidT = sbuf(ph, "hidT", [128, 16, 512], BF16)
                    hidb = Buf("hidT")
                    r32 = Rot([(sbuf(ph, "r32_%d" % i, [128, 512], F32), Buf()) for i in range(2)])
                    for q4 in range(4):
                        c0 = hp_ * 2048 + q4 * 512
                        Sc.dma("sync", ch_w[q4], DMA(wupr[:, :, q4 * 512:(q4 + 1) * 512], wsc["up"][:, c0:c0 + 512].rearrange("(c p) n -> p c n", p=128)),
                               reads=wscb["up"], writes=[wupb[q4]])
                    for q4 in range(4):
                        r0_ = hp_ * 2048 + q4 * 512
                        Sc.dma("sync", ch_w[4 + q4], DMA(wdnr[:, q4 * 4:(q4 + 1) * 4, :], wsc["dn"][r0_:r0_ + 512, :].rearrange("(c p) n -> p c n", p=128)),
                               reads=wscb["dn"], writes=[wdnb[q4]])
                    hq = Rot([(banks[0], P.bank[0]), (banks[1], P.bank[1])])
                    pendF = [None]
                    if hp_ == 0:
                        cpart = [(sbuf(ph, "cpart%d" % i, [128, D], F32), Buf()) for i in range(2)]
                    else:
                        wpg = sbuf(ph, "wpg", [128, 8, 1024], BF16)
                        wpgb = Buf("wpg")
                        wpl = sbuf(ph, "wpl", [128, 2, 1024], BF16)
                        wplb = Buf("wpl")
                        ptile = [(sbuf(ph, "ptile%d" % i, [128, 2, 512], BF16), Buf()) for i in range(1)]
                        gbc = sbuf(ph, "g2bc", [128, D], F32)
                        bbc = sbuf(ph, "b2bc", [128, D], F32)
                        x1 = [(sbuf(ph, "x1_%d" % i, [128, D], F32), Buf()) for i in range(2)]
                        c1 = [(sbuf(ph, "c1_%d" % i, [128, D], F32), Buf()) for i in range(1)]
                        sgt = Rot([(sbuf(ph, "sgF%d" % i, [128, 512], F32), Buf()) for i in range(2)])
                        xn = [(sbuf(ph, "xnF%d" % i, [128, D], F32), Buf()) for i in range(1)]
                        xnb = [(sbuf(ph, "xnbF%d" % i, [128, D], BF16), Buf()) for i in range(2)]
                        stats = sbuf(ph, "statsF", [128, 12], F32)
                        mv = sbuf(ph, "mvF", [128, 2], F32)
                        lnb = Buf("lnF")
                        Sc.dma("sync", ch_w[8], DMA(wpg[:], wsc["pg"].rearrange("(c p) n -> p c n", p=128)), reads=wscb["pg"], writes=[wpgb])
                        Sc.dma("sync", ch_w[9], DMA(wpl[:], wsc["ple"].rearrange("(c p) n -> p c n", p=128)), reads=wscb["ple"], writes=[wplb])
                        Sc.dma("sync", ch_misc[0], DMA(gbc[:], ln2g_d[l, :].partition_broadcast(128)), writes=[P.const])
                        Sc.dma("sync", ch_misc[1], DMA(bbc[:], ln2b_d[l, :].partition_broadcast(128)), writes=[P.const])
                    for tg in range(G):
                        if hp_ == 1:
                            pt_, ptb = ptile[0]
                            Sc.dma("gpsimd", ch_w[10], DMA(pt_[:], pT_d[l, :, tg * 512:(tg + 1) * 512].rearrange("(c p) n -> p c n", p=128)), writes=[ptb])
                        for hc in range(16):
                            hb_, hbb = hq.next()
                            for kc in range(8):
                                Sc.op("tensor", MM(hb_[:], wupr[:, kc, hc * 128:(hc + 1) * 128], xT[:, kc, tg * 512:(tg + 1) * 512], kc == 0, kc == 7),
                                      reads=[wupb[hc // 4], P.xT[tg]], writes=[hbb], signal=(kc == 7))
                            r_, rb = r32.next()
                            Sc.op("scalar", ACT(r_[:], hb_[:], AF.Relu), reads=[hbb], writes=[rb])
                            Sc.op("vector", STT(hidT[:, hc, :], hb_[:], 0.0, r_[:], ALU.max, ALU.mult), reads=[hbb, rb], writes=[hidb])
                        for tt_ in range(4):
                            t = tg * 4 + tt_
                            cb2 = [(banks[2], P.bank[2]), (banks[3], P.bank[3])] if tt_ % 2 == 0 else [(banks[4], P.bank[4]), (banks[5], P.bank[5])]
                            for half in range(2):
                                cs = slice(half * 512, (half + 1) * 512)
                                cb, cbb = cb2[half]
                                for hc in range(16):
                                    Sc.op("tensor", MM(cb[:], hidT[:, hc, tt_ * 128:(tt_ + 1) * 128], wdnr[:, hc, cs], hc == 0, hc == 15),
                                          reads=[hidb, wdnb[hc // 4]], writes=[cbb], signal=(hc == 15))
                            if hp_ == 0:
                                cp_, cpb = cpart[t % 2]
                                Sc.op("scalar", ACT(cp_[:, 0:512], cb2[0][0][:], AF.Copy), reads=[cb2[0][1]], writes=[cpb])
                                Sc.op("vector", CP(cp_[:, 512:1024], cb2[1][0][:]), reads=[cb2[1][1]], writes=[cpb])
                                Sc.dma("sync", ch_res1, DMA(c1_d[t * 128:(t + 1) * 128, :], cp_[:]), reads=[cpb], writes=[P.c1[t]])
                            else:
                                h_, hb = x1[t % 2]
                                c1t, c1b = c1[0]
                                Sc.dma("sync", ch_misc[2 + t % 2], DMA(h_[:], res1_d[t * 128:(t + 1) * 128, :]), reads=[P.res1[t]], writes=[hb])
                                Sc.dma("sync", ch_misc[4], DMA(c1t[:], c1_d[t * 128:(t + 1) * 128, :]), reads=[P.c1[t]], writes=[c1b])
                                for half in range(2):
                                    cs = slice(half * 512, (half + 1) * 512)
                                    cb, cbb = cb2[half]
                                    gbk, gbkb = (banks[6], P.bank[6])
                                    for kc in range(8):
                                        Sc.op("tensor", MM(gbk[:], xT[:, kc, t * 128:(t + 1) * 128], wpg[:, kc, cs], kc == 0, kc == 7),
                                              reads=[wpgb, P.xT[tg]], writes=[gbkb], signal=(kc == 7))
                                    pbk, pbkb = (banks[0], P.bank[0]) if half == 0 else (banks[1], P.bank[1])
                                    for c2 in range(2):
                                        Sc.op("tensor", MM(pbk[:], pt_[:, c2, tt_ * 128:(tt_ + 1) * 128], wpl[:, c2, cs], c2 == 0, c2 == 1),
                                              reads=[wplb, ptb], writes=[pbkb], signal=(c2 == 1))
                                    s_, sb2 = sgt.next()
                                    Sc.op("scalar", ACT(s_[:], gbk[:], AF.Sigmoid), reads=[gbkb], writes=[sb2])
                                    Sc.op("vector", TT(s_[:], s_[:], pbk[:], ALU.mult), reads=[sb2, pbkb], writes=[sb2])
                                    Sc.op("vector", STT(h_[:, cs], h_[:, cs], ALPHA, s_[:], ALU.mult, ALU.add), reads=[hb, sb2], writes=[hb])
                                    Sc.op("vector", TT(h_[:, cs], h_[:, cs], cb[:], ALU.add), reads=[hb, cbb], writes=[hb])
                                Sc.op("gpsimd", TT(h_[:], h_[:], c1t[:], ALU.add), reads=[hb, c1b], writes=[hb])
                                if pendF[0] is not None:
                                    pendF[0]()
                                pendF[0] = ln_tail(ph, h_, hb, gbc, bbc, lnb, t, out_d, P.outb[t], ch_out, stats, mv, xn[0][0], xnb[t % 2][0], xn[0][1], xnb[t % 2][1])
                    if pendF[0] is not None:
                        pendF[0]()
                    end_phase()

        Sc.wait_all("sync", [(ch_out, ch_out.val)])
        if dbg:
            Sc.wait_all("sync", [(ch_res1, ch_res1.val), (ch_ot, ch_ot.val)])
        with nc.Block() as block:
            Sc.emit(block)
    return nc


def _bucket(d):
    d = np.maximum(d, 0).astype(np.int64)
    dm = np.maximum(d, 1).astype(np.float32)
    lr = np.log(dm / np.float32(16.0)) / np.float32(math.log(2048 / 16))
    large = 16 + (lr * np.float32(16.0)).astype(np.int32)
    return np.where(d < 16, d, np.minimum(large, 31)).astype(np.int64)


def bias_tables(rel_bias, S):
    WA = S + PADA
    p = np.arange(128)[:, None]
    j = np.arange(WA)[None, :]
    bA = _bucket(j - PADA - p)
    biasA = np.ascontiguousarray(rel_bias[bA][:, :, 0:4].transpose(2, 0, 1)).astype(np.float32)
    f = np.arange(256)[None, :]
    step = np.clip(f - p, 0, 128)
    biasD = np.zeros((12, 128, 256), np.float32)
    for gi, (_, dl) in enumerate(DIL):
        bD = _bucket(step * dl)
        for hs in range(4):
            biasD[gi * 4 + hs] = rel_bias[bD, 4 + gi * 4 + hs]
    return biasA, biasD


def lam_consts(layers):
    out = np.zeros((len(layers), 128, 2), np.float32)
    for i, l in enumerate(layers):
        li = 0.8 - 0.6 * math.exp(-0.3 * l)
        out[i, :, 0] = li
        out[i, :, 1] = 1.0 - li
    return out


_WNAMES = ["w_in", "da_lambda", "da_norm", "w_branch_da", "w_branch_sb", "w_branch_dil", "w_out", "ln1_g", "ln1_b",
           "w_up", "w_down", "w_ple_gate", "w_ple", "ln2_g", "ln2_b"]
_PROG = {}


def get_program(S, NL):
    key = (S, NL)
    if key not in _PROG:
        _PROG[key] = build_program(S, NL)
    return _PROG[key]


def make_in_maps(inputs, xs, layers, S):
    f32 = lambda a: np.ascontiguousarray(np.asarray(a, dtype=np.float32))
    biasA, biasD = bias_tables(f32(inputs["rel_bias"]), S)
    lamc = lam_consts(layers)
    shared = {}
    for n in _WNAMES:
        a = f32(inputs[n])[layers]
        if n == "da_lambda":
            a = a.reshape(len(layers), 256)
        shared[n] = np.ascontiguousarray(a)
    shared["biasA"] = biasA
    shared["biasD"] = biasD
    shared["lamc"] = lamc
    p = inputs["p"]
    maps = []
    for b in range(len(xs)):
        m = dict(shared)
        m["x"] = f32(xs[b])
        m["pT"] = np.ascontiguousarray(np.asarray(p[layers, b], dtype=np.float32).transpose(0, 2, 1))
        maps.append(m)
    return maps


FUSED = True


def kernel(**inputs):
    x = np.asarray(inputs["x"], dtype=np.float32)
    B, S, _ = x.shape
    NLT = inputs["w_in"].shape[0]
    xs = [x[b] for b in range(B)]
    if FUSED:
        nc = get_program(S, NLT)
        maps = make_in_maps(inputs, xs, list(range(NLT)), S)
        res = run_bass_kernel_spmd(nc, maps, core_ids=list(range(B)))
        xs = [np.asarray(r["out"]) for r in res.results]
    else:
        nc = get_program(S, 1)
        for l in range(NLT):
            maps = make_in_maps(inputs, xs, [l], S)
            res = run_bass_kernel_spmd(nc, maps, core_ids=list(range(B)))
            xs = [np.asarray(r["out"]) for r in res.results]
    return np.stack(xs, axis=0).astype(np.float32)
```
